# Optimizing a Trainium2 kernel written in Bass

```python
import math
import jax, jax.numpy as jnp
from jax import lax
import numpy as np

D_MODEL = 2048
BATCH = 32
SEQ = 256
DEPTH = 2
DEC_BATCH = 2
DEC_SEQ = 1024
PAST_LEN = 256

GRID_W = 64
Q_BLOCK = 128
ROPE_THETA = 10000.0
EPS = 1e-6
N_EVEN = (DEPTH + 1) // 2
N_ODD = DEPTH // 2
H_A = 8
DK_A = 64
DV_A = 2 * DK_A
W_A = H_A * DV_A
POOL_WINDOWS = (2, 4, 8, 16)
N_POOL = len(POOL_WINDOWS)
W_B = D_MODEL - W_A
GW_B = W_B // N_POOL
EVEN_IN = 3 * W_A + W_B + (W_A + W_B)
HD_C = 128
H_C = D_MODEL // HD_C
KVH_C = 4
G_C = H_C // KVH_C
W_C = H_C * HD_C
ODD_IN = W_C + 2 * KVH_C * HD_C + W_C

kernel_name = "hybrid_diffattn_pool_gqa_prefix_dit_step"


def rmsnorm(x, w):
    xf = x.astype(jnp.float32)
    y = xf * lax.rsqrt(jnp.mean(xf * xf, axis=-1, keepdims=True) + EPS) * w.astype(jnp.float32)
    return y.astype(x.dtype)


def axial_rope_tables(n_tokens, dim):
    rows = n_tokens // GRID_W
    row = jnp.repeat(jnp.arange(rows), GRID_W).astype(jnp.float32)
    col = jnp.tile(jnp.arange(GRID_W), rows).astype(jnp.float32)
    quarter = dim // 4
    freqs = ROPE_THETA ** (-jnp.arange(quarter, dtype=jnp.float32) / quarter)
    ar = row[:, None] * freqs
    ac = col[:, None] * freqs
    cos = jnp.concatenate([jnp.cos(ar), jnp.cos(ar), jnp.cos(ac), jnp.cos(ac)], axis=-1)
    sin = jnp.concatenate([jnp.sin(ar), jnp.sin(ar), jnp.sin(ac), jnp.sin(ac)], axis=-1)
    return cos, sin


def apply_rope(x, cos, sin):
    xf = x.astype(jnp.float32)
    x0, x1, x2, x3 = jnp.split(xf, 4, axis=-1)
    rot = jnp.concatenate([-x1, x0, -x3, x2], axis=-1)
    shape = (cos.shape[0],) + (1,) * (x.ndim - 3) + (cos.shape[-1],)
    return (xf * cos.reshape(shape) + rot * sin.reshape(shape)).astype(x.dtype)


def sweep_query_blocks(fn, q):
    lead, L, d = q.shape[:-2], q.shape[-2], q.shape[-1]
    nb = L // Q_BLOCK
    qb = jnp.moveaxis(q.reshape(*lead, nb, Q_BLOCK, d), -3, 0)
    ob = lax.map(fn, qb)
    ob = jnp.moveaxis(ob, 0, -3)
    return ob.reshape(*ob.shape[:-3], L, ob.shape[-1])


def modulate(x, cond, norm_w, ada_w, ada_b):
    m = jax.nn.silu(cond) @ ada_w + ada_b
    shift, scale, gate = jnp.split(m, 3, axis=-1)
    h = rmsnorm(x, norm_w) * (1 + scale[:, None]) + shift[:, None]
    return h, gate[:, None]


def diff_attention(q, k, v, lam, lam_init, subln_w):
    k1, k2 = k[..., :DK_A], k[..., DK_A:]
    vf = v.astype(jnp.float32)
    scale = DK_A ** -0.5

    def block(qb):
        s1 = jnp.einsum('bhqd,bhkd->bhqk', qb[:, :, 0], k1).astype(jnp.float32) * scale
        s2 = jnp.einsum('bhqd,bhkd->bhqk', qb[:, :, 1], k2).astype(jnp.float32) * scale
        p = jax.nn.softmax(s1, axis=-1) - lam * jax.nn.softmax(s2, axis=-1)
        return jnp.einsum('bhqk,bhkd->bhqd', p, vf)

    o = sweep_query_blocks(block, q)
    o = rmsnorm(o, subln_w) * (1.0 - lam_init)
    return o.astype(v.dtype)


def multiscale_pool(u, pool_w, pool_scale):
    B, L, _ = u.shape
    ug = u.reshape(B, L, N_POOL, GW_B).astype(jnp.float32)
    S = jnp.concatenate([jnp.zeros((B, 1, N_POOL, GW_B), jnp.float32),
                         jnp.cumsum(ug, axis=1)], axis=1)
    t = np.arange(L)[:, None]
    half = np.array(POOL_WINDOWS)[None, :] // 2
    lo = np.clip(t - half, 0, L)
    hi = np.clip(t + half, 0, L)
    gidx = np.arange(N_POOL)[None, :]
    cnt = jnp.asarray((hi - lo).astype(np.float32))[None, :, :, None]
    mean = (S[:, hi, gidx] - S[:, lo, gidx]) / cnt
    out = jnp.einsum('blgc,gcd->blgd', mean - ug, pool_w.astype(jnp.float32))
    return (out.reshape(B, L, W_B) * pool_scale).astype(u.dtype)


def even_mixer(h, p, lam, lam_init, rope, ctx_k, ctx_v):
    B, L, _ = h.shape
    proj = h @ p['w_in']
    q, k, v, u, g = jnp.split(proj, [W_A, 2 * W_A, 3 * W_A, 3 * W_A + W_B], axis=-1)
    q = rmsnorm(q.reshape(B, L, H_A, 2, DK_A), p['q_norm_w'])
    k = rmsnorm(k.reshape(B, L, H_A, 2, DK_A), p['k_norm_w'])
    if rope is not None:
        q = apply_rope(q, *rope)
        k = apply_rope(k, *rope)
    q = q.transpose(0, 2, 3, 1, 4)
    k_own = k.transpose(0, 2, 1, 3, 4).reshape(B, H_A, L, 2 * DK_A)
    v_own = v.reshape(B, L, H_A, DV_A).transpose(0, 2, 1, 3)
    if ctx_k is None:
        k_all, v_all = k_own, v_own
    else:
        k_all = jnp.concatenate([ctx_k.astype(k_own.dtype), k_own], axis=2)
        v_all = jnp.concatenate([ctx_v.astype(v_own.dtype), v_own], axis=2)
    attn = diff_attention(q, k_all, v_all, lam, lam_init, p['subln_w'])
    attn = attn.transpose(0, 2, 1, 3).reshape(B, L, W_A)
    pool = multiscale_pool(u, p['pool_w'], p['pool_scale'])
    y = jnp.concatenate([attn, pool], axis=-1) * jax.nn.silu(g)
    return y @ p['w_out'], k_own, v_own


def gqa_attention(q, k, v):
    vf = v.astype(jnp.float32)
    scale = HD_C ** -0.5

    def block(qb):
        s = jnp.einsum('bgrqd,bgkd->bgrqk', qb, k).astype(jnp.float32) * scale
        return jnp.einsum('bgrqk,bgkd->bgrqd', jax.nn.softmax(s, axis=-1), vf)

    return sweep_query_blocks(block, q).astype(v.dtype)


def odd_mixer(h, p, rope, ctx_k, ctx_v):
    B, L, _ = h.shape
    proj = h @ p['w_in']
    q, k, v, g = jnp.split(proj, [W_C, W_C + KVH_C * HD_C, W_C + 2 * KVH_C * HD_C], axis=-1)
    q = rmsnorm(q.reshape(B, L, H_C, HD_C), p['q_norm_w'])
    k = rmsnorm(k.reshape(B, L, KVH_C, HD_C), p['k_norm_w'])
    if rope is not None:
        q = apply_rope(q, *rope)
        k = apply_rope(k, *rope)
    q = q.reshape(B, L, KVH_C, G_C, HD_C).transpose(0, 2, 3, 1, 4)
    k_own = k.transpose(0, 2, 1, 3)
    v_own = v.reshape(B, L, KVH_C, HD_C).transpose(0, 2, 1, 3)
    if ctx_k is None:
        k_all, v_all = k_own, v_own
    else:
        k_all = jnp.concatenate([ctx_k.astype(k_own.dtype), k_own], axis=2)
        v_all = jnp.concatenate([ctx_v.astype(v_own.dtype), v_own], axis=2)
    o = gqa_attention(q, k_all, v_all)
    o = o.transpose(0, 3, 1, 2, 4).reshape(B, L, W_C)
    return (o * jax.nn.silu(g)) @ p['w_out'], k_own, v_own


def setup_inputs(seed: int = 0) -> dict:
    key = jax.random.key(seed)
    ks = jax.random.split(key, 32)
    f32 = jnp.float32
    nrm = lambda k, shape, s=1.0: (jax.random.normal(k, shape, f32) * s).astype(f32)
    D = D_MODEL
    return {
        'x_prompt': nrm(ks[0], (BATCH, SEQ, D)),
        'x_sample': nrm(ks[1], (DEC_BATCH, DEC_SEQ, D)),
        'cache_a_k': nrm(ks[2], (DEC_BATCH, N_EVEN, H_A, PAST_LEN, 2 * DK_A)),
        'cache_a_v': nrm(ks[3], (DEC_BATCH, N_EVEN, H_A, PAST_LEN, DV_A)),
        'cache_c_k': nrm(ks[4], (DEC_BATCH, N_ODD, KVH_C, PAST_LEN, HD_C)),
        'cache_c_v': nrm(ks[5], (DEC_BATCH, N_ODD, KVH_C, PAST_LEN, HD_C)),
        'c': nrm(ks[6], (DEC_BATCH, D)),
        'c_ctx': nrm(ks[7], (D,)),
        'norm_w': 1.0 + nrm(ks[8], (DEPTH, D), 0.05),
        'ada_w': nrm(ks[9], (DEPTH, D, 3 * D), D ** -0.5),
        'ada_b': nrm(ks[10], (DEPTH, 3 * D), 0.02),
        'even_w_in': nrm(ks[11], (N_EVEN, D, EVEN_IN), D ** -0.5),
        'even_q_norm_w': 1.0 + nrm(ks[12], (N_EVEN, DK_A), 0.05),
        'even_k_norm_w': 1.0 + nrm(ks[13], (N_EVEN, DK_A), 0.05),
        'even_lam_q1': nrm(ks[14], (N_EVEN, DK_A), 0.1),
        'even_lam_k1': nrm(ks[15], (N_EVEN, DK_A), 0.1),
        'even_lam_q2': nrm(ks[16], (N_EVEN, DK_A), 0.1),
        'even_lam_k2': nrm(ks[17], (N_EVEN, DK_A), 0.1),
        'even_subln_w': 1.0 + nrm(ks[18], (N_EVEN, DV_A), 0.05),
        'even_pool_w': nrm(ks[19], (N_EVEN, N_POOL, GW_B, GW_B), GW_B ** -0.5),
        'even_pool_scale': 1.0 + nrm(ks[20], (N_EVEN, W_B), 0.1),
        'even_w_out': nrm(ks[21], (N_EVEN, W_A + W_B, D), (W_A + W_B) ** -0.5),
        'gqa_w_in': nrm(ks[22], (N_ODD, D, ODD_IN), D ** -0.5),
        'gqa_q_norm_w': 1.0 + nrm(ks[23], (N_ODD, HD_C), 0.05),
        'gqa_k_norm_w': 1.0 + nrm(ks[24], (N_ODD, HD_C), 0.05),
        'gqa_w_out': nrm(ks[25], (N_ODD, W_C, D), W_C ** -0.5),
    }


def reference(x_prompt, x_sample, cache_a_k, cache_a_v, cache_c_k, cache_c_v, c, c_ctx,
              norm_w, ada_w, ada_b,
              even_w_in, even_q_norm_w, even_k_norm_w, even_lam_q1, even_lam_k1,
              even_lam_q2, even_lam_k2, even_subln_w, even_pool_w, even_pool_scale,
              even_w_out, gqa_w_in, gqa_q_norm_w, gqa_k_norm_w, gqa_w_out):
    n_lat = x_sample.shape[1]
    rope_a = axial_rope_tables(n_lat, DK_A)
    rope_c = axial_rope_tables(n_lat, HD_C)
    cond_ctx = jnp.broadcast_to(c_ctx[None, :], (x_prompt.shape[0], D_MODEL))

    y_p, y_s = x_prompt, x_sample
    new_a_k, new_a_v, new_c_k, new_c_v = [], [], [], []
    for i in range(DEPTH):
        h_p, gate_p = modulate(y_p, cond_ctx, norm_w[i], ada_w[i], ada_b[i])
        h_s, gate_s = modulate(y_s, c, norm_w[i], ada_w[i], ada_b[i])
        if i % 2 == 0:
            j = i // 2
            p = {'w_in': even_w_in[j], 'q_norm_w': even_q_norm_w[j], 'k_norm_w': even_k_norm_w[j],
                 'subln_w': even_subln_w[j], 'pool_w': even_pool_w[j],
                 'pool_scale': even_pool_scale[j], 'w_out': even_w_out[j]}
            lam_init = 0.8 - 0.6 * math.exp(-0.3 * i)
            lam = (jnp.exp(jnp.sum(even_lam_q1[j].astype(jnp.float32) * even_lam_k1[j].astype(jnp.float32)))
                   - jnp.exp(jnp.sum(even_lam_q2[j].astype(jnp.float32) * even_lam_k2[j].astype(jnp.float32)))
                   + lam_init)
            out_p, k_new, v_new = even_mixer(h_p, p, lam, lam_init, None, None, None)
            out_s, _, _ = even_mixer(h_s, p, lam, lam_init, rope_a, cache_a_k[:, j], cache_a_v[:, j])
            new_a_k.append(k_new)
            new_a_v.append(v_new)
        else:
            j = i // 2
            p = {'w_in': gqa_w_in[j], 'q_norm_w': gqa_q_norm_w[j], 'k_norm_w': gqa_k_norm_w[j],
                 'w_out': gqa_w_out[j]}
            out_p, k_new, v_new = odd_mixer(h_p, p, None, None, None)
            out_s, _, _ = odd_mixer(h_s, p, rope_c, cache_c_k[:, j], cache_c_v[:, j])
            new_c_k.append(k_new)
            new_c_v.append(v_new)
        y_p = y_p + gate_p * out_p
        y_s = y_s + gate_s * out_s

    state_a_k = jnp.stack(new_a_k, axis=1)
    state_a_v = jnp.stack(new_a_v, axis=1)
    state_c_k = jnp.stack(new_c_k, axis=1)
    state_c_v = jnp.stack(new_c_v, axis=1)
    return (y_p, y_s, state_a_k, state_a_v, state_c_k, state_c_v)
```

```python
import contextlib
import math
import os
import numpy as np
import ml_dtypes
import concourse.bass as bass
import concourse.mybir as mybir
from concourse.bass_utils import run_bass_kernel_spmd

F32 = mybir.dt.float32
BF16 = mybir.dt.bfloat16
AF = mybir.ActivationFunctionType
ALU = mybir.AluOpType
AX = mybir.AxisListType

D = 2048
KC = 16
EPS = 1e-6
NTOK = 1280
HALO0 = 1280
GRID_W = 64
ROPE_THETA = 10000.0
POOL_WINDOWS = (2, 4, 8, 16)


class Sched:
    def __init__(self, nc, stack):
        self.nc = nc
        self.stack = stack
        self.engs = {"pe": nc.tensor, "act": nc.scalar, "dve": nc.vector,
                     "pool": nc.gpsimd, "sp": nc.sync}
        self.sems = {}
        self.count = {}
        for e in self.engs:
            self.sems[e] = stack.enter_context(nc.semaphore("s_" + e))
            self.count[e] = 0
        self.waited = {e: {} for e in self.engs}
        self.last_write = {}
        self.reads = {}
        self.n_ops = 0

    def _deps(self, reads, writes):
        deps = {}

        def add(ev):
            if ev is None:
                return
            k, v = ev
            if deps.get(k, 0) < v:
                deps[k] = v
        for r in reads:
            add(self.last_write.get(r))
        for w in writes:
            add(self.last_write.get(w))
            for ev in self.reads.get(w, ()):
                add(ev)
        return deps

    def _wait(self, eng, deps, skip_self=False):
        e = self.engs[eng]
        for k, v in deps.items():
            if skip_self and k == eng:
                continue
            if self.waited[eng].get(k, 0) >= v:
                continue
            e.wait_ge(self.sems[k], v)
            self.waited[eng][k] = v

    def _record(self, ev, reads, writes):
        for r in reads:
            lst = self.reads.setdefault(r, [])
            lst.append(ev)
            if len(lst) > 64:
                mx = {}
                for k, v in lst:
                    if mx.get(k, 0) < v:
                        mx[k] = v
                self.reads[r] = list(mx.items())
        for w in writes:
            self.last_write[w] = ev
            self.reads[w] = []

    def op(self, eng, reads, writes, fn):
        deps = self._deps(reads, writes)
        self._wait(eng, deps, skip_self=(eng == "pe"))
        ins = fn(self.engs[eng])
        ins.then_inc(self.sems[eng], 1)
        self.count[eng] += 1
        ev = (eng, self.count[eng])
        self._record(ev, reads, writes)
        self.n_ops += 1
        return ev

    def dma(self, queue, semkey, reads, writes, fn, inc=16):
        if semkey not in self.sems:
            nm = "d_" + "".join(ch for ch in str(semkey) if ch.isalnum() or ch == "_")
            self.sems[semkey] = self.stack.enter_context(self.nc.semaphore(nm))
            self.count[semkey] = 0
        deps = self._deps(reads, writes)
        self._wait(queue, deps)
        inss = fn(self.engs[queue])
        if not isinstance(inss, (list, tuple)):
            inss = [inss]
        for ins in inss:
            ins.then_inc(self.sems[semkey], inc)
            self.count[semkey] += inc
        ev = (semkey, self.count[semkey])
        self._record(ev, reads, writes)
        return ev

    def alias(self, new_names, old_names):
        evs = []
        for o in old_names:
            if self.last_write.get(o) is not None:
                evs.append(self.last_write[o])
            evs.extend(self.reads.get(o, ()))
        mx = {}
        for k, v in evs:
            if mx.get(k, 0) < v:
                mx[k] = v
        for n in new_names:
            prev = []
            if self.last_write.get(n) is not None:
                prev.append(self.last_write[n])
            prev.extend(self.reads.get(n, ()))
            m2 = dict(mx)
            for k, v in prev:
                if m2.get(k, 0) < v:
                    m2[k] = v
            self.last_write[n] = None
            self.reads[n] = list(m2.items())

    def wait_all(self, eng):
        deps = {k: c for k, c in self.count.items() if c > 0}
        self._wait(eng, deps)


class Ring:
    def __init__(self, name, n):
        self.name, self.n, self.i = name, n, 0

    def next(self):
        s = self.i % self.n
        self.i += 1
        return s, (self.name, s)


def build_program(stage=99):
    nc = bass.Bass("TRN2", target_bir_lowering=False)

    def din(name, shape, dt=F32):
        return nc.dram_tensor(name, list(shape), dt, kind="ExternalInput").ap()

    def dout(name, shape, dt=F32):
        return nc.dram_tensor(name, list(shape), dt, kind="ExternalOutput").ap()

    def dint(name, shape, dt=F32):
        return nc.dram_tensor(name, list(shape), dt).ap()

    xp = din("xp", [1024, D]); xs = din("xs", [256, D]); xh = din("xh", [16, D])
    cond = din("cond", [2, D])
    cak = din("cak", [8, 256, 128]); cav = din("cav", [8, 256, 128])
    cck = din("cck", [4, 256, 128]); ccv = din("ccv", [4, 256, 128])
    norm_w = din("norm_w", [2, D])
    adaw = din("adaw", [2, D, 1536]); adab = din("adab", [2, 1536])
    ew_in = din("ew_in", [D, 6144]); ew_out = din("ew_out", [D, D])
    gw_in = din("gw_in", [D, 5120]); gw_out = din("gw_out", [D, D])
    eqn = din("eqn", [64]); ekn = din("ekn", [64]); lamv = din("lamv", [4, 64])
    subw = din("subw", [128]); poolw = din("poolw", [4, 256, 256]); pscale = din("pscale", [1024])
    gqn = din("gqn", [128]); gkn = din("gkn", [128])
    ident = din("ident", [128, 128])
    ropeA = din("ropeA", [256, 2, 64]); ropeC = din("ropeC", [256, 2, 128])
    Bp = din("Bp", [4, 256, 256]); Bs = din("Bs", [4, 272, 256])
    onesd = din("onesd", [128, 5200])

    yp = dout("yp", [1024, D]); ys = dout("ys", [256, D])
    nak = dout("nak", [4, 8, 256, 128]); nav = dout("nav", [4, 8, 256, 128])
    nck = dout("nck", [4, 4, 256, 128]); ncv = dout("ncv", [4, 4, 256, 128])

    x1 = dint("x1", [NTOK, D])
    agm_in = [dint("agm_in%d" % l, [2, 1536]) for l in range(2)]
    agm_out = [dint("agm_out%d" % l, [8, 1536]) for l in range(2)]
    mlin = dint("mlin", [2, 2, 6144])
    warm_in = dint("warm_in", [2, 64]); warm_out = dint("warm_out", [8, 64])
    agkv_in = [dint("agkv_in%d" % i, [1024, 256], BF16) for i in range(3)]
    agkv_out = [dint("agkv_out%d" % i, [4096, 256], BF16) for i in range(3)]

    with contextlib.ExitStack() as st:
        S = Sched(nc, st)

        def sbt(name, shape, dt):
            return st.enter_context(nc.sbuf_tensor(name, list(shape), dt))

        def pst(name, shape, dt=F32):
            return st.enter_context(nc.psum_tensor(name, list(shape), dt))

        R1 = sbt("R1", [128, 16 * 1296], BF16)
        R2 = sbt("R2", [128, 16 * 1280], BF16)
        WB = sbt("WB", [128, 2, 16, 512], BF16)
        R5 = sbt("R5", [128, 8256], BF16)
        R6 = sbt("R6", [128, 10320], BF16)
        QT = sbt("QT", [128, 4, 1280], BF16)
        SG = sbt("SG", [128, 10, 512], BF16)
        PT = sbt("PT", [128, 2, 1280], BF16)

        hT = R1[:, :].rearrange("p (k t) -> p k t", k=16)
        yT = R2[:, :].rearrange("p (k t) -> p k t", k=16)
        KTp = R5[:, 0:4096].rearrange("p (h t) -> p h t", h=4)
        V1p = R5[:, 4096:8256].rearrange("p (t h c) -> p t h c", t=8, h=4)
        KTs = R6[:, 0:5120].rearrange("p (h t) -> p h t", h=4)
        V1s = R6[:, 5120:10320].rearrange("p (t h c) -> p t h c", t=10, h=4)
        xts = [R2[:, s * 4096:(s + 1) * 4096].bitcast(F32) for s in range(3)]
        sqj = R2[:, 12288:12288 + 2048]
        xsb = [R2[:, 14336 + s * 2048:14336 + (s + 1) * 2048] for s in range(2)]
        condt = R2[:, 0:4096].bitcast(F32)
        sct = R2[:, 4096:8192].bitcast(F32)
        adabt = R2[:, 8192:8192 + 6144].bitcast(F32).rearrange("p (l n) -> p l n", l=2)
        mrow = R2[:, 14336:14336 + 6144].bitcast(F32).rearrange("p (l n) -> p l n", l=2)
        gbc = [R6[:, j * 4096:(j + 1) * 4096].bitcast(F32) for j in range(2)]
        xblk = [R5[:, s * 1024:(s + 1) * 1024].bitcast(F32) for s in range(4)]
        oblk = [R5[:, 4096 + s * 1024: 4096 + (s + 1) * 1024].bitcast(F32) for s in range(2)]
        u_tok = R5[:, 0:11 * 512].rearrange("p (t c) -> p t c", t=11)
        Bpt = R6[:, 0:2048].rearrange("p (g s t) -> p g s t", g=4, s=2)
        Bst = R6[:, 2048:2048 + 3072].rearrange("p (g s t) -> p g s t", g=4, s=3)
        pwt = R6[:, 5120:5120 + 2048].rearrange("p (g c d) -> p g c d", g=4, c=2)
        pscbc = R6[:, 7168:7168 + 2048].bitcast(F32)

        QTf = QT[:, :, :].rearrange("p h t -> p (h t)")
        SGf = SG[:, :, :].rearrange("p t c -> p (t c)")
        PTf = PT[:, :, :].rearrange("p s q -> p (s q)")
        xts_alt = [QTf[:, 0:4096].bitcast(F32), SGf[:, 0:4096].bitcast(F32)]
        sqj_alt = PTf[:, 0:2048]
        xsb_alt = [R5[:, 6144:8192], R6[:, 8192:10240]]
        identf = sbt("identf", [128, 128], F32)
        identb = sbt("identb", [128, 128], BF16)
        dg = sbt("dg", [128, 2, 128], F32)
        stat = sbt("stat", [128, 16, 16], F32)
        shsc = sbt("shsc", [128, 2, 2, 2, 16], F32)
        nwt = sbt("nwt", [128, 2, 16], F32)
        s1t = sbt("s1t", [128, 2, 2, 16], F32)
        scT = sbt("scT", [128, 16, 2], BF16)
        qnw = sbt("qnw", [128, 2, 128], F32)
        lams = sbt("lams", [128, 8], F32)
        nlam = sbt("nlam", [128, 1], F32)
        wsub = sbt("wsub", [128, 128], F32)
        ropet = sbt("ropet", [128, 2, 2, 128], F32)
        sqf = sbt("sqf", [128, 1, 512], F32)
        qkf = sbt("qkf", [128, 2, 512], F32)
        qkg = sbt("qkg", [128, 1, 512], F32)
        qkb = sbt("qkb", [128, 2, 512], BF16)
        kst = sbt("kst", [128, 1, 512], F32)
        vst = sbt("vst", [128, 1, 512], F32)
        lamt = qkf[:, 0, 0:256].rearrange("p (a d) -> p a d", a=4)
        lampt = qkf[:, 0, 256:384].rearrange("p (a d) -> p a d", a=2)
        onesr = qkf[:, 0, 384:512]
        ktst = sbt("ktst", [128, 4, 256], BF16)
        vsst = sbt("vsst", [128, 2, 512], BF16)
        cstg = sbt("cstg", [128, 2, 4, 128], BF16)
        otm = sbt("otm", [128, 4, 128], F32)
        ofm = sbt("ofm", [128, 2, 128], F32)
        ytk = sbt("ytk", [128, 2, 256], BF16)
        pooledT = sbt("pooledT", [128, 2, 2, 256], BF16)

        pP = [pst("pP%d" % i, [128, 512]) for i in range(2)]
        pS = [pst("pS%d" % i, [128, 512]) for i in range(2)]
        pO = [pst("pO%d" % i, [128, 2, 256]) for i in range(4)]

        statr = Ring("stat", 16)
        sqr, qkfr, qkgr, qkbr = Ring("sqf", 1), Ring("qkf", 2), Ring("qkg", 1), Ring("qkb", 2)
        kstr, vstr = Ring("kst", 1), Ring("vst", 1)
        otr, ofr, ytr, ytr4 = Ring("otm", 4), Ring("ofm", 2), Ring("ytkp", 2), Ring("ytk", 4)
        pPr, pSr, pOr = Ring("pP", 2), Ring("pS", 2), Ring("pO", 4)
        ptr_, ppr = Ring("ptmp", 2), Ring("pooledT", 2)

        R2mod = [("xt", 0), ("xt", 1), ("xt", 2), "sqj", ("xsb", 0), ("xsb", 1)]
        R2ada = ["condt", "sct"]
        ALTmod = [("xta", 0), ("xta", 1), "sqja", ("xsba", 0), ("xsba", 1)]
        QSP_names = [("QT", t) for t in range(10)] + [("SG", t) for t in range(10)] + [("PT", 0), ("PT", 1)]
        R1wo = [("gbc", 0), ("gbc", 1), ("xblk", 0), ("xblk", 1), ("xblk", 2), ("xblk", 3), ("oblk", 0), ("oblk", 1)]
        hT_names = [("hT", t, k) for t in range(11) for k in range(16)]
        yT_names = [("yT", k) for k in range(16)]
        R5att = ["KTp", "V1p"]
        R6att = ["KTs", "V1s"]
        R6pool = ["Bt", "pwt", "pscbc"]

        S.dma("sp", "c0", [], ["identf", "qnw0", "condt"], lambda e: [
            e.dma_start(out=identf[:], in_=ident),
            e.dma_start(out=condt[0:2, :], in_=cond),
            e.dma_start(out=wsub[:], in_=subw.partition_broadcast(128)),
            e.dma_start(out=lamt[0:1, :, :], in_=lamv.rearrange("(o a) d -> o a d", o=1)),
        ])

        def load_layer_consts(L):
            if L == 0:
                S.dma("sp", "c2", [], ["qnw"], lambda e: [
                    e.dma_start(out=qnw[:, 0, 0:64], in_=eqn.partition_broadcast(128)),
                    e.dma_start(out=qnw[:, 1, 0:64], in_=ekn.partition_broadcast(128)),
                ] + [e.dma_start(out=ropet[:, t, :, 0:64], in_=ropeA[t * 128:(t + 1) * 128]) for t in range(2)])
            else:
                S.dma("sp", "c2", [], ["qnw"], lambda e: [
                    e.dma_start(out=qnw[:, 0, :], in_=gqn.partition_broadcast(128)),
                    e.dma_start(out=qnw[:, 1, :], in_=gkn.partition_broadcast(128)),
                ] + [e.dma_start(out=ropet[:, t, :, :], in_=ropeC[t * 128:(t + 1) * 128]) for t in range(2)])
        S.op("dve", ["identf"], ["identb"], lambda e: e.tensor_copy(out=identb[:], in_=identf[:]))
        LAM_INIT = 0.8 - 0.6 * math.exp(0.0)
        S.op("dve", ["qnw0"], ["wsub"], lambda e: e.tensor_scalar(
            out=wsub[:], in0=wsub[:], scalar1=1.0 - LAM_INIT, scalar2=None, op0=ALU.mult))
        S.op("dve", ["qnw0"], ["lamp"], lambda e: e.tensor_tensor(
            out=lampt[0:1, :, :], in0=lamt[0:1, 0:4:2, :], in1=lamt[0:1, 1:4:2, :], op=ALU.mult))
        S.op("dve", ["lamp"], ["lams0"], lambda e: e.tensor_reduce(
            out=lams[0:1, 0:2], in_=lampt[0:1, :, :], axis=AX.X, op=ALU.add))
        S.op("act", ["lams0"], ["lams1"], lambda e: e.activation(
            out=lams[0:1, 2:4], in_=lams[0:1, 0:2], func=AF.Exp))
        S.op("dve", ["lams1"], ["lams2"], lambda e: e.tensor_tensor(
            out=lams[0:1, 4:5], in0=lams[0:1, 3:4], in1=lams[0:1, 2:3], op=ALU.subtract))
        S.op("dve", ["lams2"], ["lams3"], lambda e: e.tensor_scalar(
            out=lams[0:1, 5:6], in0=lams[0:1, 4:5], scalar1=-LAM_INIT, scalar2=None, op0=ALU.add))
        S.op("dve", [], ["onesr"], lambda e: e.memset(onesr[0:1, :], 1.0))
        pi0, pn0 = pPr.next()
        S.op("pe", ["onesr", "lams3"], [pn0], lambda e: e.matmul(
            pP[pi0][:, 0:1], lhsT=onesr[0:1, :], rhs=lams[0:1, 5:6], start=True, stop=True))
        S.op("dve", [pn0], ["nlam"], lambda e: e.tensor_copy(out=nlam[:], in_=pP[pi0][:, 0:1]))

        S.op("act", ["condt"], ["sct"], lambda e: e.activation(out=sct[0:2, :], in_=condt[0:2, :], func=AF.Silu))

        ps0, psn0 = pSr.next()

        def mm_sct(e):
            for kc in range(16):
                ins = e.matmul(pS[ps0][:, 2 * kc:2 * kc + 2], lhsT=sct[0:2, kc * 128:(kc + 1) * 128],
                               rhs=identf[0:2, 0:2], start=True, stop=True)
            return ins
        S.op("pe", ["sct", "identf"], [psn0], mm_sct)
        S.op("dve", [psn0], ["scT"], lambda e: e.tensor_copy(
            out=scT[:].rearrange("p k j -> p (k j)"), in_=pS[ps0][:, 0:32]))

        wstate = {"n": 0}

        def wload(src, ncols, slot=None):
            if slot is None:
                slot = wstate["n"] % 2
                wstate["n"] += 1
            S.dma("pool", ("wb", slot), [], [("wb", slot)], lambda e: e.dma_start(
                out=wsrc(slot)[:, :, 0:ncols], in_=src.rearrange("(kc p) n -> p kc n", p=128)))
            return slot

        def wsrc(slot):
            if slot == 2:
                return R6[:, 0:8192].rearrange("p (k n) -> p k n", k=16)
            return WB[:, slot, :, :]

        GROUPS = [[0, 1, 2, 3], [4, 5, 6, 7]]
        s1_names = [("s1t", l, j) for l in range(2) for j in range(2)]

        def ada_block(L, nb, slot):
            S.dma("sp", "c1", [], [("vst", 0)], lambda e: [
                e.dma_start(out=vst[p:p + 1, 0, :], in_=adab[L:L + 1, nb * 512:(nb + 1) * 512]) for p in range(2)])
            pi, pn = pPr.next()

            def mm_ada(e):
                for kc in range(16):
                    ins = e.matmul(pP[pi][0:2, :], lhsT=scT[:, kc, :], rhs=wsrc(slot)[:, kc, :],
                                   start=(kc == 0), stop=(kc == 15))
                return ins
            S.op("pe", ["scT", ("wb", slot)], [pn], mm_ada)
            S.op("dve", [pn, ("vst", 0)], [("kst", 0)], lambda e: e.tensor_tensor(
                out=kst[0:2, 0, :], in0=pP[pi][0:2, :], in1=vst[0:2, 0, :], op=ALU.add))
            S.dma("sp", "c1", [("kst", 0)], [("agm_in", L)], lambda e: e.dma_start(
                out=agm_in[L][:, nb * 512:(nb + 1) * 512], in_=kst[0:2, 0, :]))

        def ada_finish(L, split=False):
            S.dma("pool", "cc", [("agm_in", L)], [("agm_out", L)], lambda e: e.collective_compute(
                "AllGather", ALU.bypass, replica_groups=GROUPS, ins=[agm_in[L]], outs=[agm_out[L]]), inc=1)
            S.dma("sp", "c1", [("agm_out", L)], [("mlin", L)], lambda e: e.dma_start(
                out=mlin[L].rearrange("j (r i) -> r j i", r=4),
                in_=agm_out[L].rearrange("(r j) i -> r j i", r=4)))
            if split:
                mt_a = otm[:, 0:2, :]
                mt_b = otm[:, 2, :]
                stg = [("otm", 0), ("otm", 1), ("otm", 2)]
            else:
                mt_a = kst[:, 0, 0:256].rearrange("p (i c) -> p i c", i=2)
                mt_b = vst[:, 0, 0:128]
                stg = [("kst", 0), ("vst", 0)]
            S.dma("sp", "c1", [("mlin", L)], stg, lambda e: [
                e.dma_start(out=mt_a[0:32, j, :], in_=mlin[L, j, 0:4096].rearrange("(a p) -> a p", p=128))
                for j in range(2)] + [
                e.dma_start(out=mt_b[0:16, :], in_=norm_w[L].rearrange("(a p) -> a p", p=128))])
            def part_b():
                ps1, psn1 = pSr.next()

                def mm_mt(e):
                    for j in range(2):
                        ins = e.matmul(pS[ps1][:, j * 32:(j + 1) * 32], lhsT=mt_a[0:32, j, :], rhs=identf[0:32, 0:32], start=True, stop=True)
                    ins = e.matmul(pS[ps1][:, 64:80], lhsT=mt_b[0:16, :], rhs=identf[0:16, 0:16], start=True, stop=True)
                    return ins
                S.op("pe", stg + ["identf"], [psn1], mm_mt)
                S.op("dve", [psn1], ["shsc"], lambda e: e.tensor_copy(
                    out=shsc[:, L, :, :, :].rearrange("p j t k -> p (j t k)"), in_=pS[ps1][:, 0:64]))
                S.op("dve", [psn1], ["shsc"], lambda e: e.tensor_copy(out=nwt[:, L, :], in_=pS[ps1][:, 64:80]))
                for j in range(2):
                    S.op("dve", ["shsc"], [("s1t", L, j)], lambda e: e.scalar_tensor_tensor(
                        out=s1t[:, L, j, :], in0=shsc[:, L, j, 1, :], scalar=1.0, in1=nwt[:, L, :],
                        op0=ALU.add, op1=ALU.mult))

            if split:
                return part_b
            part_b()

        ada0_slots = [wload(adaw[0, :, nb * 512:(nb + 1) * 512], 512) for nb in range(2)]
        ada0_slots.append(wload(adaw[0, :, 1024:1536], 512, slot=2))
        S.dma("sp", "c1", ["identf"], ["warm_in"], lambda e: e.dma_start(out=warm_in, in_=identf[0:2, 0:64]))
        S.dma("pool", "cc", ["warm_in"], ["warm_out"], lambda e: e.collective_compute(
            "AllGather", ALU.bypass, replica_groups=GROUPS, ins=[warm_in], outs=[warm_out]), inc=1)

        def tile_tok0(tt):
            return HALO0 if tt == 10 else tt * 128

        DEFER = []

        def defer(fn):
            DEFER.append(fn)

        def flush_defer(keep=0):
            while len(DEFER) > keep:
                DEFER.pop(0)()

        def make_modulate(L, alt, first_dep=None):
            if alt:
                xt_t, sq_t, xb_t, nslot = xts_alt, sqj_alt, xsb_alt, 2
                nxt_, nsq, nxb, q = "xta", "sqja", "xsba", "pool"
            else:
                xt_t, sq_t, xb_t, nslot = xts, sqj, xsb, 3
                nxt_, nsq, nxb, q = "xt", "sqj", "xsb", "sp"

            def xload(tt):
                npt = 16 if tt == 10 else 128
                slot = tt % nslot
                if L == 0:
                    src = xp[tt * 128:(tt + 1) * 128] if tt < 8 else (xs[(tt - 8) * 128:(tt - 7) * 128] if tt < 10 else xh)
                    rd = list(first_dep) if (first_dep and tt == 0) else []
                else:
                    src = x1[tt * 128:(tt + 1) * 128]
                    rd = [("x1", tt)]
                S.dma(q, (nxt_, slot), rd, [(nxt_, slot)], lambda e: e.dma_start(out=xt_t[slot][0:npt, :], in_=src))

            def stats(tt):
                npt = 16 if tt == 10 else 128
                slot = tt % nslot
                bslot = tt % 2
                si, sn = statr.next()
                S.op("act", [(nxt_, slot)], [nsq, sn], lambda e: e.activation(
                    out=sq_t[0:npt, :], in_=xt_t[slot][0:npt, :], func=AF.Square, accum_out=stat[0:npt, si, 0:1]))
                S.op("act", [sn], [sn], lambda e: e.activation(
                    out=stat[0:npt, si, 1:2], in_=stat[0:npt, si, 0:1], func=AF.Ln, scale=1.0 / D, bias=EPS))
                S.op("act", [sn], [sn], lambda e: e.activation(
                    out=stat[0:npt, si, 2:3], in_=stat[0:npt, si, 1:2], func=AF.Exp, scale=-0.5))
                S.op("dve", [sn, (nxt_, slot)], [(nxb, bslot)], lambda e: e.tensor_scalar(
                    out=xb_t[bslot][0:npt, :], in0=xt_t[slot][0:npt, :], scalar1=stat[0:npt, si, 2:3],
                    scalar2=None, op0=ALU.mult))

            def pe_part(tt):
                npt = 16 if tt == 10 else 128
                slot = tt % 2
                t0 = tile_tok0(tt)
                for g in range(2):
                    pi, pn = pOr.next()
                    pv = pO[pi][:, :, :].rearrange("p a b -> p (a b)").bitcast(BF16)

                    def mm(e):
                        for j in range(8):
                            c = g * 8 + j
                            ins = e.transpose(pv[:, j * 128:j * 128 + npt], xb_t[slot][0:npt, c * 128:(c + 1) * 128],
                                              identb[0:npt, 0:npt])
                        return ins
                    S.op("pe", [(nxb, slot), "identb"], [pn], mm)
                    pv3 = pv.rearrange("p (k t) -> p k t", k=8)[:, :, 0:npt]
                    wr = [("hT", tt, g * 8 + j) for j in range(8)]
                    if g == 0:
                        S.op("dve", [pn], wr, lambda e: e.tensor_copy(out=hT[:, 0:8, t0:t0 + npt], in_=pv3))
                    else:
                        S.op("act", [pn], wr, lambda e: e.activation(out=hT[:, 8:16, t0:t0 + npt], in_=pv3, func=AF.Copy))
            return xload, stats, pe_part

        def modulate(L, tiles, hooks=None, first_dep=None):
            flush_defer()
            S.alias(R2mod, yT_names + R2ada)
            xload, stats, pe_part = make_modulate(L, False, first_dep)
            xload(tiles[0])
            xload(tiles[1])
            stats(tiles[0])
            for i, tt in enumerate(tiles):
                if i + 2 < len(tiles):
                    xload(tiles[i + 2])
                if i + 1 < len(tiles):
                    stats(tiles[i + 1])
                pe_part(tt)
                if hooks and tt in hooks:
                    hooks[tt]()

        def modulate_affine(L, with_halo):
            tiles_p = list(range(8))
            tiles_s = [8, 9] + ([10] if with_halo else [])
            for kc in range(16):
                for jsel, tl, a0, a1 in ((0, tiles_p, 0, 1024), (1, tiles_s, 1024, 1296 if with_halo else 1280)):
                    names = [("hT", tt, kc) for tt in tl]
                    if kc % 2 == 0:
                        S.op("dve", names + s1_names + ["shsc"], names, lambda e: e.tensor_scalar(
                            out=hT[:, kc, a0:a1], in0=hT[:, kc, a0:a1],
                            scalar1=s1t[:, L, jsel, kc:kc + 1], scalar2=shsc[:, L, jsel, 0, kc:kc + 1],
                            op0=ALU.mult, op1=ALU.add))
                    else:
                        S.op("act", names + s1_names + ["shsc"], names, lambda e: e.activation(
                            out=hT[:, kc, a0:a1], in_=hT[:, kc, a0:a1], func=AF.Identity,
                            scale=s1t[:, L, jsel, kc:kc + 1], bias=shsc[:, L, jsel, 0, kc:kc + 1]))

        def proj(slot, ncols, tiles, evac):
            for tt in tiles:
                npt = 16 if tt == 10 else 128
                t0 = tile_tok0(tt)
                pi, pn = pPr.next()

                def mm(e, pi=pi, npt=npt, t0=t0):
                    for kc in range(16):
                        ins = e.matmul(pP[pi][0:npt, 0:ncols], lhsT=hT[:, kc, t0:t0 + npt], rhs=WB[:, slot, kc, 0:ncols],
                                       start=(kc == 0), stop=(kc == 15))
                    return ins
                S.op("pe", [("hT", tt, k) for k in range(16)] + [("wb", slot)], [pn], mm)
                n0 = len(DEFER)
                evac(tt, pP[pi], pn)
                flush_defer(keep=len(DEFER) - n0)

        def transpose4(src_bf, src_name, dst_fn, dst_names, nblk=4):
            pi, pn = pSr.next()
            pv = pS[pi][:, :].bitcast(BF16)[:, 0:nblk * 128]

            def mm(e):
                for b in range(nblk):
                    ins = e.transpose(pv[:, b * 128:(b + 1) * 128], src_bf[:, b * 128:(b + 1) * 128], identb[:])
                return ins
            S.op("pe", [src_name, "identb"], [pn], mm)
            if int(os.environ.get("MK_SUBK", "99")) <= 5:
                return
            S.op("act", [pn], dst_names, lambda e: dst_fn(e, pv.rearrange("p (b t) -> p b t", b=nblk)))

        def evac_qk(L, kind, hg):
            dk = 64 if L == 0 else 128
            nch = 512 // dk
            wrow = qnw[:, 0 if kind == 'q' else 1, 0:dk]
            rope = ropet[:, :, :, 0:dk]
            q4 = dk // 4

            SUBK = int(os.environ.get("MK_SUBK", "99"))

            def f(tt, ps, pn):
                sample = tt >= 8
                if SUBK <= 1:
                    return
                si, sn = statr.next()
                qi, qn_ = sqr.next()
                S.op("act", [pn], [qn_], lambda e: e.activation(out=sqf[:, qi, :], in_=ps[:, :], func=AF.Square))
                S.op("dve", [qn_], [sn], lambda e: e.tensor_reduce(
                    out=stat[:, si, 0:nch], in_=sqf[:, qi, :].rearrange("p (c d) -> p c d", d=dk), axis=AX.X, op=ALU.add))
                S.op("act", [sn], [sn], lambda e: e.activation(
                    out=stat[:, si, 8:8 + nch], in_=stat[:, si, 0:nch], func=AF.Ln, scale=1.0 / dk, bias=EPS))
                S.op("act", [sn], [sn], lambda e: e.activation(
                    out=stat[:, si, 0:nch], in_=stat[:, si, 8:8 + nch], func=AF.Exp, scale=-0.5))
                if SUBK <= 2:
                    return
                fi, fn_ = qkfr.next()
                S.op("dve", [pn, sn], [fn_], lambda e: e.tensor_tensor(
                    out=qkf[:, fi, :].rearrange("p (c d) -> p c d", d=dk), in0=ps[:, :].rearrange("p (c d) -> p c d", d=dk),
                    in1=stat[:, si, 0:nch].unsqueeze(2).to_broadcast([128, nch, dk]), op=ALU.mult))
                if SUBK <= 3:
                    return
                bi, bn_ = qkbr.next()
                wbc = wrow.unsqueeze(1).to_broadcast([128, nch, dk])
                if not sample:
                    if kind == 'k':
                        ki, kn_ = kstr.next()
                        S.op("dve", [fn_, "qnw"], [kn_], lambda e: e.tensor_tensor(
                            out=kst[:, ki, :].rearrange("p (c d) -> p c d", d=dk),
                            in0=qkf[:, fi, :].rearrange("p (c d) -> p c d", d=dk), in1=wbc, op=ALU.mult))
                        seq, s0 = tt // 2, (tt % 2) * 128
                        if L == 0:
                            dst = nak[seq, hg * 4:(hg + 1) * 4, s0:s0 + 128, :]
                        else:
                            dst = nck[seq, :, s0:s0 + 128, :]
                        S.dma("sp", kn_, [kn_], [("out_k", L, hg, tt)], lambda e: e.dma_start(
                            out=dst.rearrange("h s d -> s h d"), in_=kst[:, ki, :].rearrange("p (h d) -> p h d", h=4)))
                        S.op("act", [kn_], [bn_], lambda e: e.activation(out=qkb[:, bi, :], in_=kst[:, ki, :], func=AF.Copy))
                    else:
                        S.op("dve", [fn_, "qnw"], [bn_], lambda e: e.tensor_tensor(
                            out=qkb[:, bi, :].rearrange("p (c d) -> p c d", d=dk),
                            in0=qkf[:, fi, :].rearrange("p (c d) -> p c d", d=dk), in1=wbc, op=ALU.mult))
                else:
                    ts = tt - 8
                    gi, gn_ = qkgr.next()
                    S.op("dve", [fn_, "qnw"], [gn_], lambda e: e.tensor_tensor(
                        out=qkg[:, gi, :].rearrange("p (c d) -> p c d", d=dk),
                        in0=qkf[:, fi, :].rearrange("p (c d) -> p c d", d=dk), in1=wbc, op=ALU.mult))
                    cosb = rope[:, ts, 0, :].unsqueeze(1).to_broadcast([128, nch, dk])
                    S.op("dve", [gn_, "qnw"], [fn_], lambda e: e.tensor_tensor(
                        out=qkf[:, fi, :].rearrange("p (c d) -> p c d", d=dk),
                        in0=qkg[:, gi, :].rearrange("p (c d) -> p c d", d=dk), in1=cosb, op=ALU.mult))
                    x5 = qkg[:, gi, :].rearrange("p (c a b q) -> p c a b q", a=2, b=2, q=q4)
                    t5 = sqf[:, qi, :].rearrange("p (c a b q) -> p c a b q", a=2, b=2, q=q4)
                    s5 = rope[:, ts, 1, :].rearrange("p (a b q) -> p a b q", a=2, b=2)
                    for b in range(2):
                        S.op("dve", [gn_, "qnw"], [qn_], lambda e, b=b: e.tensor_tensor(
                            out=t5[:, :, :, b, :], in0=x5[:, :, :, 1 - b, :],
                            in1=s5[:, :, b, :].unsqueeze(1).to_broadcast([128, nch, 2, q4]), op=ALU.mult))
                    S.op("dve", [fn_, qn_], [bn_], lambda e: e.tensor_tensor(
                        out=qkb[:, bi, :], in0=qkf[:, fi, :], in1=sqf[:, qi, :], op=ALU.add))
                t0 = tt * 128
                if SUBK <= 4:
                    return
                if kind == 'q':
                    defer(lambda: transpose4(qkb[:, bi, :], bn_, lambda e, pv: e.activation(out=QT[:, :, t0:t0 + 128], in_=pv, func=AF.Copy), [("QT", tt)]))
                elif not sample:
                    defer(lambda: transpose4(qkb[:, bi, :], bn_, lambda e, pv: e.activation(out=KTp[:, :, t0:t0 + 128], in_=pv, func=AF.Copy), ["KTp"]))
                else:
                    ts = tt - 8
                    defer(lambda: transpose4(qkb[:, bi, :], bn_, lambda e, pv: e.activation(out=ktst[:, :, ts * 128:(ts + 1) * 128], in_=pv, func=AF.Copy), [("ktst", ts)]))
            return f

        def evac_v(L, hg):
            def f(tt, ps, pn):
                if tt < 8:
                    vi, vn_ = vstr.next()
                    S.op("act", [pn], [vn_], lambda e: e.activation(out=vst[:, vi, :], in_=ps[:, :], func=AF.Copy))
                    seq, s0 = tt // 2, (tt % 2) * 128
                    if L == 0:
                        dst = nav[seq, hg * 4:(hg + 1) * 4, s0:s0 + 128, :]
                    else:
                        dst = ncv[seq, :, s0:s0 + 128, :]
                    S.dma("sp", vn_, [vn_], [("out_v", L, hg, tt)], lambda e: e.dma_start(
                        out=dst.rearrange("h s d -> s h d"), in_=vst[:, vi, :].rearrange("p (h d) -> p h d", h=4)))
                    S.op("dve", [vn_], ["V1p"], lambda e: e.tensor_copy(
                        out=V1p[:, tt, :, 0:128], in_=vst[:, vi, :].rearrange("p (h d) -> p h d", h=4)))
                else:
                    ts = tt - 8
                    S.op("dve", [pn], [("vsst", ts)], lambda e: e.tensor_copy(out=vsst[:, ts, :], in_=ps[:, :]))
            return f

        def evac_g(tt, ps, pn):
            S.op("act", [pn], [("SG", tt)], lambda e: e.activation(out=SG[:, tt, :], in_=ps[:, :], func=AF.Silu))

        def evac_g_sub(tt, ps, pn):
            qi, qn_ = qkfr.next()
            S.op("act", [pn], [qn_], lambda e: e.activation(out=qkf[:, qi, :], in_=ps[:, :], func=AF.Silu))
            S.op("dve", [qn_, "wsub"], [("SG", tt)], lambda e: e.tensor_tensor(
                out=SG[:, tt, :].rearrange("p (h d) -> p h d", h=4), in0=qkf[:, qi, :].rearrange("p (h d) -> p h d", h=4),
                in1=wsub[:, :].unsqueeze(1).to_broadcast([128, 4, 128]), op=ALU.mult))

        def evac_u(tt, ps, pn):
            npt = 16 if tt == 10 else 128
            S.op("dve", [pn], [("u_tok", tt)], lambda e: e.tensor_copy(out=u_tok[0:npt, tt, :], in_=ps[0:npt, :]))

        def kv_prefetch(L, hg):
            ck = cak if L == 0 else cck
            cv = cav if L == 0 else ccv
            h0 = hg * 4 if L == 0 else 0
            S.dma("pool", "cstg", [], ["cstg"], lambda e: [e.dma_start(
                out=cstg[:, t, :, :], in_=ck[h0:h0 + 4, t * 128:(t + 1) * 128, :].rearrange("h p d -> p h d")) for t in range(2)])
            S.dma("pool", "v1s_a", [], ["V1s"], lambda e: [e.dma_start(
                out=V1s[:, t, :, 0:128], in_=cv[h0:h0 + 4, t * 128:(t + 1) * 128, :].rearrange("h p d -> p h d")) for t in range(2)])

        def kv_exchange(L, hg, agi):
            flush_defer()
            ain, aout = agkv_in[agi], agkv_out[agi]
            S.dma("sp", "kvst", [("ktst", 0), ("ktst", 1), ("vsst", 0), ("vsst", 1)], [("agin", agi)], lambda e: [e.dma_start(
                out=ain[0:512, :].rearrange("(h d) t -> d h t", h=4), in_=ktst[:, :, :]), e.dma_start(
                out=ain[512:1024, :].rearrange("(t p a) b -> p t (a b)", t=2, a=2), in_=vsst[:, :, :])])
            S.dma("pool", "cc", [("agin", agi)], [("agout", agi)], lambda e: e.collective_compute(
                "AllGather", ALU.bypass, replica_groups=GROUPS, ins=[ain], outs=[aout]), inc=1)
            for t in range(2):
                transpose4(cstg[:, t, :, :].rearrange("p h d -> p (h d)"), "cstg",
                           lambda e, pv, t=t: e.activation(out=KTs[:, :, t * 128:(t + 1) * 128], in_=pv, func=AF.Copy), ["KTs"])
            aview = aout.rearrange("(r x) t -> r x t", r=4)
            S.dma("sp", "kts_b", [("agout", agi)], ["KTs"], lambda e: [e.dma_start(
                out=KTs[:, :, 256 + r * 256:256 + (r + 1) * 256],
                in_=aview[r, 0:512, :].rearrange("(h d) t -> d h t", h=4)) for r in range(4)])
            S.dma("sp", "v1s_b", [("agout", agi)], ["V1s"], lambda e: [e.dma_start(
                out=V1s[:, 2 + 2 * r + t, :, 0:128],
                in_=aview[r, 512 + t * 256:512 + (t + 1) * 256, :].rearrange("(p a) b -> p (a b)", a=2).rearrange("p (h d) -> p h d", h=4))
                for r in range(4) for t in range(2)])

        def set_ones():
            S.dma("pool", "c_ones", [], ["V1p", "V1s"], lambda e: [
                e.dma_start(out=R5[:, 4096:8256], in_=onesd[:, 0:4160]),
                e.dma_start(out=R6[:, 5120:10320], in_=onesd[:, 0:5200])])

        def attention(L, kc0, kv_of_head, hooks=None):
            flush_defer()
            nj = 2 if L == 0 else 1
            dk = 64 if L == 0 else 128
            sc = dk ** -0.5
            units = []
            for seq in range(4):
                for qh in range(4):
                    for j in range(nj):
                        units.append((seq, qh, j, (0, 1)))
            for qh in range(4):
                for qt in range(2):
                    for j in range(nj):
                        units.append((4, qh, j, (qt,)))

            def keys_of(seq):
                if seq < 4:
                    return [("p", seq * 2 + t) for t in range(2)]
                return [("s", t) for t in range(10)]

            def scores(u, ui):
                seq, qh, j, qts = u
                kvh = kv_of_head(qh)
                kts = keys_of(seq)
                nq = 128 * len(qts)
                q0 = seq * 256 + qts[0] * 128
                pslot = ui % 2
                per = 512 // nq
                ptv = PT[:, pslot, 0:len(kts) * nq].rearrange("p (k q) -> p k q", q=nq)
                for c in range(0, len(kts), per):
                    pi, pn = pSr.next()
                    nper = min(per, len(kts) - c)

                    def mm(e, c=c, pi=pi, nper=nper):
                        for t in range(nper):
                            kind, kt = kts[c + t]
                            if kind == "p":
                                ksrc = KTp[j * dk:(j + 1) * dk, kvh, kt * 128:(kt + 1) * 128]
                            else:
                                ksrc = KTs[j * dk:(j + 1) * dk, kvh, kt * 128:(kt + 1) * 128]
                            ins = e.matmul(pS[pi][:, t * nq:(t + 1) * nq], lhsT=ksrc,
                                           rhs=QT[j * dk:(j + 1) * dk, qh, q0:q0 + nq], start=True, stop=True)
                        return ins
                    S.op("pe", ["KTp" if seq < 4 else "KTs", ("QT", seq * 2), ("QT", seq * 2 + 1)], [pn], mm)
                    S.op("act", [pn], [("PT", pslot)], lambda e, c=c, pi=pi, nper=nper: e.activation(
                        out=ptv[:, c:c + nper, :], in_=pS[pi][:, 0:nper * nq].rearrange("p (t q) -> p t q", t=nper),
                        func=AF.Exp, scale=sc))

            def pv(u, ui, oslots):
                seq, qh, j, qts = u
                kvh = kv_of_head(qh)
                kts = keys_of(seq)
                nq = 128 * len(qts)
                pslot = ui % 2
                ptv = PT[:, pslot, 0:len(kts) * nq].rearrange("p (k q) -> p k q", q=nq)
                for qi_, qt in enumerate(qts):
                    oi, on = oslots[qi_]

                    def mm(e, qi_=qi_, oi=oi):
                        for i, (kind, kt) in enumerate(kts):
                            vsrc = V1p[:, kt, kvh, 0:129] if kind == "p" else V1s[:, kt, kvh, 0:129]
                            ins = e.matmul(pO[oi][:, j, 0:129], lhsT=ptv[:, i, qi_ * 128:(qi_ + 1) * 128], rhs=vsrc,
                                           start=(i == 0), stop=(i == len(kts) - 1))
                        return ins
                    S.op("pe", [("PT", pslot), "V1p" if seq < 4 else "V1s"], [on], mm)

            def combine(seq, qh, qts, oslots):
                chains = []
                for qi_, qt in enumerate(qts):
                    chains.append(combine_ops(seq, qh, qt, oslots[qi_]))
                n = max(len(c) for c in chains)
                for k in range(n):
                    for c in chains:
                        if k < len(c):
                            eng, rd, wr, fn = c[k]
                            if eng == "defer":
                                defer(fn)
                            else:
                                S.op(eng, rd, wr, fn)

            def combine_ops(seq, qh, qt, oslot):
                ops = []
                oi, on = oslot
                tt = seq * 2 + qt
                si, sn = statr.next()
                yi, yn = ytr4.next()
                ytv = ytk[:, yi // 2, (yi % 2) * 128:(yi % 2 + 1) * 128]
                if L == 0:
                    ti, tn = otr.next()
                    fi, fn_ = ofr.next()
                    ops.append(("dve", [on], [sn], lambda e: e.reciprocal(out=stat[:, si, 0:2], in_=pO[oi][:, :, 128])))
                    ops.append(("dve", [sn, "nlam"], [sn], lambda e: e.tensor_tensor(
                        out=stat[:, si, 2:3], in0=stat[:, si, 1:2], in1=nlam[:, 0:1], op=ALU.mult)))
                    ops.append(("dve", [on, sn], [tn], lambda e: e.tensor_scalar(
                        out=otm[:, ti, :], in0=pO[oi][:, 1, 0:128], scalar1=stat[:, si, 2:3], scalar2=None, op0=ALU.mult)))
                    ops.append(("dve", [on, sn, tn], [fn_], lambda e: e.scalar_tensor_tensor(
                        out=ofm[:, fi, :], in0=pO[oi][:, 0, 0:128], scalar=stat[:, si, 0:1], in1=otm[:, ti, :],
                        op0=ALU.mult, op1=ALU.add)))
                    ops.append(("act", [fn_], [tn, sn], lambda e: e.activation(
                        out=otm[:, ti, :], in_=ofm[:, fi, :], func=AF.Square, accum_out=stat[:, si, 4:5])))
                    ops.append(("act", [sn], [sn], lambda e: e.activation(
                        out=stat[:, si, 5:6], in_=stat[:, si, 4:5], func=AF.Ln, scale=1.0 / 128, bias=EPS)))
                    ops.append(("act", [sn], [sn], lambda e: e.activation(
                        out=stat[:, si, 6:7], in_=stat[:, si, 5:6], func=AF.Exp, scale=-0.5)))
                    ops.append(("dve", [fn_, sn, ("SG", tt)], [yn], lambda e: e.scalar_tensor_tensor(
                        out=ytv, in0=ofm[:, fi, :], scalar=stat[:, si, 6:7], in1=SG[:, tt, qh * 128:(qh + 1) * 128],
                        op0=ALU.mult, op1=ALU.mult)))
                else:
                    ops.append(("dve", [on], [sn], lambda e: e.reciprocal(out=stat[:, si, 0:1], in_=pO[oi][:, 0, 128:129])))
                    ops.append(("dve", [on, sn, ("SG", tt)], [yn], lambda e: e.scalar_tensor_tensor(
                        out=ytv, in0=pO[oi][:, 0, 0:128], scalar=stat[:, si, 0:1],
                        in1=SG[:, tt, qh * 128:(qh + 1) * 128], op0=ALU.mult, op1=ALU.mult)))

                def ytrans():
                    pi, pn = pPr.next()
                    pvw = pP[pi][:, :].bitcast(BF16)[:, 0:128]
                    S.op("pe", [yn, "identb"], [pn], lambda e: e.transpose(pvw, ytv, identb[:]))
                    S.op("act", [pn], [("yT", kc0 + qh)], lambda e: e.activation(
                        out=yT[:, kc0 + qh, tt * 128:(tt + 1) * 128], in_=pvw, func=AF.Copy))
                ops.append(("defer", None, None, ytrans))
                return ops

            scores(units[0], 0)
            oslots = None
            for ui, u in enumerate(units):
                if ui + 1 < len(units):
                    scores(units[ui + 1], ui + 1)
                seq, qh, j, qts = u
                if j == 0:
                    oslots = [pOr.next() for _ in qts]
                pv(u, ui, oslots)
                if hooks and ui in hooks:
                    hooks[ui]()
                flush_defer(keep=2 if L == 0 else 0)
                if j == nj - 1:
                    combine(seq, qh, qts, oslots)

        def pool_block(ub):
            flush_defer()
            its = [(seq, gl) for seq in range(5) for gl in range(2)]
            state = {}

            def stage_a(i):
                seq, gl = its[i]
                g = ub * 2 + gl
                stiles = [(seq * 2 + t, 128) for t in range(2)] if seq < 4 else [(8, 128), (9, 128), (10, 16)]
                qi, qn_ = ppr.next()
                for cb in range(2):
                    pi, pn = pSr.next()

                    def mm(e):
                        for k, (tt, npt) in enumerate(stiles):
                            bsrc = Bpt[0:npt, g, k, :] if seq < 4 else Bst[0:npt, g, k, :]
                            ins = e.matmul(pS[pi][:, 0:256], lhsT=u_tok[0:npt, tt, gl * 256 + cb * 128: gl * 256 + (cb + 1) * 128],
                                           rhs=bsrc, start=(k == 0), stop=(k == len(stiles) - 1))
                        return ins
                    S.op("pe", [("u_tok", tt) for tt, _ in stiles] + ["Bt"], [pn], mm)
                    S.op("act", [pn], [qn_], lambda e: e.activation(
                        out=pooledT[:, qi, cb, :], in_=pS[pi][:, 0:256], func=AF.Copy))
                state[i] = {"q": (qi, qn_), "y": []}

            def stage_b(i):
                seq, gl = its[i]
                g = ub * 2 + gl
                qi, qn_ = state[i]["q"]
                for qt in range(2):
                    tt = seq * 2 + qt
                    pi, pn = pPr.next()

                    def mm2(e):
                        for cb in range(2):
                            ins = e.matmul(pP[pi][:, 0:256], lhsT=pooledT[:, qi, cb, qt * 128:(qt + 1) * 128],
                                           rhs=pwt[:, g, cb, :], start=(cb == 0), stop=(cb == 1))
                        return ins
                    S.op("pe", [qn_, "pwt"], [pn], mm2)
                    ti, _ = ptr_.next()
                    tns = [("otm", 2 * ti), ("otm", 2 * ti + 1)]
                    ptv = otm[:, 2 * ti:2 * ti + 2, :].rearrange("p a d -> p (a d)")
                    S.op("dve", [pn, "pscbc"], tns, lambda e: e.tensor_tensor(
                        out=ptv, in0=pP[pi][:, 0:256], in1=pscbc[:, g * 256:(g + 1) * 256], op=ALU.mult))
                    yi, _ = ytr.next()
                    yns = [("ytk", 2 * yi), ("ytk", 2 * yi + 1)]
                    S.op("dve", tns + [("SG", tt)], yns, lambda e: e.tensor_tensor(
                        out=ytk[:, yi, :], in0=ptv, in1=SG[:, tt, gl * 256:(gl + 1) * 256], op=ALU.mult))
                    state[i]["y"].append((yi, yns, tt))

            def stage_c(i):
                seq, gl = its[i]
                g = ub * 2 + gl
                kcb = 8 + g * 2
                for yi, yns, tt in state[i]["y"]:
                    p2, pn2 = pPr.next()
                    pvw = pP[p2][:, :].bitcast(BF16)[:, 0:256]

                    def tr(e):
                        for b_ in range(2):
                            ins = e.transpose(pvw[:, b_ * 128:(b_ + 1) * 128], ytk[:, yi, b_ * 128:(b_ + 1) * 128], identb[:])
                        return ins
                    S.op("pe", yns + ["identb"], [pn2], tr)
                    S.op("act", [pn2], [("yT", kcb), ("yT", kcb + 1)], lambda e: e.activation(
                        out=yT[:, kcb:kcb + 2, tt * 128:(tt + 1) * 128], in_=pvw.rearrange("p (b t) -> p b t", b=2), func=AF.Copy))

            n = len(its)
            for i in range(n + 2):
                if i < n:
                    stage_a(i)
                if 0 <= i - 2 < n:
                    stage_c(i - 2)
                if 0 <= i - 1 < n:
                    stage_b(i - 1)

        def w_out_phase(L, w_out, inter=None, next_w=None):
            flush_defer()
            S.alias(R1wo, R5att + R6att + R6pool + [("u_tok", t) for t in range(11)])
            if inter is not None:
                S.alias(ALTmod, QSP_names + R5att + R6att + R6pool + [("u_tok", t) for t in range(11)])
                ixload, istats, ipe = make_modulate(L + 1, True)
            for j in range(2):
                S.dma("sp", ("gbc", j), [("mlin", L)], [("gbc", j)], lambda e, j=j: e.dma_start(
                    out=gbc[j][:, :], in_=mlin[L, j, 4096:6144].partition_broadcast(128)))
            slot = wload(w_out[:, 0:512], 512)

            def tile_io(tt):
                if L == 0:
                    src = xp[tt * 128:(tt + 1) * 128] if tt < 8 else xs[(tt - 8) * 128:(tt - 7) * 128]
                    return src, [], x1[tt * 128:(tt + 1) * 128], lambda cb: [("x1", tt)]
                dst = yp[tt * 128:(tt + 1) * 128] if tt < 8 else ys[(tt - 8) * 128:(tt - 7) * 128]
                return x1[tt * 128:(tt + 1) * 128], [("x1", tt)], dst, lambda cb: [("out_y", tt, cb)]

            xr = Ring("xblk", 4)
            xslots = {}
            pre = {}

            def xb_load(cb, tt):
                src, rd, _, _ = tile_io(tt)
                xi, xn = xr.next()
                xslots[(cb, tt)] = (xi, xn)
                S.dma("act", xn, rd, [xn], lambda e: e.dma_start(out=xblk[xi][:, :], in_=src[:, cb * 512:(cb + 1) * 512]))

            seq = [(cb, tt) for cb in range(4) for tt in range(10)]
            for k in range(3):
                xb_load(*seq[k])
            orr = Ring("oblk", 2)
            for k, (cb, tt) in enumerate(seq):
                if tt == 0:
                    if cb < 3:
                        nslot = wload(w_out[:, (cb + 1) * 512:(cb + 2) * 512], 512)
                    else:
                        nslot = None
                if cb == 3 and tt == 6 and next_w is not None:
                    pre["slot"] = wload(next_w, 512)
                if k + 3 < len(seq):
                    xb_load(*seq[k + 3])
                jsel = 0 if tt < 8 else 1
                _, _, dst, wrf = tile_io(tt)
                xi, xn = xslots[(cb, tt)]
                pi, pn = pPr.next()

                def mm(e):
                    for kc in range(16):
                        ins = e.matmul(pP[pi][:, :], lhsT=yT[:, kc, tt * 128:(tt + 1) * 128], rhs=WB[:, slot, kc, :],
                                       start=(kc == 0), stop=(kc == 15))
                    return ins
                S.op("pe", yT_names + [("wb", slot)], [pn], mm)
                oi, on = orr.next()
                S.op("dve", [pn, ("gbc", jsel)], [on], lambda e: e.tensor_tensor(
                    out=oblk[oi][:, :], in0=pP[pi][:, :], in1=gbc[jsel][:, cb * 512:(cb + 1) * 512], op=ALU.mult))
                S.op("dve", [on, xn], [on], lambda e: e.tensor_tensor(
                    out=oblk[oi][:, :], in0=oblk[oi][:, :], in1=xblk[xi][:, :], op=ALU.add))
                S.dma("sp", on, [on], wrf(cb), lambda e: e.dma_start(
                    out=dst[:, cb * 512:(cb + 1) * 512], in_=oblk[oi][:, :]))
                if inter is not None and cb == 3:
                    if tt >= 4:
                        ipe(tt - 4)
                    if tt >= 2:
                        istats(tt - 2)
                    ixload(tt)
                if tt == 9:
                    slot = nslot
            if inter is not None:
                ipe(6)
                istats(8)
                ipe(7)
                istats(9)
                ipe(8)
                ipe(9)
            return pre.get("slot")

        ALL10 = list(range(10))

        class Stop(Exception):
            pass

        def gate(n):
            if stage < n:
                raise Stop()

        def main_flow():
            S.alias([("qkf", 0), ("qkf", 1)], ["qnw0", "lamp", "onesr"])
            gate(1)
            load_layer_consts(0)
            for nb in range(3):
                ada_block(0, nb, ada0_slots[nb])
            fin0 = ada_finish(0, split=True)
            nxt = wload(ew_in[:, 1024:1536], 512)
            modulate(0, list(range(11)), None, first_dep=[("wb", 0), ("wb", 1), ("wb", 2)])
            fin0()
            modulate_affine(0, True)
            S.alias(yT_names, R2mod + R2ada)
            S.alias(R6att, [("wb", 2)])
            set_ones()
            gate(2)
            for hg in range(2):
                sK = nxt
                kv_prefetch(0, hg)
                sV = wload(ew_in[:, 2048 + hg * 512:2048 + (hg + 1) * 512], 512)
                proj(sK, 512, ALL10, evac_qk(0, 'k', hg))
                if hg == 1:
                    fin_b()
                if int(os.environ.get("MK_SUBK", "99")) < 99:
                    raise Stop()
                sQ = wload(ew_in[:, hg * 512:(hg + 1) * 512], 512)
                proj(sV, 512, ALL10, evac_v(0, hg))
                gate(3)
                kv_exchange(0, hg, hg)
                gate(4)
                sG = wload(ew_in[:, 4096 + hg * 512:4096 + (hg + 1) * 512], 512)
                proj(sQ, 512, ALL10, evac_qk(0, 'q', hg))
                nxt = wload(ew_in[:, 1024 + 512:1024 + 1024], 512) if hg == 0 else wload(ew_in[:, 3072:3584], 512)
                proj(sG, 512, ALL10, evac_g_sub)
                gate(5)
                if hg == 0:
                    hk = {ui: (lambda nb=nb: ada_block(1, nb, wload(adaw[1, :, nb * 512:(nb + 1) * 512], 512, slot=sG)))
                          for nb, ui in enumerate((10, 28, 46))}
                    attention(0, hg * 4, lambda qh: qh, hk)
                    fin_b = ada_finish(1, split=True)
                else:
                    attention(0, hg * 4, lambda qh: qh)
                gate(6)
            flush_defer()
            S.alias(["Bt", "pwt", "pscbc"], R6att)
            S.alias([("u_tok", t) for t in range(11)], R5att)
            S.dma("pool", "c_B", [], ["Bt", "pwt", "pscbc"], lambda e: [
                e.dma_start(out=Bst[0:16, :, 2, :], in_=Bs[:, 256:272, :].rearrange("g p t -> p g t")),
                e.dma_start(out=pscbc[:, :], in_=pscale.partition_broadcast(128))] + [
                e.dma_start(out=Bpt[:, g, :, :], in_=Bp[g].rearrange("(s p) t -> p s t", p=128)) for g in range(4)] + [
                e.dma_start(out=Bst[:, g, 0:2, :], in_=Bs[g, 0:256, :].rearrange("(s p) t -> p s t", p=128)) for g in range(4)] + [
                e.dma_start(out=pwt[:, g, :, :], in_=poolw[g].rearrange("(c p) d -> p c d", p=128)) for g in range(4)])
            for ub in range(2):
                sU = nxt
                sGp = wload(ew_in[:, 5120 + ub * 512:5120 + (ub + 1) * 512], 512)
                proj(sU, 512, list(range(11)), evac_u)
                if ub == 0:
                    nxt = wload(ew_in[:, 3584:4096], 512)
                proj(sGp, 512, ALL10, evac_g)
                pool_block(ub)
            gate(7)
            load_layer_consts(1)
            sK1 = w_out_phase(0, ew_out, inter=True, next_w=gw_in[:, 2048:2560])
            gate(8)

            modulate_affine(1, False)
            S.alias(QSP_names, ALTmod)
            S.alias(R6att, R1wo + ALTmod)
            S.alias(R5att, R1wo + ALTmod)
            set_ones()
            sK = sK1
            kv_prefetch(1, 0)
            sV = wload(gw_in[:, 2560:3072], 512)
            proj(sK, 512, ALL10, evac_qk(1, 'k', 0))
            sQ = wload(gw_in[:, 0:512], 512)
            proj(sV, 512, ALL10, evac_v(1, 0))
            gate(9)
            kv_exchange(1, 0, 2)
            for kvh in range(4):
                sG = wload(gw_in[:, 3072 + kvh * 512:3072 + (kvh + 1) * 512], 512)
                proj(sQ, 512, ALL10, evac_qk(1, 'q', kvh))
                if kvh < 3:
                    sQ = wload(gw_in[:, (kvh + 1) * 512:(kvh + 2) * 512], 512)
                proj(sG, 512, ALL10, evac_g)
                attention(1, kvh * 4, lambda qh, kvh=kvh: kvh)
            gate(10)
            w_out_phase(1, gw_out)

        try:
            main_flow()
        except Stop:
            pass
        flush_defer()

        S.wait_all("sp")
        build_program.stats = (S.n_ops, dict(S.count))
    return nc


def _rope_tables(pos0, dim):
    pos = np.arange(pos0, pos0 + 256)
    row = (pos // GRID_W).astype(np.float32)
    col = (pos % GRID_W).astype(np.float32)
    quarter = dim // 4
    freqs = (ROPE_THETA ** (-np.arange(quarter, dtype=np.float32) / quarter)).astype(np.float32)
    ar = row[:, None] * freqs
    ac = col[:, None] * freqs
    cos = np.concatenate([np.cos(ar), np.cos(ar), np.cos(ac), np.cos(ac)], -1)
    sinm = np.concatenate([-np.sin(ar), np.sin(ar), -np.sin(ac), np.sin(ac)], -1)
    return np.stack([cos, sinm], 1).astype(np.float32)


def _pool_mats(L, t0, t1, src_idx):
    out = np.zeros((4, len(src_idx), t1 - t0), np.float32)
    for g, w in enumerate(POOL_WINDOWS):
        half = w // 2
        for tl, t in enumerate(range(t0, t1)):
            lo, hi = max(t - half, 0), min(t + half, L)
            cnt = float(hi - lo)
            for sl, s in enumerate(src_idx):
                if s < 0:
                    continue
                v = 0.0
                if lo <= s < hi:
                    v += 1.0 / cnt
                if s == t:
                    v -= 1.0
                out[g, sl, tl] = v
    return out


_CACHE = {}


def _make_in_maps(x_prompt, x_sample, cache_a_k, cache_a_v, cache_c_k, cache_c_v, c, c_ctx,
           norm_w, ada_w, ada_b,
           even_w_in, even_q_norm_w, even_k_norm_w, even_lam_q1, even_lam_k1,
           even_lam_q2, even_lam_k2, even_subln_w, even_pool_w, even_pool_scale,
           even_w_out, gqa_w_in, gqa_q_norm_w, gqa_k_norm_w, gqa_w_out):
    f = lambda a: np.ascontiguousarray(np.asarray(a, dtype=np.float32))
    x_prompt, x_sample = f(x_prompt), f(x_sample)
    ada_w, ada_b = f(ada_w), f(ada_b)
    ident = np.eye(128, dtype=np.float32)
    Bp = _pool_mats(256, 0, 256, list(range(256)))
    lamv = np.stack([f(even_lam_q1)[0], f(even_lam_k1)[0], f(even_lam_q2)[0], f(even_lam_k2)[0]], 0)
    shared = {
        "norm_w": f(norm_w), "ew_in": f(even_w_in)[0], "ew_out": f(even_w_out)[0],
        "gw_in": f(gqa_w_in)[0], "gw_out": f(gqa_w_out)[0],
        "eqn": f(even_q_norm_w)[0], "ekn": f(even_k_norm_w)[0], "lamv": f(lamv),
        "subw": f(even_subln_w)[0], "poolw": f(even_pool_w)[0], "pscale": f(even_pool_scale)[0],
        "gqn": f(gqa_q_norm_w)[0], "gkn": f(gqa_k_norm_w)[0], "ident": ident, "Bp": Bp,
        "onesd": np.ones((128, 5200), np.float32),
    }
    in_maps = []
    for core in range(8):
        b, r = core // 4, core % 4
        t0 = r * 256
        halo_idx = list(range(t0 - 8, t0)) + list(range(t0 + 256, t0 + 264))
        xh = np.zeros((16, D), np.float32)
        src_idx = list(range(t0, t0 + 256))
        for i, s in enumerate(halo_idx):
            if 0 <= s < 1024:
                xh[i] = x_sample[b, s]
                src_idx.append(s)
            else:
                src_idx.append(-1)
        m = dict(shared)
        m.update({
            "xp": x_prompt[core * 4:(core + 1) * 4].reshape(1024, D),
            "xs": np.ascontiguousarray(x_sample[b, t0:t0 + 256]),
            "xh": xh,
            "cond": np.stack([f(c_ctx), f(c)[b]], 0),
            "cak": f(cache_a_k)[b, 0], "cav": f(cache_a_v)[b, 0],
            "cck": f(cache_c_k)[b, 0], "ccv": f(cache_c_v)[b, 0],
            "adaw": np.ascontiguousarray(ada_w[:, :, r * 1536:(r + 1) * 1536]),
            "adab": np.ascontiguousarray(ada_b[:, r * 1536:(r + 1) * 1536]),
            "ropeA": _rope_tables(t0, 64), "ropeC": _rope_tables(t0, 128),
            "Bs": _pool_mats(1024, t0, t0 + 256, src_idx),
        })
        in_maps.append(m)
    return in_maps


def _assemble(R):
    y_p = np.concatenate([R[i]["yp"].reshape(4, 256, D) for i in range(8)], 0)
    y_s = np.stack([np.concatenate([R[b * 4 + r]["ys"] for r in range(4)], 0) for b in range(2)], 0)
    nak = np.concatenate([R[i]["nak"] for i in range(8)], 0)[:, None]
    nav = np.concatenate([R[i]["nav"] for i in range(8)], 0)[:, None]
    nck = np.concatenate([R[i]["nck"] for i in range(8)], 0)[:, None]
    ncv = np.concatenate([R[i]["ncv"] for i in range(8)], 0)[:, None]
    return (y_p.astype(np.float32), y_s.astype(np.float32), nak.astype(np.float32),
            nav.astype(np.float32), nck.astype(np.float32), ncv.astype(np.float32))


def kernel(**inputs):
    stage = _CACHE.get("stage", 99)
    if ("nc", stage) not in _CACHE:
        _CACHE[("nc", stage)] = build_program(stage)
    nc = _CACHE[("nc", stage)]
    in_maps = _make_in_maps(**inputs)
    res = run_bass_kernel_spmd(nc, in_maps, core_ids=list(range(8)))
    return _assemble(res.results)
```

```python
import contextlib
import math
import os
import numpy as np
import ml_dtypes
import concourse.bass as bass
import concourse.mybir as mybir
from concourse.bass_utils import run_bass_kernel_spmd

F32 = mybir.dt.float32
BF16 = mybir.dt.bfloat16
AF = mybir.ActivationFunctionType
ALU = mybir.AluOpType
AX = mybir.AxisListType

D = 2048
KC = 16
EPS = 1e-6
NTOK = 1280
HALO0 = 1280
GRID_W = 64
ROPE_THETA = 10000.0
POOL_WINDOWS = (2, 4, 8, 16)


class Sched:
    def __init__(self, nc, stack):
        self.nc = nc
        self.stack = stack
        self.engs = {"pe": nc.tensor, "act": nc.scalar, "dve": nc.vector,
                     "pool": nc.gpsimd, "sp": nc.sync}
        self.sems = {}
        self.count = {}
        for e in self.engs:
            self.sems[e] = stack.enter_context(nc.semaphore("s_" + e))
            self.count[e] = 0
        self.waited = {e: {} for e in self.engs}
        self.last_write = {}
        self.reads = {}
        self.n_ops = 0

    def _deps(self, reads, writes):
        deps = {}

        def add(ev):
            if ev is None:
                return
            k, v = ev
            if deps.get(k, 0) < v:
                deps[k] = v
        for r in reads:
            add(self.last_write.get(r))
        for w in writes:
            add(self.last_write.get(w))
            for ev in self.reads.get(w, ()):
                add(ev)
        return deps

    def _wait(self, eng, deps, skip_self=False):
        e = self.engs[eng]
        for k, v in deps.items():
            if skip_self and k == eng:
                continue
            if self.waited[eng].get(k, 0) >= v:
                continue
            e.wait_ge(self.sems[k], v)
            self.waited[eng][k] = v

    def _record(self, ev, reads, writes):
        for r in reads:
            lst = self.reads.setdefault(r, [])
            lst.append(ev)
            if len(lst) > 64:
                mx = {}
                for k, v in lst:
                    if mx.get(k, 0) < v:
                        mx[k] = v
                self.reads[r] = list(mx.items())
        for w in writes:
            self.last_write[w] = ev
            self.reads[w] = []

    def op(self, eng, reads, writes, fn):
        deps = self._deps(reads, writes)
        self._wait(eng, deps, skip_self=(eng == "pe"))
        ins = fn(self.engs[eng])
        ins.then_inc(self.sems[eng], 1)
        self.count[eng] += 1
        ev = (eng, self.count[eng])
        self._record(ev, reads, writes)
        self.n_ops += 1
        return ev

    def dma(self, queue, semkey, reads, writes, fn, inc=16):
        if semkey not in self.sems:
            nm = "d_" + "".join(ch for ch in str(semkey) if ch.isalnum() or ch == "_")
            self.sems[semkey] = self.stack.enter_context(self.nc.semaphore(nm))
            self.count[semkey] = 0
        deps = self._deps(reads, writes)
        self._wait(queue, deps)
        inss = fn(self.engs[queue])
        if not isinstance(inss, (list, tuple)):
            inss = [inss]
        for ins in inss:
            ins.then_inc(self.sems[semkey], inc)
            self.count[semkey] += inc
        ev = (semkey, self.count[semkey])
        self._record(ev, reads, writes)
        return ev

    def alias(self, new_names, old_names):
        evs = []
        for o in old_names:
            if self.last_write.get(o) is not None:
                evs.append(self.last_write[o])
            evs.extend(self.reads.get(o, ()))
        mx = {}
        for k, v in evs:
            if mx.get(k, 0) < v:
                mx[k] = v
        for n in new_names:
            prev = []
            if self.last_write.get(n) is not None:
                prev.append(self.last_write[n])
            prev.extend(self.reads.get(n, ()))
            m2 = dict(mx)
            for k, v in prev:
                if m2.get(k, 0) < v:
                    m2[k] = v
            self.last_write[n] = None
            self.reads[n] = list(m2.items())

    def wait_all(self, eng):
        deps = {k: c for k, c in self.count.items() if c > 0}
        self._wait(eng, deps)


class Ring:
    def __init__(self, name, n):
        self.name, self.n, self.i = name, n, 0

    def next(self):
        s = self.i % self.n
        self.i += 1
        return s, (self.name, s)


def build_program(stage=99):
    nc = bass.Bass("TRN2", target_bir_lowering=False)

    def din(name, shape, dt=F32):
        return nc.dram_tensor(name, list(shape), dt, kind="ExternalInput").ap()

    def dout(name, shape, dt=F32):
        return nc.dram_tensor(name, list(shape), dt, kind="ExternalOutput").ap()

    def dint(name, shape, dt=F32):
        return nc.dram_tensor(name, list(shape), dt).ap()

    xp = din("xp", [1024, D]); xs = din("xs", [256, D]); xh = din("xh", [16, D])
    cond = din("cond", [2, D])
    cak = din("cak", [8, 256, 128]); cav = din("cav", [8, 256, 128])
    cck = din("cck", [4, 256, 128]); ccv = din("ccv", [4, 256, 128])
    norm_w = din("norm_w", [2, D])
    adaw = din("adaw", [2, D, 1536]); adab = din("adab", [2, 1536])
    ew_in = din("ew_in", [D, 6144]); ew_out = din("ew_out", [D, D])
    gw_in = din("gw_in", [D, 5120]); gw_out = din("gw_out", [D, D])
    eqn = din("eqn", [64]); ekn = din("ekn", [64]); lamv = din("lamv", [4, 64])
    subw = din("subw", [128]); poolw = din("poolw", [4, 256, 256]); pscale = din("pscale", [1024])
    gqn = din("gqn", [128]); gkn = din("gkn", [128])
    ident = din("ident", [128, 128])
    ropeA = din("ropeA", [256, 2, 64]); ropeC = din("ropeC", [256, 2, 128])
    Bp = din("Bp", [4, 256, 256]); Bs = din("Bs", [4, 272, 256])
    onesd = din("onesd", [128, 5200])

    yp = dout("yp", [1024, D]); ys = dout("ys", [256, D])
    nak = dout("nak", [4, 8, 256, 128]); nav = dout("nav", [4, 8, 256, 128])
    nck = dout("nck", [4, 4, 256, 128]); ncv = dout("ncv", [4, 4, 256, 128])

    x1 = dint("x1", [NTOK, D])
    agm_in = [dint("agm_in%d" % l, [2, 1536]) for l in range(2)]
    agm_out = [dint("agm_out%d" % l, [8, 1536]) for l in range(2)]
    mlin = dint("mlin", [2, 2, 6144])
    warm_in = dint("warm_in", [2, 64]); warm_out = dint("warm_out", [8, 64])
    agkv_in = [dint("agkv_in%d" % i, [1024, 256], BF16) for i in range(3)]
    agkv_out = [dint("agkv_out%d" % i, [4096, 256], BF16) for i in range(3)]

    with contextlib.ExitStack() as st:
        S = Sched(nc, st)

        def sbt(name, shape, dt):
            return st.enter_context(nc.sbuf_tensor(name, list(shape), dt))

        def pst(name, shape, dt=F32):
            return st.enter_context(nc.psum_tensor(name, list(shape), dt))

        R1 = sbt("R1", [128, 16 * 1296], BF16)
        R2 = sbt("R2", [128, 16 * 1280], BF16)
        WB = sbt("WB", [128, 2, 16, 512], BF16)
        R5 = sbt("R5", [128, 8256], BF16)
        R6 = sbt("R6", [128, 10320], BF16)
        QT = sbt("QT", [128, 4, 1280], BF16)
        SG = sbt("SG", [128, 10, 512], BF16)
        PT = sbt("PT", [128, 2, 1280], BF16)

        hT = R1[:, :].rearrange("p (k t) -> p k t", k=16)
        yT = R2[:, :].rearrange("p (k t) -> p k t", k=16)
        KTp = R5[:, 0:4096].rearrange("p (h t) -> p h t", h=4)
        V1p = R5[:, 4096:8256].rearrange("p (t h c) -> p t h c", t=8, h=4)
        KTs = R6[:, 0:5120].rearrange("p (h t) -> p h t", h=4)
        V1s = R6[:, 5120:10320].rearrange("p (t h c) -> p t h c", t=10, h=4)
        xts = [R2[:, s * 4096:(s + 1) * 4096].bitcast(F32) for s in range(3)]
        sqj = R2[:, 12288:12288 + 2048]
        xsb = [R2[:, 14336 + s * 2048:14336 + (s + 1) * 2048] for s in range(2)]
        condt = R2[:, 0:4096].bitcast(F32)
        sct = R2[:, 4096:8192].bitcast(F32)
        adabt = R2[:, 8192:8192 + 6144].bitcast(F32).rearrange("p (l n) -> p l n", l=2)
        mrow = R2[:, 14336:14336 + 6144].bitcast(F32).rearrange("p (l n) -> p l n", l=2)
        gbc = [R6[:, j * 4096:(j + 1) * 4096].bitcast(F32) for j in range(2)]
        xblk = [R5[:, s * 1024:(s + 1) * 1024].bitcast(F32) for s in range(4)]
        oblk = [R5[:, 4096 + s * 1024: 4096 + (s + 1) * 1024].bitcast(F32) for s in range(2)]
        u_tok = R5[:, 0:11 * 512].rearrange("p (t c) -> p t c", t=11)
        Bpt = R6[:, 0:2048].rearrange("p (g s t) -> p g s t", g=4, s=2)
        Bst = R6[:, 2048:2048 + 3072].rearrange("p (g s t) -> p g s t", g=4, s=3)
        pwt = R6[:, 5120:5120 + 2048].rearrange("p (g c d) -> p g c d", g=4, c=2)
        pscbc = R6[:, 7168:7168 + 2048].bitcast(F32)

        QTf = QT[:, :, :].rearrange("p h t -> p (h t)")
        SGf = SG[:, :, :].rearrange("p t c -> p (t c)")
        PTf = PT[:, :, :].rearrange("p s q -> p (s q)")
        xts_alt = [QTf[:, 0:4096].bitcast(F32), SGf[:, 0:4096].bitcast(F32)]
        sqj_alt = PTf[:, 0:2048]
        xsb_alt = [R5[:, 6144:8192], R6[:, 8192:10240]]
        identf = sbt("identf", [128, 128], F32)
        identb = sbt("identb", [128, 128], BF16)
        dg = sbt("dg", [128, 2, 128], F32)
        stat = sbt("stat", [128, 16, 16], F32)
        shsc = sbt("shsc", [128, 2, 2, 2, 16], F32)
        nwt = sbt("nwt", [128, 2, 16], F32)
        s1t = sbt("s1t", [128, 2, 2, 16], F32)
        scT = sbt("scT", [128, 16, 2], BF16)
        qnw = sbt("qnw", [128, 2, 128], F32)
        lams = sbt("lams", [128, 8], F32)
        nlam = sbt("nlam", [128, 1], F32)
        wsub = sbt("wsub", [128, 128], F32)
        ropet = sbt("ropet", [128, 2, 2, 128], F32)
        sqf = sbt("sqf", [128, 1, 512], F32)
        qkf = sbt("qkf", [128, 2, 512], F32)
        qkg = sbt("qkg", [128, 1, 512], F32)
        qkb = sbt("qkb", [128, 2, 512], BF16)
        kst = sbt("kst", [128, 1, 512], F32)
        vst = sbt("vst", [128, 1, 512], F32)
        lamt = qkf[:, 0, 0:256].rearrange("p (a d) -> p a d", a=4)
        lampt = qkf[:, 0, 256:384].rearrange("p (a d) -> p a d", a=2)
        onesr = qkf[:, 0, 384:512]
        ktst = sbt("ktst", [128, 4, 256], BF16)
        vsst = sbt("vsst", [128, 2, 512], BF16)
        cstg = sbt("cstg", [128, 2, 4, 128], BF16)
        otm = sbt("otm", [128, 4, 128], F32)
        ofm = sbt("ofm", [128, 2, 128], F32)
        ytk = sbt("ytk", [128, 2, 256], BF16)
        pooledT = sbt("pooledT", [128, 2, 2, 256], BF16)

        pP = [pst("pP%d" % i, [128, 512]) for i in range(2)]
        pS = [pst("pS%d" % i, [128, 512]) for i in range(2)]
        pO = [pst("pO%d" % i, [128, 2, 256]) for i in range(4)]

        statr = Ring("stat", 16)
        sqr, qkfr, qkgr, qkbr = Ring("sqf", 1), Ring("qkf", 2), Ring("qkg", 1), Ring("qkb", 2)
        kstr, vstr = Ring("kst", 1), Ring("vst", 1)
        otr, ofr, ytr, ytr4 = Ring("otm", 4), Ring("ofm", 2), Ring("ytkp", 2), Ring("ytk", 4)
        pPr, pSr, pOr = Ring("pP", 2), Ring("pS", 2), Ring("pO", 4)
        ptr_, ppr = Ring("ptmp", 2), Ring("pooledT", 2)

        R2mod = [("xt", 0), ("xt", 1), ("xt", 2), "sqj", ("xsb", 0), ("xsb", 1)]
        R2ada = ["condt", "sct"]
        ALTmod = [("xta", 0), ("xta", 1), "sqja", ("xsba", 0), ("xsba", 1)]
        QSP_names = [("QT", t) for t in range(10)] + [("SG", t) for t in range(10)] + [("PT", 0), ("PT", 1)]
        R1wo = [("gbc", 0), ("gbc", 1), ("xblk", 0), ("xblk", 1), ("xblk", 2), ("xblk", 3), ("oblk", 0), ("oblk", 1)]
        hT_names = [("hT", t, k) for t in range(11) for k in range(16)]
        yT_names = [("yT", k) for k in range(16)]
        R5att = ["KTp", "V1p"]
        R6att = ["KTs", "V1s"]
        R6pool = ["Bt", "pwt", "pscbc"]

        S.dma("sp", "c0", [], ["identf", "qnw0", "condt"], lambda e: [
            e.dma_start(out=identf[:], in_=ident),
            e.dma_start(out=condt[0:2, :], in_=cond),
            e.dma_start(out=wsub[:], in_=subw.partition_broadcast(128)),
            e.dma_start(out=lamt[0:1, :, :], in_=lamv.rearrange("(o a) d -> o a d", o=1)),
        ])

        def load_layer_consts(L):
            if L == 0:
                S.dma("sp", "c2", [], ["qnw"], lambda e: [
                    e.dma_start(out=qnw[:, 0, 0:64], in_=eqn.partition_broadcast(128)),
                    e.dma_start(out=qnw[:, 1, 0:64], in_=ekn.partition_broadcast(128)),
                ] + [e.dma_start(out=ropet[:, t, :, 0:64], in_=ropeA[t * 128:(t + 1) * 128]) for t in range(2)])
            else:
                S.dma("sp", "c2", [], ["qnw"], lambda e: [
                    e.dma_start(out=qnw[:, 0, :], in_=gqn.partition_broadcast(128)),
                    e.dma_start(out=qnw[:, 1, :], in_=gkn.partition_broadcast(128)),
                ] + [e.dma_start(out=ropet[:, t, :, :], in_=ropeC[t * 128:(t + 1) * 128]) for t in range(2)])
        S.op("dve", ["identf"], ["identb"], lambda e: e.tensor_copy(out=identb[:], in_=identf[:]))
        LAM_INIT = 0.8 - 0.6 * math.exp(0.0)
        S.op("dve", ["qnw0"], ["wsub"], lambda e: e.tensor_scalar(
            out=wsub[:], in0=wsub[:], scalar1=1.0 - LAM_INIT, scalar2=None, op0=ALU.mult))
        S.op("dve", ["qnw0"], ["lamp"], lambda e: e.tensor_tensor(
            out=lampt[0:1, :, :], in0=lamt[0:1, 0:4:2, :], in1=lamt[0:1, 1:4:2, :], op=ALU.mult))
        S.op("dve", ["lamp"], ["lams0"], lambda e: e.tensor_reduce(
            out=lams[0:1, 0:2], in_=lampt[0:1, :, :], axis=AX.X, op=ALU.add))
        S.op("act", ["lams0"], ["lams1"], lambda e: e.activation(
            out=lams[0:1, 2:4], in_=lams[0:1, 0:2], func=AF.Exp))
        S.op("dve", ["lams1"], ["lams2"], lambda e: e.tensor_tensor(
            out=lams[0:1, 4:5], in0=lams[0:1, 3:4], in1=lams[0:1, 2:3], op=ALU.subtract))
        S.op("dve", ["lams2"], ["lams3"], lambda e: e.tensor_scalar(
            out=lams[0:1, 5:6], in0=lams[0:1, 4:5], scalar1=-LAM_INIT, scalar2=None, op0=ALU.add))
        S.op("dve", [], ["onesr"], lambda e: e.memset(onesr[0:1, :], 1.0))
        pi0, pn0 = pPr.next()
        S.op("pe", ["onesr", "lams3"], [pn0], lambda e: e.matmul(
            pP[pi0][:, 0:1], lhsT=onesr[0:1, :], rhs=lams[0:1, 5:6], start=True, stop=True))
        S.op("dve", [pn0], ["nlam"], lambda e: e.tensor_copy(out=nlam[:], in_=pP[pi0][:, 0:1]))

        S.op("act", ["condt"], ["sct"], lambda e: e.activation(out=sct[0:2, :], in_=condt[0:2, :], func=AF.Silu))

        ps0, psn0 = pSr.next()

        def mm_sct(e):
            for kc in range(16):
                ins = e.matmul(pS[ps0][:, 2 * kc:2 * kc + 2], lhsT=sct[0:2, kc * 128:(kc + 1) * 128],
                               rhs=identf[0:2, 0:2], start=True, stop=True)
            return ins
        S.op("pe", ["sct", "identf"], [psn0], mm_sct)
        S.op("dve", [psn0], ["scT"], lambda e: e.tensor_copy(
            out=scT[:].rearrange("p k j -> p (k j)"), in_=pS[ps0][:, 0:32]))

        wstate = {"n": 0}

        def wload(src, ncols, slot=None):
            if slot is None:
                slot = wstate["n"] % 2
                wstate["n"] += 1
            S.dma("pool", ("wb", slot), [], [("wb", slot)], lambda e: e.dma_start(
                out=WB[:, slot, :, 0:ncols], in_=src.rearrange("(kc p) n -> p kc n", p=128)))
            return slot

        GROUPS = [[0, 1, 2, 3], [4, 5, 6, 7]]
        s1_names = [("s1t", l, j) for l in range(2) for j in range(2)]

        def ada_block(L, nb, slot):
            S.dma("sp", "c1", [], [("vst", 0)], lambda e: [
                e.dma_start(out=vst[p:p + 1, 0, :], in_=adab[L:L + 1, nb * 512:(nb + 1) * 512]) for p in range(2)])
            pi, pn = pPr.next()

            def mm_ada(e):
                for kc in range(16):
                    ins = e.matmul(pP[pi][0:2, :], lhsT=scT[:, kc, :], rhs=WB[:, slot, kc, :],
                                   start=(kc == 0), stop=(kc == 15))
                return ins
            S.op("pe", ["scT", ("wb", slot)], [pn], mm_ada)
            S.op("dve", [pn, ("vst", 0)], [("kst", 0)], lambda e: e.tensor_tensor(
                out=kst[0:2, 0, :], in0=pP[pi][0:2, :], in1=vst[0:2, 0, :], op=ALU.add))
            S.dma("sp", "c1", [("kst", 0)], [("agm_in", L)], lambda e: e.dma_start(
                out=agm_in[L][:, nb * 512:(nb + 1) * 512], in_=kst[0:2, 0, :]))

        def ada_finish(L, split=False):
            S.dma("pool", "cc", [("agm_in", L)], [("agm_out", L)], lambda e: e.collective_compute(
                "AllGather", ALU.bypass, replica_groups=GROUPS, ins=[agm_in[L]], outs=[agm_out[L]]), inc=1)
            S.dma("sp", "c1", [("agm_out", L)], [("mlin", L)], lambda e: e.dma_start(
                out=mlin[L].rearrange("j (r i) -> r j i", r=4),
                in_=agm_out[L].rearrange("(r j) i -> r j i", r=4)))
            if split:
                mt_a = otm[:, 0:2, :]
                mt_b = otm[:, 2, :]
                stg = [("otm", 0), ("otm", 1), ("otm", 2)]
            else:
                mt_a = kst[:, 0, 0:256].rearrange("p (i c) -> p i c", i=2)
                mt_b = vst[:, 0, 0:128]
                stg = [("kst", 0), ("vst", 0)]
            S.dma("sp", "c1", [("mlin", L)], stg, lambda e: [
                e.dma_start(out=mt_a[0:32, j, :], in_=mlin[L, j, 0:4096].rearrange("(a p) -> a p", p=128))
                for j in range(2)] + [
                e.dma_start(out=mt_b[0:16, :], in_=norm_w[L].rearrange("(a p) -> a p", p=128))])
            def part_b():
                ps1, psn1 = pSr.next()

                def mm_mt(e):
                    for j in range(2):
                        ins = e.matmul(pS[ps1][:, j * 32:(j + 1) * 32], lhsT=mt_a[0:32, j, :], rhs=identf[0:32, 0:32], start=True, stop=True)
                    ins = e.matmul(pS[ps1][:, 64:80], lhsT=mt_b[0:16, :], rhs=identf[0:16, 0:16], start=True, stop=True)
                    return ins
                S.op("pe", stg + ["identf"], [psn1], mm_mt)
                S.op("dve", [psn1], ["shsc"], lambda e: e.tensor_copy(
                    out=shsc[:, L, :, :, :].rearrange("p j t k -> p (j t k)"), in_=pS[ps1][:, 0:64]))
                S.op("dve", [psn1], ["shsc"], lambda e: e.tensor_copy(out=nwt[:, L, :], in_=pS[ps1][:, 64:80]))
                for j in range(2):
                    S.op("dve", ["shsc"], [("s1t", L, j)], lambda e: e.scalar_tensor_tensor(
                        out=s1t[:, L, j, :], in0=shsc[:, L, j, 1, :], scalar=1.0, in1=nwt[:, L, :],
                        op0=ALU.add, op1=ALU.mult))

            if split:
                return part_b
            part_b()

        ada0_slots = [wload(adaw[0, :, nb * 512:(nb + 1) * 512], 512) for nb in range(2)]
        S.dma("sp", "c1", ["identf"], ["warm_in"], lambda e: e.dma_start(out=warm_in, in_=identf[0:2, 0:64]))
        S.dma("pool", "cc", ["warm_in"], ["warm_out"], lambda e: e.collective_compute(
            "AllGather", ALU.bypass, replica_groups=GROUPS, ins=[warm_in], outs=[warm_out]), inc=1)

        def tile_tok0(tt):
            return HALO0 if tt == 10 else tt * 128

        DEFER = []

        def defer(fn):
            DEFER.append(fn)

        def flush_defer(keep=0):
            while len(DEFER) > keep:
                DEFER.pop(0)()

        def make_modulate(L, alt):
            if alt:
                xt_t, sq_t, xb_t, nslot = xts_alt, sqj_alt, xsb_alt, 2
                nxt_, nsq, nxb, q = "xta", "sqja", "xsba", "pool"
            else:
                xt_t, sq_t, xb_t, nslot = xts, sqj, xsb, 3
                nxt_, nsq, nxb, q = "xt", "sqj", "xsb", "sp"

            def xload(tt):
                npt = 16 if tt == 10 else 128
                slot = tt % nslot
                if L == 0:
                    src = xp[tt * 128:(tt + 1) * 128] if tt < 8 else (xs[(tt - 8) * 128:(tt - 7) * 128] if tt < 10 else xh)
                    rd = []
                else:
                    src = x1[tt * 128:(tt + 1) * 128]
                    rd = [("x1", tt)]
                S.dma(q, (nxt_, slot), rd, [(nxt_, slot)], lambda e: e.dma_start(out=xt_t[slot][0:npt, :], in_=src))

            def stats(tt):
                npt = 16 if tt == 10 else 128
                slot = tt % nslot
                bslot = tt % 2
                si, sn = statr.next()
                S.op("act", [(nxt_, slot)], [nsq, sn], lambda e: e.activation(
                    out=sq_t[0:npt, :], in_=xt_t[slot][0:npt, :], func=AF.Square, accum_out=stat[0:npt, si, 0:1]))
                S.op("act", [sn], [sn], lambda e: e.activation(
                    out=stat[0:npt, si, 1:2], in_=stat[0:npt, si, 0:1], func=AF.Ln, scale=1.0 / D, bias=EPS))
                S.op("act", [sn], [sn], lambda e: e.activation(
                    out=stat[0:npt, si, 2:3], in_=stat[0:npt, si, 1:2], func=AF.Exp, scale=-0.5))
                S.op("dve", [sn, (nxt_, slot)], [(nxb, bslot)], lambda e: e.tensor_scalar(
                    out=xb_t[bslot][0:npt, :], in0=xt_t[slot][0:npt, :], scalar1=stat[0:npt, si, 2:3],
                    scalar2=None, op0=ALU.mult))

            def pe_part(tt):
                npt = 16 if tt == 10 else 128
                slot = tt % 2
                t0 = tile_tok0(tt)
                for g in range(2):
                    pi, pn = pOr.next()
                    pv = pO[pi][:, :, :].rearrange("p a b -> p (a b)").bitcast(BF16)

                    def mm(e):
                        for j in range(8):
                            c = g * 8 + j
                            ins = e.transpose(pv[:, j * 128:j * 128 + npt], xb_t[slot][0:npt, c * 128:(c + 1) * 128],
                                              identb[0:npt, 0:npt])
                        return ins
                    S.op("pe", [(nxb, slot), "identb"], [pn], mm)
                    pv3 = pv.rearrange("p (k t) -> p k t", k=8)[:, :, 0:npt]
                    wr = [("hT", tt, g * 8 + j) for j in range(8)]
                    if g == 0:
                        S.op("dve", [pn], wr, lambda e: e.tensor_copy(out=hT[:, 0:8, t0:t0 + npt], in_=pv3))
                    else:
                        S.op("act", [pn], wr, lambda e: e.activation(out=hT[:, 8:16, t0:t0 + npt], in_=pv3, func=AF.Copy))
            return xload, stats, pe_part

        def modulate(L, tiles, hooks=None):
            flush_defer()
            S.alias(R2mod, yT_names + R2ada)
            xload, stats, pe_part = make_modulate(L, False)
            xload(tiles[0])
            xload(tiles[1])
            stats(tiles[0])
            for i, tt in enumerate(tiles):
                if i + 2 < len(tiles):
                    xload(tiles[i + 2])
                if i + 1 < len(tiles):
                    stats(tiles[i + 1])
                pe_part(tt)
                if hooks and tt in hooks:
                    hooks[tt]()

        def modulate_affine(L, with_halo):
            tiles_p = list(range(8))
            tiles_s = [8, 9] + ([10] if with_halo else [])
            for jsel, tl, a0, a1 in ((0, tiles_p, 0, 1024), (1, tiles_s, 1024, 1296 if with_halo else 1280)):
                for kc in range(16):
                    names = [("hT", tt, kc) for tt in tl]
                    if kc % 2 == 0:
                        S.op("dve", names + s1_names + ["shsc"], names, lambda e: e.tensor_scalar(
                            out=hT[:, kc, a0:a1], in0=hT[:, kc, a0:a1],
                            scalar1=s1t[:, L, jsel, kc:kc + 1], scalar2=shsc[:, L, jsel, 0, kc:kc + 1],
                            op0=ALU.mult, op1=ALU.add))
                    else:
                        S.op("act", names + s1_names + ["shsc"], names, lambda e: e.activation(
                            out=hT[:, kc, a0:a1], in_=hT[:, kc, a0:a1], func=AF.Identity,
                            scale=s1t[:, L, jsel, kc:kc + 1], bias=shsc[:, L, jsel, 0, kc:kc + 1]))

        def proj(slot, ncols, tiles, evac):
            for tt in tiles:
                npt = 16 if tt == 10 else 128
                t0 = tile_tok0(tt)
                pi, pn = pPr.next()

                def mm(e, pi=pi, npt=npt, t0=t0):
                    for kc in range(16):
                        ins = e.matmul(pP[pi][0:npt, 0:ncols], lhsT=hT[:, kc, t0:t0 + npt], rhs=WB[:, slot, kc, 0:ncols],
                                       start=(kc == 0), stop=(kc == 15))
                    return ins
                S.op("pe", [("hT", tt, k) for k in range(16)] + [("wb", slot)], [pn], mm)
                n0 = len(DEFER)
                evac(tt, pP[pi], pn)
                flush_defer(keep=len(DEFER) - n0)

        def transpose4(src_bf, src_name, dst_fn, dst_names, nblk=4):
            pi, pn = pSr.next()
            pv = pS[pi][:, :].bitcast(BF16)[:, 0:nblk * 128]

            def mm(e):
                for b in range(nblk):
                    ins = e.transpose(pv[:, b * 128:(b + 1) * 128], src_bf[:, b * 128:(b + 1) * 128], identb[:])
                return ins
            S.op("pe", [src_name, "identb"], [pn], mm)
            if int(os.environ.get("MK_SUBK", "99")) <= 5:
                return
            S.op("act", [pn], dst_names, lambda e: dst_fn(e, pv.rearrange("p (b t) -> p b t", b=nblk)))

        def evac_qk(L, kind, hg):
            dk = 64 if L == 0 else 128
            nch = 512 // dk
            wrow = qnw[:, 0 if kind == 'q' else 1, 0:dk]
            rope = ropet[:, :, :, 0:dk]
            q4 = dk // 4

            SUBK = int(os.environ.get("MK_SUBK", "99"))

            def f(tt, ps, pn):
                sample = tt >= 8
                if SUBK <= 1:
                    return
                si, sn = statr.next()
                qi, qn_ = sqr.next()
                S.op("act", [pn], [qn_], lambda e: e.activation(out=sqf[:, qi, :], in_=ps[:, :], func=AF.Square))
                S.op("dve", [qn_], [sn], lambda e: e.tensor_reduce(
                    out=stat[:, si, 0:nch], in_=sqf[:, qi, :].rearrange("p (c d) -> p c d", d=dk), axis=AX.X, op=ALU.add))
                S.op("act", [sn], [sn], lambda e: e.activation(
                    out=stat[:, si, 8:8 + nch], in_=stat[:, si, 0:nch], func=AF.Ln, scale=1.0 / dk, bias=EPS))
                S.op("act", [sn], [sn], lambda e: e.activation(
                    out=stat[:, si, 0:nch], in_=stat[:, si, 8:8 + nch], func=AF.Exp, scale=-0.5))
                if SUBK <= 2:
                    return
                fi, fn_ = qkfr.next()
                S.op("dve", [pn, sn], [fn_], lambda e: e.tensor_tensor(
                    out=qkf[:, fi, :].rearrange("p (c d) -> p c d", d=dk), in0=ps[:, :].rearrange("p (c d) -> p c d", d=dk),
                    in1=stat[:, si, 0:nch].unsqueeze(2).to_broadcast([128, nch, dk]), op=ALU.mult))
                if SUBK <= 3:
                    return
                bi, bn_ = qkbr.next()
                wbc = wrow.unsqueeze(1).to_broadcast([128, nch, dk])
                if not sample:
                    if kind == 'k':
                        ki, kn_ = kstr.next()
                        S.op("dve", [fn_, "qnw"], [kn_], lambda e: e.tensor_tensor(
                            out=kst[:, ki, :].rearrange("p (c d) -> p c d", d=dk),
                            in0=qkf[:, fi, :].rearrange("p (c d) -> p c d", d=dk), in1=wbc, op=ALU.mult))
                        seq, s0 = tt // 2, (tt % 2) * 128
                        if L == 0:
                            dst = nak[seq, hg * 4:(hg + 1) * 4, s0:s0 + 128, :]
                        else:
                            dst = nck[seq, :, s0:s0 + 128, :]
                        S.dma("sp", kn_, [kn_], [("out_k", L, hg, tt)], lambda e: e.dma_start(
                            out=dst.rearrange("h s d -> s h d"), in_=kst[:, ki, :].rearrange("p (h d) -> p h d", h=4)))
                        S.op("act", [kn_], [bn_], lambda e: e.activation(out=qkb[:, bi, :], in_=kst[:, ki, :], func=AF.Copy))
                    else:
                        S.op("dve", [fn_, "qnw"], [bn_], lambda e: e.tensor_tensor(
                            out=qkb[:, bi, :].rearrange("p (c d) -> p c d", d=dk),
                            in0=qkf[:, fi, :].rearrange("p (c d) -> p c d", d=dk), in1=wbc, op=ALU.mult))
                else:
                    ts = tt - 8
                    gi, gn_ = qkgr.next()
                    S.op("dve", [fn_, "qnw"], [gn_], lambda e: e.tensor_tensor(
                        out=qkg[:, gi, :].rearrange("p (c d) -> p c d", d=dk),
                        in0=qkf[:, fi, :].rearrange("p (c d) -> p c d", d=dk), in1=wbc, op=ALU.mult))
                    cosb = rope[:, ts, 0, :].unsqueeze(1).to_broadcast([128, nch, dk])
                    S.op("dve", [gn_, "qnw"], [fn_], lambda e: e.tensor_tensor(
                        out=qkf[:, fi, :].rearrange("p (c d) -> p c d", d=dk),
                        in0=qkg[:, gi, :].rearrange("p (c d) -> p c d", d=dk), in1=cosb, op=ALU.mult))
                    x5 = qkg[:, gi, :].rearrange("p (c a b q) -> p c a b q", a=2, b=2, q=q4)
                    t5 = sqf[:, qi, :].rearrange("p (c a b q) -> p c a b q", a=2, b=2, q=q4)
                    s5 = rope[:, ts, 1, :].rearrange("p (a b q) -> p a b q", a=2, b=2)
                    for b in range(2):
                        S.op("dve", [gn_, "qnw"], [qn_], lambda e, b=b: e.tensor_tensor(
                            out=t5[:, :, :, b, :], in0=x5[:, :, :, 1 - b, :],
                            in1=s5[:, :, b, :].unsqueeze(1).to_broadcast([128, nch, 2, q4]), op=ALU.mult))
                    S.op("dve", [fn_, qn_], [bn_], lambda e: e.tensor_tensor(
                        out=qkb[:, bi, :], in0=qkf[:, fi, :], in1=sqf[:, qi, :], op=ALU.add))
                t0 = tt * 128
                if SUBK <= 4:
                    return
                if kind == 'q':
                    defer(lambda: transpose4(qkb[:, bi, :], bn_, lambda e, pv: e.activation(out=QT[:, :, t0:t0 + 128], in_=pv, func=AF.Copy), [("QT", tt)]))
                elif not sample:
                    defer(lambda: transpose4(qkb[:, bi, :], bn_, lambda e, pv: e.activation(out=KTp[:, :, t0:t0 + 128], in_=pv, func=AF.Copy), ["KTp"]))
                else:
                    ts = tt - 8
                    defer(lambda: transpose4(qkb[:, bi, :], bn_, lambda e, pv: e.activation(out=ktst[:, :, ts * 128:(ts + 1) * 128], in_=pv, func=AF.Copy), [("ktst", ts)]))
            return f

        def evac_v(L, hg):
            def f(tt, ps, pn):
                if tt < 8:
                    vi, vn_ = vstr.next()
                    S.op("act", [pn], [vn_], lambda e: e.activation(out=vst[:, vi, :], in_=ps[:, :], func=AF.Copy))
                    seq, s0 = tt // 2, (tt % 2) * 128
                    if L == 0:
                        dst = nav[seq, hg * 4:(hg + 1) * 4, s0:s0 + 128, :]
                    else:
                        dst = ncv[seq, :, s0:s0 + 128, :]
                    S.dma("sp", vn_, [vn_], [("out_v", L, hg, tt)], lambda e: e.dma_start(
                        out=dst.rearrange("h s d -> s h d"), in_=vst[:, vi, :].rearrange("p (h d) -> p h d", h=4)))
                    S.op("dve", [vn_], ["V1p"], lambda e: e.tensor_copy(
                        out=V1p[:, tt, :, 0:128], in_=vst[:, vi, :].rearrange("p (h d) -> p h d", h=4)))
                else:
                    ts = tt - 8
                    S.op("dve", [pn], [("vsst", ts)], lambda e: e.tensor_copy(out=vsst[:, ts, :], in_=ps[:, :]))
            return f

        def evac_g(tt, ps, pn):
            S.op("act", [pn], [("SG", tt)], lambda e: e.activation(out=SG[:, tt, :], in_=ps[:, :], func=AF.Silu))

        def evac_g_sub(tt, ps, pn):
            qi, qn_ = qkfr.next()
            S.op("act", [pn], [qn_], lambda e: e.activation(out=qkf[:, qi, :], in_=ps[:, :], func=AF.Silu))
            S.op("dve", [qn_, "wsub"], [("SG", tt)], lambda e: e.tensor_tensor(
                out=SG[:, tt, :].rearrange("p (h d) -> p h d", h=4), in0=qkf[:, qi, :].rearrange("p (h d) -> p h d", h=4),
                in1=wsub[:, :].unsqueeze(1).to_broadcast([128, 4, 128]), op=ALU.mult))

        def evac_u(tt, ps, pn):
            npt = 16 if tt == 10 else 128
            S.op("dve", [pn], [("u_tok", tt)], lambda e: e.tensor_copy(out=u_tok[0:npt, tt, :], in_=ps[0:npt, :]))

        def kv_prefetch(L, hg):
            ck = cak if L == 0 else cck
            cv = cav if L == 0 else ccv
            h0 = hg * 4 if L == 0 else 0
            S.dma("pool", "cstg", [], ["cstg"], lambda e: [e.dma_start(
                out=cstg[:, t, :, :], in_=ck[h0:h0 + 4, t * 128:(t + 1) * 128, :].rearrange("h p d -> p h d")) for t in range(2)])
            S.dma("pool", "v1s_a", [], ["V1s"], lambda e: [e.dma_start(
                out=V1s[:, t, :, 0:128], in_=cv[h0:h0 + 4, t * 128:(t + 1) * 128, :].rearrange("h p d -> p h d")) for t in range(2)])

        def kv_exchange(L, hg, agi):
            flush_defer()
            ain, aout = agkv_in[agi], agkv_out[agi]
            S.dma("sp", "kvst", [("ktst", 0), ("ktst", 1), ("vsst", 0), ("vsst", 1)], [("agin", agi)], lambda e: [e.dma_start(
                out=ain[0:512, :].rearrange("(h d) t -> d h t", h=4), in_=ktst[:, :, :]), e.dma_start(
                out=ain[512:1024, :].rearrange("(t p a) b -> p t (a b)", t=2, a=2), in_=vsst[:, :, :])])
            S.dma("pool", "cc", [("agin", agi)], [("agout", agi)], lambda e: e.collective_compute(
                "AllGather", ALU.bypass, replica_groups=GROUPS, ins=[ain], outs=[aout]), inc=1)
            for t in range(2):
                transpose4(cstg[:, t, :, :].rearrange("p h d -> p (h d)"), "cstg",
                           lambda e, pv, t=t: e.activation(out=KTs[:, :, t * 128:(t + 1) * 128], in_=pv, func=AF.Copy), ["KTs"])
            aview = aout.rearrange("(r x) t -> r x t", r=4)
            S.dma("sp", "kts_b", [("agout", agi)], ["KTs"], lambda e: [e.dma_start(
                out=KTs[:, :, 256 + r * 256:256 + (r + 1) * 256],
                in_=aview[r, 0:512, :].rearrange("(h d) t -> d h t", h=4)) for r in range(4)])
            S.dma("sp", "v1s_b", [("agout", agi)], ["V1s"], lambda e: [e.dma_start(
                out=V1s[:, 2 + 2 * r + t, :, 0:128],
                in_=aview[r, 512 + t * 256:512 + (t + 1) * 256, :].rearrange("(p a) b -> p (a b)", a=2).rearrange("p (h d) -> p h d", h=4))
                for r in range(4) for t in range(2)])

        def set_ones():
            S.dma("pool", "c_ones", [], ["V1p", "V1s"], lambda e: [
                e.dma_start(out=R5[:, 4096:8256], in_=onesd[:, 0:4160]),
                e.dma_start(out=R6[:, 5120:10320], in_=onesd[:, 0:5200])])

        def attention(L, kc0, kv_of_head, hooks=None):
            flush_defer()
            nj = 2 if L == 0 else 1
            dk = 64 if L == 0 else 128
            sc = dk ** -0.5
            units = []
            for seq in range(4):
                for qh in range(4):
                    for j in range(nj):
                        units.append((seq, qh, j, (0, 1)))
            for qh in range(4):
                for qt in range(2):
                    for j in range(nj):
                        units.append((4, qh, j, (qt,)))

            def keys_of(seq):
                if seq < 4:
                    return [("p", seq * 2 + t) for t in range(2)]
                return [("s", t) for t in range(10)]

            def scores(u, ui):
                seq, qh, j, qts = u
                kvh = kv_of_head(qh)
                kts = keys_of(seq)
                nq = 128 * len(qts)
                q0 = seq * 256 + qts[0] * 128
                pslot = ui % 2
                per = 512 // nq
                ptv = PT[:, pslot, 0:len(kts) * nq].rearrange("p (k q) -> p k q", q=nq)
                for c in range(0, len(kts), per):
                    pi, pn = pSr.next()
                    nper = min(per, len(kts) - c)

                    def mm(e, c=c, pi=pi, nper=nper):
                        for t in range(nper):
                            kind, kt = kts[c + t]
                            if kind == "p":
                                ksrc = KTp[j * dk:(j + 1) * dk, kvh, kt * 128:(kt + 1) * 128]
                            else:
                                ksrc = KTs[j * dk:(j + 1) * dk, kvh, kt * 128:(kt + 1) * 128]
                            ins = e.matmul(pS[pi][:, t * nq:(t + 1) * nq], lhsT=ksrc,
                                           rhs=QT[j * dk:(j + 1) * dk, qh, q0:q0 + nq], start=True, stop=True)
                        return ins
                    S.op("pe", ["KTp" if seq < 4 else "KTs", ("QT", seq * 2), ("QT", seq * 2 + 1)], [pn], mm)
                    S.op("act", [pn], [("PT", pslot)], lambda e, c=c, pi=pi, nper=nper: e.activation(
                        out=ptv[:, c:c + nper, :], in_=pS[pi][:, 0:nper * nq].rearrange("p (t q) -> p t q", t=nper),
                        func=AF.Exp, scale=sc))

            def pv(u, ui, oslots):
                seq, qh, j, qts = u
                kvh = kv_of_head(qh)
                kts = keys_of(seq)
                nq = 128 * len(qts)
                pslot = ui % 2
                ptv = PT[:, pslot, 0:len(kts) * nq].rearrange("p (k q) -> p k q", q=nq)
                for qi_, qt in enumerate(qts):
                    oi, on = oslots[qi_]

                    def mm(e, qi_=qi_, oi=oi):
                        for i, (kind, kt) in enumerate(kts):
                            vsrc = V1p[:, kt, kvh, 0:129] if kind == "p" else V1s[:, kt, kvh, 0:129]
                            ins = e.matmul(pO[oi][:, j, 0:129], lhsT=ptv[:, i, qi_ * 128:(qi_ + 1) * 128], rhs=vsrc,
                                           start=(i == 0), stop=(i == len(kts) - 1))
                        return ins
                    S.op("pe", [("PT", pslot), "V1p" if seq < 4 else "V1s"], [on], mm)

            def combine(seq, qh, qts, oslots):
                chains = []
                for qi_, qt in enumerate(qts):
                    chains.append(combine_ops(seq, qh, qt, oslots[qi_]))
                n = max(len(c) for c in chains)
                for k in range(n):
                    for c in chains:
                        if k < len(c):
                            eng, rd, wr, fn = c[k]
                            if eng == "defer":
                                defer(fn)
                            else:
                                S.op(eng, rd, wr, fn)

            def combine_ops(seq, qh, qt, oslot):
                ops = []
                oi, on = oslot
                tt = seq * 2 + qt
                si, sn = statr.next()
                yi, yn = ytr4.next()
                ytv = ytk[:, yi // 2, (yi % 2) * 128:(yi % 2 + 1) * 128]
                if L == 0:
                    ti, tn = otr.next()
                    fi, fn_ = ofr.next()
                    ops.append(("dve", [on], [sn], lambda e: e.reciprocal(out=stat[:, si, 0:2], in_=pO[oi][:, :, 128])))
                    ops.append(("dve", [sn, "nlam"], [sn], lambda e: e.tensor_tensor(
                        out=stat[:, si, 2:3], in0=stat[:, si, 1:2], in1=nlam[:, 0:1], op=ALU.mult)))
                    ops.append(("dve", [on, sn], [tn], lambda e: e.tensor_scalar(
                        out=otm[:, ti, :], in0=pO[oi][:, 1, 0:128], scalar1=stat[:, si, 2:3], scalar2=None, op0=ALU.mult)))
                    ops.append(("dve", [on, sn, tn], [fn_], lambda e: e.scalar_tensor_tensor(
                        out=ofm[:, fi, :], in0=pO[oi][:, 0, 0:128], scalar=stat[:, si, 0:1], in1=otm[:, ti, :],
                        op0=ALU.mult, op1=ALU.add)))
                    ops.append(("act", [fn_], [tn, sn], lambda e: e.activation(
                        out=otm[:, ti, :], in_=ofm[:, fi, :], func=AF.Square, accum_out=stat[:, si, 4:5])))
                    ops.append(("act", [sn], [sn], lambda e: e.activation(
                        out=stat[:, si, 5:6], in_=stat[:, si, 4:5], func=AF.Ln, scale=1.0 / 128, bias=EPS)))
                    ops.append(("act", [sn], [sn], lambda e: e.activation(
                        out=stat[:, si, 6:7], in_=stat[:, si, 5:6], func=AF.Exp, scale=-0.5)))
                    ops.append(("dve", [fn_, sn, ("SG", tt)], [yn], lambda e: e.scalar_tensor_tensor(
                        out=ytv, in0=ofm[:, fi, :], scalar=stat[:, si, 6:7], in1=SG[:, tt, qh * 128:(qh + 1) * 128],
                        op0=ALU.mult, op1=ALU.mult)))
                else:
                    ops.append(("dve", [on], [sn], lambda e: e.reciprocal(out=stat[:, si, 0:1], in_=pO[oi][:, 0, 128:129])))
                    ops.append(("dve", [on, sn, ("SG", tt)], [yn], lambda e: e.scalar_tensor_tensor(
                        out=ytv, in0=pO[oi][:, 0, 0:128], scalar=stat[:, si, 0:1],
                        in1=SG[:, tt, qh * 128:(qh + 1) * 128], op0=ALU.mult, op1=ALU.mult)))

                def ytrans():
                    pi, pn = pPr.next()
                    pvw = pP[pi][:, :].bitcast(BF16)[:, 0:128]
                    S.op("pe", [yn, "identb"], [pn], lambda e: e.transpose(pvw, ytv, identb[:]))
                    S.op("act", [pn], [("yT", kc0 + qh)], lambda e: e.activation(
                        out=yT[:, kc0 + qh, tt * 128:(tt + 1) * 128], in_=pvw, func=AF.Copy))
                ops.append(("defer", None, None, ytrans))
                return ops

            scores(units[0], 0)
            oslots = None
            for ui, u in enumerate(units):
                if ui + 1 < len(units):
                    scores(units[ui + 1], ui + 1)
                seq, qh, j, qts = u
                if j == 0:
                    oslots = [pOr.next() for _ in qts]
                pv(u, ui, oslots)
                if hooks and ui in hooks:
                    hooks[ui]()
                flush_defer(keep=2 if L == 0 else 0)
                if j == nj - 1:
                    combine(seq, qh, qts, oslots)

        def pool_block(ub):
            flush_defer()
            its = [(seq, gl) for seq in range(5) for gl in range(2)]
            state = {}

            def stage_a(i):
                seq, gl = its[i]
                g = ub * 2 + gl
                stiles = [(seq * 2 + t, 128) for t in range(2)] if seq < 4 else [(8, 128), (9, 128), (10, 16)]
                qi, qn_ = ppr.next()
                for cb in range(2):
                    pi, pn = pSr.next()

                    def mm(e):
                        for k, (tt, npt) in enumerate(stiles):
                            bsrc = Bpt[0:npt, g, k, :] if seq < 4 else Bst[0:npt, g, k, :]
                            ins = e.matmul(pS[pi][:, 0:256], lhsT=u_tok[0:npt, tt, gl * 256 + cb * 128: gl * 256 + (cb + 1) * 128],
                                           rhs=bsrc, start=(k == 0), stop=(k == len(stiles) - 1))
                        return ins
                    S.op("pe", [("u_tok", tt) for tt, _ in stiles] + ["Bt"], [pn], mm)
                    S.op("act", [pn], [qn_], lambda e: e.activation(
                        out=pooledT[:, qi, cb, :], in_=pS[pi][:, 0:256], func=AF.Copy))
                state[i] = {"q": (qi, qn_), "y": []}

            def stage_b(i):
                seq, gl = its[i]
                g = ub * 2 + gl
                qi, qn_ = state[i]["q"]
                for qt in range(2):
                    tt = seq * 2 + qt
                    pi, pn = pPr.next()

                    def mm2(e):
                        for cb in range(2):
                            ins = e.matmul(pP[pi][:, 0:256], lhsT=pooledT[:, qi, cb, qt * 128:(qt + 1) * 128],
                                           rhs=pwt[:, g, cb, :], start=(cb == 0), stop=(cb == 1))
                        return ins
                    S.op("pe", [qn_, "pwt"], [pn], mm2)
                    ti, _ = ptr_.next()
                    tns = [("otm", 2 * ti), ("otm", 2 * ti + 1)]
                    ptv = otm[:, 2 * ti:2 * ti + 2, :].rearrange("p a d -> p (a d)")
                    S.op("dve", [pn, "pscbc"], tns, lambda e: e.tensor_tensor(
                        out=ptv, in0=pP[pi][:, 0:256], in1=pscbc[:, g * 256:(g + 1) * 256], op=ALU.mult))
                    yi, _ = ytr.next()
                    yns = [("ytk", 2 * yi), ("ytk", 2 * yi + 1)]
                    S.op("dve", tns + [("SG", tt)], yns, lambda e: e.tensor_tensor(
                        out=ytk[:, yi, :], in0=ptv, in1=SG[:, tt, gl * 256:(gl + 1) * 256], op=ALU.mult))
                    state[i]["y"].append((yi, yns, tt))

            def stage_c(i):
                seq, gl = its[i]
                g = ub * 2 + gl
                kcb = 8 + g * 2
                for yi, yns, tt in state[i]["y"]:
                    p2, pn2 = pPr.next()
                    pvw = pP[p2][:, :].bitcast(BF16)[:, 0:256]

                    def tr(e):
                        for b_ in range(2):
                            ins = e.transpose(pvw[:, b_ * 128:(b_ + 1) * 128], ytk[:, yi, b_ * 128:(b_ + 1) * 128], identb[:])
                        return ins
                    S.op("pe", yns + ["identb"], [pn2], tr)
                    S.op("act", [pn2], [("yT", kcb), ("yT", kcb + 1)], lambda e: e.activation(
                        out=yT[:, kcb:kcb + 2, tt * 128:(tt + 1) * 128], in_=pvw.rearrange("p (b t) -> p b t", b=2), func=AF.Copy))

            n = len(its)
            for i in range(n + 2):
                if i < n:
                    stage_a(i)
                if 0 <= i - 2 < n:
                    stage_c(i - 2)
                if 0 <= i - 1 < n:
                    stage_b(i - 1)

        def w_out_phase(L, w_out, inter=None, next_w=None):
            flush_defer()
            S.alias(R1wo, R5att + R6att + R6pool + [("u_tok", t) for t in range(11)])
            if inter is not None:
                S.alias(ALTmod, QSP_names + R5att + R6att + R6pool + [("u_tok", t) for t in range(11)])
                ixload, istats, ipe = make_modulate(L + 1, True)
            for j in range(2):
                S.dma("sp", ("gbc", j), [("mlin", L)], [("gbc", j)], lambda e, j=j: e.dma_start(
                    out=gbc[j][:, :], in_=mlin[L, j, 4096:6144].partition_broadcast(128)))
            slot = wload(w_out[:, 0:512], 512)

            def tile_io(tt):
                if L == 0:
                    src = xp[tt * 128:(tt + 1) * 128] if tt < 8 else xs[(tt - 8) * 128:(tt - 7) * 128]
                    return src, [], x1[tt * 128:(tt + 1) * 128], lambda cb: [("x1", tt)]
                dst = yp[tt * 128:(tt + 1) * 128] if tt < 8 else ys[(tt - 8) * 128:(tt - 7) * 128]
                return x1[tt * 128:(tt + 1) * 128], [("x1", tt)], dst, lambda cb: [("out_y", tt, cb)]

            xr = Ring("xblk", 4)
            xslots = {}
            pre = {}

            def xb_load(cb, tt):
                src, rd, _, _ = tile_io(tt)
                xi, xn = xr.next()
                xslots[(cb, tt)] = (xi, xn)
                S.dma("act", xn, rd, [xn], lambda e: e.dma_start(out=xblk[xi][:, :], in_=src[:, cb * 512:(cb + 1) * 512]))

            seq = [(cb, tt) for cb in range(4) for tt in range(10)]
            for k in range(3):
                xb_load(*seq[k])
            orr = Ring("oblk", 2)
            for k, (cb, tt) in enumerate(seq):
                if tt == 0:
                    if cb < 3:
                        nslot = wload(w_out[:, (cb + 1) * 512:(cb + 2) * 512], 512)
                    else:
                        nslot = None
                        if next_w is not None:
                            pre["slot"] = wload(next_w, 512)
                if k + 3 < len(seq):
                    xb_load(*seq[k + 3])
                jsel = 0 if tt < 8 else 1
                _, _, dst, wrf = tile_io(tt)
                xi, xn = xslots[(cb, tt)]
                pi, pn = pPr.next()

                def mm(e):
                    for kc in range(16):
                        ins = e.matmul(pP[pi][:, :], lhsT=yT[:, kc, tt * 128:(tt + 1) * 128], rhs=WB[:, slot, kc, :],
                                       start=(kc == 0), stop=(kc == 15))
                    return ins
                S.op("pe", yT_names + [("wb", slot)], [pn], mm)
                oi, on = orr.next()
                S.op("dve", [pn, ("gbc", jsel)], [on], lambda e: e.tensor_tensor(
                    out=oblk[oi][:, :], in0=pP[pi][:, :], in1=gbc[jsel][:, cb * 512:(cb + 1) * 512], op=ALU.mult))
                S.op("dve", [on, xn], [on], lambda e: e.tensor_tensor(
                    out=oblk[oi][:, :], in0=oblk[oi][:, :], in1=xblk[xi][:, :], op=ALU.add))
                S.dma("sp", on, [on], wrf(cb), lambda e: e.dma_start(
                    out=dst[:, cb * 512:(cb + 1) * 512], in_=oblk[oi][:, :]))
                if inter is not None and cb == 3:
                    if tt >= 4:
                        ipe(tt - 4)
                    if tt >= 2:
                        istats(tt - 2)
                    ixload(tt)
                if tt == 9:
                    slot = nslot
            if inter is not None:
                ipe(6)
                istats(8)
                ipe(7)
                istats(9)
                ipe(8)
                ipe(9)
            return pre.get("slot")

        ALL10 = list(range(10))

        class Stop(Exception):
            pass

        def gate(n):
            if stage < n:
                raise Stop()

        def main_flow():
            S.alias([("qkf", 0), ("qkf", 1)], ["qnw0", "lamp", "onesr"])
            gate(1)
            load_layer_consts(0)
            st3 = {}

            def hk0():
                ada_block(0, 0, ada0_slots[0])
                st3["s"] = wload(adaw[0, :, 1024:1536], 512)
            hooks0 = {5: hk0, 7: (lambda: ada_block(0, 1, ada0_slots[1]))}
            modulate(0, list(range(11)), hooks0)
            ada_block(0, 2, st3["s"])
            ada_finish(0)
            nxt = wload(ew_in[:, 1024:1536], 512)
            modulate_affine(0, True)
            S.alias(yT_names, R2mod + R2ada)
            set_ones()
            gate(2)
            for hg in range(2):
                sK = nxt
                kv_prefetch(0, hg)
                sV = wload(ew_in[:, 2048 + hg * 512:2048 + (hg + 1) * 512], 512)
                proj(sK, 512, ALL10, evac_qk(0, 'k', hg))
                if hg == 1:
                    fin_b()
                if int(os.environ.get("MK_SUBK", "99")) < 99:
                    raise Stop()
                sQ = wload(ew_in[:, hg * 512:(hg + 1) * 512], 512)
                proj(sV, 512, ALL10, evac_v(0, hg))
                gate(3)
                kv_exchange(0, hg, hg)
                gate(4)
                sG = wload(ew_in[:, 4096 + hg * 512:4096 + (hg + 1) * 512], 512)
                proj(sQ, 512, ALL10, evac_qk(0, 'q', hg))
                nxt = wload(ew_in[:, 1024 + 512:1024 + 1024], 512) if hg == 0 else wload(ew_in[:, 3072:3584], 512)
                proj(sG, 512, ALL10, evac_g_sub)
                gate(5)
                if hg == 0:
                    hk = {ui: (lambda nb=nb: ada_block(1, nb, wload(adaw[1, :, nb * 512:(nb + 1) * 512], 512, slot=sG)))
                          for nb, ui in enumerate((10, 28, 46))}
                    attention(0, hg * 4, lambda qh: qh, hk)
                    fin_b = ada_finish(1, split=True)
                else:
                    attention(0, hg * 4, lambda qh: qh)
                gate(6)
            flush_defer()
            S.alias(["Bt", "pwt", "pscbc"], R6att)
            S.alias([("u_tok", t) for t in range(11)], R5att)
            S.dma("pool", "c_B", [], ["Bt", "pwt", "pscbc"], lambda e: [
                e.dma_start(out=Bst[0:16, :, 2, :], in_=Bs[:, 256:272, :].rearrange("g p t -> p g t")),
                e.dma_start(out=pscbc[:, :], in_=pscale.partition_broadcast(128))] + [
                e.dma_start(out=Bpt[:, g, :, :], in_=Bp[g].rearrange("(s p) t -> p s t", p=128)) for g in range(4)] + [
                e.dma_start(out=Bst[:, g, 0:2, :], in_=Bs[g, 0:256, :].rearrange("(s p) t -> p s t", p=128)) for g in range(4)] + [
                e.dma_start(out=pwt[:, g, :, :], in_=poolw[g].rearrange("(c p) d -> p c d", p=128)) for g in range(4)])
            for ub in range(2):
                sU = nxt
                sGp = wload(ew_in[:, 5120 + ub * 512:5120 + (ub + 1) * 512], 512)
                proj(sU, 512, list(range(11)), evac_u)
                if ub == 0:
                    nxt = wload(ew_in[:, 3584:4096], 512)
                proj(sGp, 512, ALL10, evac_g)
                pool_block(ub)
            gate(7)
            load_layer_consts(1)
            sK1 = w_out_phase(0, ew_out, inter=True, next_w=gw_in[:, 2048:2560])
            gate(8)

            modulate_affine(1, False)
            S.alias(QSP_names, ALTmod)
            S.alias(R6att, R1wo + ALTmod)
            S.alias(R5att, R1wo + ALTmod)
            set_ones()
            sK = sK1
            kv_prefetch(1, 0)
            sV = wload(gw_in[:, 2560:3072], 512)
            proj(sK, 512, ALL10, evac_qk(1, 'k', 0))
            sQ = wload(gw_in[:, 0:512], 512)
            proj(sV, 512, ALL10, evac_v(1, 0))
            gate(9)
            kv_exchange(1, 0, 2)
            for kvh in range(4):
                sG = wload(gw_in[:, 3072 + kvh * 512:3072 + (kvh + 1) * 512], 512)
                proj(sQ, 512, ALL10, evac_qk(1, 'q', kvh))
                if kvh < 3:
                    sQ = wload(gw_in[:, (kvh + 1) * 512:(kvh + 2) * 512], 512)
                proj(sG, 512, ALL10, evac_g)
                attention(1, kvh * 4, lambda qh, kvh=kvh: kvh)
            gate(10)
            w_out_phase(1, gw_out)

        try:
            main_flow()
        except Stop:
            pass
        flush_defer()

        S.wait_all("sp")
        build_program.stats = (S.n_ops, dict(S.count))
    return nc


def _rope_tables(pos0, dim):
    pos = np.arange(pos0, pos0 + 256)
    row = (pos // GRID_W).astype(np.float32)
    col = (pos % GRID_W).astype(np.float32)
    quarter = dim // 4
    freqs = (ROPE_THETA ** (-np.arange(quarter, dtype=np.float32) / quarter)).astype(np.float32)
    ar = row[:, None] * freqs
    ac = col[:, None] * freqs
    cos = np.concatenate([np.cos(ar), np.cos(ar), np.cos(ac), np.cos(ac)], -1)
    sinm = np.concatenate([-np.sin(ar), np.sin(ar), -np.sin(ac), np.sin(ac)], -1)
    return np.stack([cos, sinm], 1).astype(np.float32)


def _pool_mats(L, t0, t1, src_idx):
    out = np.zeros((4, len(src_idx), t1 - t0), np.float32)
    for g, w in enumerate(POOL_WINDOWS):
        half = w // 2
        for tl, t in enumerate(range(t0, t1)):
            lo, hi = max(t - half, 0), min(t + half, L)
            cnt = float(hi - lo)
            for sl, s in enumerate(src_idx):
                if s < 0:
                    continue
                v = 0.0
                if lo <= s < hi:
                    v += 1.0 / cnt
                if s == t:
                    v -= 1.0
                out[g, sl, tl] = v
    return out


_CACHE = {}


def _make_in_maps(x_prompt, x_sample, cache_a_k, cache_a_v, cache_c_k, cache_c_v, c, c_ctx,
           norm_w, ada_w, ada_b,
           even_w_in, even_q_norm_w, even_k_norm_w, even_lam_q1, even_lam_k1,
           even_lam_q2, even_lam_k2, even_subln_w, even_pool_w, even_pool_scale,
           even_w_out, gqa_w_in, gqa_q_norm_w, gqa_k_norm_w, gqa_w_out):
    f = lambda a: np.ascontiguousarray(np.asarray(a, dtype=np.float32))
    x_prompt, x_sample = f(x_prompt), f(x_sample)
    ada_w, ada_b = f(ada_w), f(ada_b)
    ident = np.eye(128, dtype=np.float32)
    Bp = _pool_mats(256, 0, 256, list(range(256)))
    lamv = np.stack([f(even_lam_q1)[0], f(even_lam_k1)[0], f(even_lam_q2)[0], f(even_lam_k2)[0]], 0)
    shared = {
        "norm_w": f(norm_w), "ew_in": f(even_w_in)[0], "ew_out": f(even_w_out)[0],
        "gw_in": f(gqa_w_in)[0], "gw_out": f(gqa_w_out)[0],
        "eqn": f(even_q_norm_w)[0], "ekn": f(even_k_norm_w)[0], "lamv": f(lamv),
        "subw": f(even_subln_w)[0], "poolw": f(even_pool_w)[0], "pscale": f(even_pool_scale)[0],
        "gqn": f(gqa_q_norm_w)[0], "gkn": f(gqa_k_norm_w)[0], "ident": ident, "Bp": Bp,
        "onesd": np.ones((128, 5200), np.float32),
    }
    in_maps = []
    for core in range(8):
        b, r = core // 4, core % 4
        t0 = r * 256
        halo_idx = list(range(t0 - 8, t0)) + list(range(t0 + 256, t0 + 264))
        xh = np.zeros((16, D), np.float32)
        src_idx = list(range(t0, t0 + 256))
        for i, s in enumerate(halo_idx):
            if 0 <= s < 1024:
                xh[i] = x_sample[b, s]
                src_idx.append(s)
            else:
                src_idx.append(-1)
        m = dict(shared)
        m.update({
            "xp": x_prompt[core * 4:(core + 1) * 4].reshape(1024, D),
            "xs": np.ascontiguousarray(x_sample[b, t0:t0 + 256]),
            "xh": xh,
            "cond": np.stack([f(c_ctx), f(c)[b]], 0),
            "cak": f(cache_a_k)[b, 0], "cav": f(cache_a_v)[b, 0],
            "cck": f(cache_c_k)[b, 0], "ccv": f(cache_c_v)[b, 0],
            "adaw": np.ascontiguousarray(ada_w[:, :, r * 1536:(r + 1) * 1536]),
            "adab": np.ascontiguousarray(ada_b[:, r * 1536:(r + 1) * 1536]),
            "ropeA": _rope_tables(t0, 64), "ropeC": _rope_tables(t0, 128),
            "Bs": _pool_mats(1024, t0, t0 + 256, src_idx),
        })
        in_maps.append(m)
    return in_maps


def _assemble(R):
    y_p = np.concatenate([R[i]["yp"].reshape(4, 256, D) for i in range(8)], 0)
    y_s = np.stack([np.concatenate([R[b * 4 + r]["ys"] for r in range(4)], 0) for b in range(2)], 0)
    nak = np.concatenate([R[i]["nak"] for i in range(8)], 0)[:, None]
    nav = np.concatenate([R[i]["nav"] for i in range(8)], 0)[:, None]
    nck = np.concatenate([R[i]["nck"] for i in range(8)], 0)[:, None]
    ncv = np.concatenate([R[i]["ncv"] for i in range(8)], 0)[:, None]
    return (y_p.astype(np.float32), y_s.astype(np.float32), nak.astype(np.float32),
            nav.astype(np.float32), nck.astype(np.float32), ncv.astype(np.float32))


def kernel(**inputs):
    stage = _CACHE.get("stage", 99)
    if ("nc", stage) not in _CACHE:
        _CACHE[("nc", stage)] = build_program(stage)
    nc = _CACHE[("nc", stage)]
    in_maps = _make_in_maps(**inputs)
    res = run_bass_kernel_spmd(nc, in_maps, core_ids=list(range(8)))
    return _assemble(res.results)
```

```python
import contextlib
import math
import os
import numpy as np
import ml_dtypes
import concourse.bass as bass
import concourse.mybir as mybir
from concourse.bass_utils import run_bass_kernel_spmd

F32 = mybir.dt.float32
BF16 = mybir.dt.bfloat16
AF = mybir.ActivationFunctionType
ALU = mybir.AluOpType
AX = mybir.AxisListType

D = 2048
KC = 16
EPS = 1e-6
NTOK = 1280
HALO0 = 1280
GRID_W = 64
ROPE_THETA = 10000.0
POOL_WINDOWS = (2, 4, 8, 16)


class Sched:
    def __init__(self, nc, stack):
        self.nc = nc
        self.stack = stack
        self.engs = {"pe": nc.tensor, "act": nc.scalar, "dve": nc.vector,
                     "pool": nc.gpsimd, "sp": nc.sync}
        self.sems = {}
        self.count = {}
        for e in self.engs:
            self.sems[e] = stack.enter_context(nc.semaphore("s_" + e))
            self.count[e] = 0
        self.waited = {e: {} for e in self.engs}
        self.last_write = {}
        self.reads = {}
        self.n_ops = 0

    def _deps(self, reads, writes):
        deps = {}

        def add(ev):
            if ev is None:
                return
            k, v = ev
            if deps.get(k, 0) < v:
                deps[k] = v
        for r in reads:
            add(self.last_write.get(r))
        for w in writes:
            add(self.last_write.get(w))
            for ev in self.reads.get(w, ()):
                add(ev)
        return deps

    def _wait(self, eng, deps, skip_self=False):
        e = self.engs[eng]
        for k, v in deps.items():
            if skip_self and k == eng:
                continue
            if self.waited[eng].get(k, 0) >= v:
                continue
            e.wait_ge(self.sems[k], v)
            self.waited[eng][k] = v

    def _record(self, ev, reads, writes):
        for r in reads:
            lst = self.reads.setdefault(r, [])
            lst.append(ev)
            if len(lst) > 64:
                mx = {}
                for k, v in lst:
                    if mx.get(k, 0) < v:
                        mx[k] = v
                self.reads[r] = list(mx.items())
        for w in writes:
            self.last_write[w] = ev
            self.reads[w] = []

    def op(self, eng, reads, writes, fn):
        deps = self._deps(reads, writes)
        self._wait(eng, deps, skip_self=(eng == "pe"))
        ins = fn(self.engs[eng])
        ins.then_inc(self.sems[eng], 1)
        self.count[eng] += 1
        ev = (eng, self.count[eng])
        self._record(ev, reads, writes)
        self.n_ops += 1
        return ev

    def dma(self, queue, semkey, reads, writes, fn, inc=16):
        if semkey not in self.sems:
            nm = "d_" + "".join(ch for ch in str(semkey) if ch.isalnum() or ch == "_")
            self.sems[semkey] = self.stack.enter_context(self.nc.semaphore(nm))
            self.count[semkey] = 0
        deps = self._deps(reads, writes)
        self._wait(queue, deps)
        inss = fn(self.engs[queue])
        if not isinstance(inss, (list, tuple)):
            inss = [inss]
        for ins in inss:
            ins.then_inc(self.sems[semkey], inc)
            self.count[semkey] += inc
        ev = (semkey, self.count[semkey])
        self._record(ev, reads, writes)
        return ev

    def alias(self, new_names, old_names):
        evs = []
        for o in old_names:
            if self.last_write.get(o) is not None:
                evs.append(self.last_write[o])
            evs.extend(self.reads.get(o, ()))
        mx = {}
        for k, v in evs:
            if mx.get(k, 0) < v:
                mx[k] = v
        for n in new_names:
            prev = []
            if self.last_write.get(n) is not None:
                prev.append(self.last_write[n])
            prev.extend(self.reads.get(n, ()))
            m2 = dict(mx)
            for k, v in prev:
                if m2.get(k, 0) < v:
                    m2[k] = v
            self.last_write[n] = None
            self.reads[n] = list(m2.items())

    def wait_all(self, eng):
        deps = {k: c for k, c in self.count.items() if c > 0}
        self._wait(eng, deps)


class Ring:
    def __init__(self, name, n):
        self.name, self.n, self.i = name, n, 0

    def next(self):
        s = self.i % self.n
        self.i += 1
        return s, (self.name, s)


def build_program(stage=99):
    nc = bass.Bass("TRN2", target_bir_lowering=False)

    def din(name, shape, dt=F32):
        return nc.dram_tensor(name, list(shape), dt, kind="ExternalInput").ap()

    def dout(name, shape, dt=F32):
        return nc.dram_tensor(name, list(shape), dt, kind="ExternalOutput").ap()

    def dint(name, shape, dt=F32):
        return nc.dram_tensor(name, list(shape), dt).ap()

    xp = din("xp", [1024, D]); xs = din("xs", [256, D]); xh = din("xh", [16, D])
    cond = din("cond", [2, D])
    cak = din("cak", [8, 256, 128]); cav = din("cav", [8, 256, 128])
    cck = din("cck", [4, 256, 128]); ccv = din("ccv", [4, 256, 128])
    norm_w = din("norm_w", [2, D])
    adaw = din("adaw", [2, D, 1536]); adab = din("adab", [2, 1536])
    ew_in = din("ew_in", [D, 6144]); ew_out = din("ew_out", [D, D])
    gw_in = din("gw_in", [D, 5120]); gw_out = din("gw_out", [D, D])
    eqn = din("eqn", [64]); ekn = din("ekn", [64]); lamv = din("lamv", [4, 64])
    subw = din("subw", [128]); poolw = din("poolw", [4, 256, 256]); pscale = din("pscale", [1024])
    gqn = din("gqn", [128]); gkn = din("gkn", [128])
    ident = din("ident", [128, 128])
    ropeA = din("ropeA", [256, 2, 64]); ropeC = din("ropeC", [256, 2, 128])
    Bp = din("Bp", [4, 256, 256]); Bs = din("Bs", [4, 272, 256])
    onesd = din("onesd", [128, 5200])

    yp = dout("yp", [1024, D]); ys = dout("ys", [256, D])
    nak = dout("nak", [4, 8, 256, 128]); nav = dout("nav", [4, 8, 256, 128])
    nck = dout("nck", [4, 4, 256, 128]); ncv = dout("ncv", [4, 4, 256, 128])

    x1 = dint("x1", [NTOK, D])
    agm_in = [dint("agm_in%d" % l, [2, 1536]) for l in range(2)]
    agm_out = [dint("agm_out%d" % l, [8, 1536]) for l in range(2)]
    mlin = dint("mlin", [2, 2, 6144])
    warm_in = dint("warm_in", [2, 64]); warm_out = dint("warm_out", [8, 64])
    agkv_in = [dint("agkv_in%d" % i, [1024, 256], BF16) for i in range(3)]
    agkv_out = [dint("agkv_out%d" % i, [4096, 256], BF16) for i in range(3)]

    with contextlib.ExitStack() as st:
        S = Sched(nc, st)

        def sbt(name, shape, dt):
            return st.enter_context(nc.sbuf_tensor(name, list(shape), dt))

        def pst(name, shape, dt=F32):
            return st.enter_context(nc.psum_tensor(name, list(shape), dt))

        R1 = sbt("R1", [128, 16 * 1296], BF16)
        R2 = sbt("R2", [128, 16 * 1280], BF16)
        WB = sbt("WB", [128, 2, 16, 512], BF16)
        R5 = sbt("R5", [128, 8256], BF16)
        R6 = sbt("R6", [128, 10320], BF16)
        QT = sbt("QT", [128, 4, 1280], BF16)
        SG = sbt("SG", [128, 10, 512], BF16)
        PT = sbt("PT", [128, 2, 1280], BF16)

        hT = R1[:, :].rearrange("p (k t) -> p k t", k=16)
        yT = R2[:, :].rearrange("p (k t) -> p k t", k=16)
        KTp = R5[:, 0:4096].rearrange("p (h t) -> p h t", h=4)
        V1p = R5[:, 4096:8256].rearrange("p (t h c) -> p t h c", t=8, h=4)
        KTs = R6[:, 0:5120].rearrange("p (h t) -> p h t", h=4)
        V1s = R6[:, 5120:10320].rearrange("p (t h c) -> p t h c", t=10, h=4)
        xts = [R2[:, s * 4096:(s + 1) * 4096].bitcast(F32) for s in range(3)]
        sqj = R2[:, 12288:12288 + 2048]
        xsb = [R2[:, 14336 + s * 2048:14336 + (s + 1) * 2048] for s in range(2)]
        condt = R2[:, 0:4096].bitcast(F32)
        sct = R2[:, 4096:8192].bitcast(F32)
        adabt = R2[:, 8192:8192 + 6144].bitcast(F32).rearrange("p (l n) -> p l n", l=2)
        mrow = R2[:, 14336:14336 + 6144].bitcast(F32).rearrange("p (l n) -> p l n", l=2)
        gbc = [R6[:, j * 4096:(j + 1) * 4096].bitcast(F32) for j in range(2)]
        xblk = [R5[:, s * 1024:(s + 1) * 1024].bitcast(F32) for s in range(4)]
        oblk = [R5[:, 4096 + s * 1024: 4096 + (s + 1) * 1024].bitcast(F32) for s in range(2)]
        u_tok = R5[:, 0:11 * 512].rearrange("p (t c) -> p t c", t=11)
        Bpt = R6[:, 0:2048].rearrange("p (g s t) -> p g s t", g=4, s=2)
        Bst = R6[:, 2048:2048 + 3072].rearrange("p (g s t) -> p g s t", g=4, s=3)
        pwt = R6[:, 5120:5120 + 2048].rearrange("p (g c d) -> p g c d", g=4, c=2)
        pscbc = R6[:, 7168:7168 + 2048].bitcast(F32)

        QTf = QT[:, :, :].rearrange("p h t -> p (h t)")
        SGf = SG[:, :, :].rearrange("p t c -> p (t c)")
        PTf = PT[:, :, :].rearrange("p s q -> p (s q)")
        xts_alt = [QTf[:, 0:4096].bitcast(F32), SGf[:, 0:4096].bitcast(F32)]
        sqj_alt = PTf[:, 0:2048]
        xsb_alt = [R5[:, 6144:8192], R6[:, 8192:10240]]
        identf = sbt("identf", [128, 128], F32)
        identb = sbt("identb", [128, 128], BF16)
        dg = sbt("dg", [128, 2, 128], F32)
        stat = sbt("stat", [128, 16, 16], F32)
        shsc = sbt("shsc", [128, 2, 2, 2, 16], F32)
        nwt = sbt("nwt", [128, 2, 16], F32)
        s1t = sbt("s1t", [128, 2, 2, 16], F32)
        scT = sbt("scT", [128, 16, 2], BF16)
        qnw = sbt("qnw", [128, 2, 128], F32)
        lams = sbt("lams", [128, 8], F32)
        nlam = sbt("nlam", [128, 1], F32)
        wsub = sbt("wsub", [128, 128], F32)
        ropet = sbt("ropet", [128, 2, 2, 128], F32)
        sqf = sbt("sqf", [128, 1, 512], F32)
        qkf = sbt("qkf", [128, 2, 512], F32)
        qkg = sbt("qkg", [128, 1, 512], F32)
        qkb = sbt("qkb", [128, 2, 512], BF16)
        kst = sbt("kst", [128, 1, 512], F32)
        vst = sbt("vst", [128, 1, 512], F32)
        lamt = qkf[:, 0, 0:256].rearrange("p (a d) -> p a d", a=4)
        lampt = qkf[:, 0, 256:384].rearrange("p (a d) -> p a d", a=2)
        onesr = qkf[:, 0, 384:512]
        ktst = sbt("ktst", [128, 4, 256], BF16)
        vsst = sbt("vsst", [128, 2, 512], BF16)
        cstg = sbt("cstg", [128, 2, 4, 128], BF16)
        otm = sbt("otm", [128, 4, 128], F32)
        ofm = sbt("ofm", [128, 2, 128], F32)
        ytk = sbt("ytk", [128, 2, 256], BF16)
        pooledT = sbt("pooledT", [128, 2, 2, 256], BF16)

        pP = [pst("pP%d" % i, [128, 512]) for i in range(2)]
        pS = [pst("pS%d" % i, [128, 512]) for i in range(2)]
        pO = [pst("pO%d" % i, [128, 2, 256]) for i in range(4)]

        statr = Ring("stat", 16)
        sqr, qkfr, qkgr, qkbr = Ring("sqf", 1), Ring("qkf", 2), Ring("qkg", 1), Ring("qkb", 2)
        kstr, vstr = Ring("kst", 1), Ring("vst", 1)
        otr, ofr, ytr, ytr4 = Ring("otm", 4), Ring("ofm", 2), Ring("ytkp", 2), Ring("ytk", 4)
        pPr, pSr, pOr = Ring("pP", 2), Ring("pS", 2), Ring("pO", 4)
        ptr_, ppr = Ring("ptmp", 2), Ring("pooledT", 2)

        R2mod = [("xt", 0), ("xt", 1), ("xt", 2), "sqj", ("xsb", 0), ("xsb", 1)]
        R2ada = ["condt", "sct"]
        ALTmod = [("xta", 0), ("xta", 1), "sqja", ("xsba", 0), ("xsba", 1)]
        QSP_names = [("QT", t) for t in range(10)] + [("SG", t) for t in range(10)] + [("PT", 0), ("PT", 1)]
        R1wo = [("gbc", 0), ("gbc", 1), ("xblk", 0), ("xblk", 1), ("xblk", 2), ("xblk", 3), ("oblk", 0), ("oblk", 1)]
        hT_names = [("hT", t, k) for t in range(11) for k in range(16)]
        yT_names = [("yT", k) for k in range(16)]
        R5att = ["KTp", "V1p"]
        R6att = ["KTs", "V1s"]
        R6pool = ["Bt", "pwt", "pscbc"]

        S.dma("sp", "c0", [], ["identf", "qnw0", "condt"], lambda e: [
            e.dma_start(out=identf[:], in_=ident),
            e.dma_start(out=condt[0:2, :], in_=cond),
            e.dma_start(out=wsub[:], in_=subw.partition_broadcast(128)),
            e.dma_start(out=lamt[0:1, :, :], in_=lamv.rearrange("(o a) d -> o a d", o=1)),
        ])

        def load_layer_consts(L):
            if L == 0:
                S.dma("sp", "c2", [], ["qnw"], lambda e: [
                    e.dma_start(out=qnw[:, 0, 0:64], in_=eqn.partition_broadcast(128)),
                    e.dma_start(out=qnw[:, 1, 0:64], in_=ekn.partition_broadcast(128)),
                ] + [e.dma_start(out=ropet[:, t, :, 0:64], in_=ropeA[t * 128:(t + 1) * 128]) for t in range(2)])
            else:
                S.dma("sp", "c2", [], ["qnw"], lambda e: [
                    e.dma_start(out=qnw[:, 0, :], in_=gqn.partition_broadcast(128)),
                    e.dma_start(out=qnw[:, 1, :], in_=gkn.partition_broadcast(128)),
                ] + [e.dma_start(out=ropet[:, t, :, :], in_=ropeC[t * 128:(t + 1) * 128]) for t in range(2)])
        S.op("dve", ["identf"], ["identb"], lambda e: e.tensor_copy(out=identb[:], in_=identf[:]))
        LAM_INIT = 0.8 - 0.6 * math.exp(0.0)
        S.op("dve", ["qnw0"], ["wsub"], lambda e: e.tensor_scalar(
            out=wsub[:], in0=wsub[:], scalar1=1.0 - LAM_INIT, scalar2=None, op0=ALU.mult))
        S.op("dve", ["qnw0"], ["lamp"], lambda e: e.tensor_tensor(
            out=lampt[0:1, :, :], in0=lamt[0:1, 0:4:2, :], in1=lamt[0:1, 1:4:2, :], op=ALU.mult))
        S.op("dve", ["lamp"], ["lams0"], lambda e: e.tensor_reduce(
            out=lams[0:1, 0:2], in_=lampt[0:1, :, :], axis=AX.X, op=ALU.add))
        S.op("act", ["lams0"], ["lams1"], lambda e: e.activation(
            out=lams[0:1, 2:4], in_=lams[0:1, 0:2], func=AF.Exp))
        S.op("dve", ["lams1"], ["lams2"], lambda e: e.tensor_tensor(
            out=lams[0:1, 4:5], in0=lams[0:1, 3:4], in1=lams[0:1, 2:3], op=ALU.subtract))
        S.op("dve", ["lams2"], ["lams3"], lambda e: e.tensor_scalar(
            out=lams[0:1, 5:6], in0=lams[0:1, 4:5], scalar1=-LAM_INIT, scalar2=None, op0=ALU.add))
        S.op("dve", [], ["onesr"], lambda e: e.memset(onesr[0:1, :], 1.0))
        pi0, pn0 = pPr.next()
        S.op("pe", ["onesr", "lams3"], [pn0], lambda e: e.matmul(
            pP[pi0][:, 0:1], lhsT=onesr[0:1, :], rhs=lams[0:1, 5:6], start=True, stop=True))
        S.op("dve", [pn0], ["nlam"], lambda e: e.tensor_copy(out=nlam[:], in_=pP[pi0][:, 0:1]))

        S.op("act", ["condt"], ["sct"], lambda e: e.activation(out=sct[0:2, :], in_=condt[0:2, :], func=AF.Silu))

        ps0, psn0 = pSr.next()

        def mm_sct(e):
            for kc in range(16):
                ins = e.matmul(pS[ps0][:, 2 * kc:2 * kc + 2], lhsT=sct[0:2, kc * 128:(kc + 1) * 128],
                               rhs=identf[0:2, 0:2], start=True, stop=True)
            return ins
        S.op("pe", ["sct", "identf"], [psn0], mm_sct)
        S.op("dve", [psn0], ["scT"], lambda e: e.tensor_copy(
            out=scT[:].rearrange("p k j -> p (k j)"), in_=pS[ps0][:, 0:32]))

        wstate = {"n": 0}

        def wload(src, ncols, slot=None):
            if slot is None:
                slot = wstate["n"] % 2
                wstate["n"] += 1
            S.dma("pool", ("wb", slot), [], [("wb", slot)], lambda e: e.dma_start(
                out=WB[:, slot, :, 0:ncols], in_=src.rearrange("(kc p) n -> p kc n", p=128)))
            return slot

        GROUPS = [[0, 1, 2, 3], [4, 5, 6, 7]]
        s1_names = [("s1t", l, j) for l in range(2) for j in range(2)]

        def ada_block(L, nb, slot):
            S.dma("sp", "c_bias", [], [("vst", 0)], lambda e: [
                e.dma_start(out=vst[p:p + 1, 0, :], in_=adab[L:L + 1, nb * 512:(nb + 1) * 512]) for p in range(2)])
            pi, pn = pPr.next()

            def mm_ada(e):
                for kc in range(16):
                    ins = e.matmul(pP[pi][0:2, :], lhsT=scT[:, kc, :], rhs=WB[:, slot, kc, :],
                                   start=(kc == 0), stop=(kc == 15))
                return ins
            S.op("pe", ["scT", ("wb", slot)], [pn], mm_ada)
            S.op("dve", [pn, ("vst", 0)], [("kst", 0)], lambda e: e.tensor_tensor(
                out=kst[0:2, 0, :], in0=pP[pi][0:2, :], in1=vst[0:2, 0, :], op=ALU.add))
            S.dma("sp", "c_st", [("kst", 0)], [("agm_in", L)], lambda e: e.dma_start(
                out=agm_in[L][:, nb * 512:(nb + 1) * 512], in_=kst[0:2, 0, :]))

        def ada_finish(L, split=False):
            S.dma("pool", "cc", [("agm_in", L)], [("agm_out", L)], lambda e: e.collective_compute(
                "AllGather", ALU.bypass, replica_groups=GROUPS, ins=[agm_in[L]], outs=[agm_out[L]]), inc=1)
            S.dma("sp", "c1", [("agm_out", L)], [("mlin", L)], lambda e: e.dma_start(
                out=mlin[L].rearrange("j (r i) -> r j i", r=4),
                in_=agm_out[L].rearrange("(r j) i -> r j i", r=4)))
            if split:
                mt_a = otm[:, 0:2, :]
                mt_b = otm[:, 2, :]
                stg = [("otm", 0), ("otm", 1), ("otm", 2)]
            else:
                mt_a = kst[:, 0, 0:256].rearrange("p (i c) -> p i c", i=2)
                mt_b = vst[:, 0, 0:128]
                stg = [("kst", 0), ("vst", 0)]
            S.dma("sp", "c1", [("mlin", L)], stg, lambda e: [
                e.dma_start(out=mt_a[0:32, j, :], in_=mlin[L, j, 0:4096].rearrange("(a p) -> a p", p=128))
                for j in range(2)] + [
                e.dma_start(out=mt_b[0:16, :], in_=norm_w[L].rearrange("(a p) -> a p", p=128))])
            def part_b():
                ps1, psn1 = pSr.next()

                def mm_mt(e):
                    for j in range(2):
                        ins = e.matmul(pS[ps1][:, j * 32:(j + 1) * 32], lhsT=mt_a[0:32, j, :], rhs=identf[0:32, 0:32], start=True, stop=True)
                    ins = e.matmul(pS[ps1][:, 64:80], lhsT=mt_b[0:16, :], rhs=identf[0:16, 0:16], start=True, stop=True)
                    return ins
                S.op("pe", stg + ["identf"], [psn1], mm_mt)
                S.op("dve", [psn1], ["shsc"], lambda e: e.tensor_copy(
                    out=shsc[:, L, :, :, :].rearrange("p j t k -> p (j t k)"), in_=pS[ps1][:, 0:64]))
                S.op("dve", [psn1], ["shsc"], lambda e: e.tensor_copy(out=nwt[:, L, :], in_=pS[ps1][:, 64:80]))
                for j in range(2):
                    S.op("dve", ["shsc"], [("s1t", L, j)], lambda e: e.scalar_tensor_tensor(
                        out=s1t[:, L, j, :], in0=shsc[:, L, j, 1, :], scalar=1.0, in1=nwt[:, L, :],
                        op0=ALU.add, op1=ALU.mult))

            if split:
                return part_b
            part_b()

        ada0_slots = [wload(adaw[0, :, nb * 512:(nb + 1) * 512], 512) for nb in range(2)]
        S.dma("sp", "c_warm", ["identf"], ["warm_in"], lambda e: e.dma_start(out=warm_in, in_=identf[0:2, 0:64]))
        S.dma("pool", "cc", ["warm_in"], ["warm_out"], lambda e: e.collective_compute(
            "AllGather", ALU.bypass, replica_groups=GROUPS, ins=[warm_in], outs=[warm_out]), inc=1)

        def tile_tok0(tt):
            return HALO0 if tt == 10 else tt * 128

        DEFER = []

        def defer(fn):
            DEFER.append(fn)

        def flush_defer(keep=0):
            while len(DEFER) > keep:
                DEFER.pop(0)()

        def make_modulate(L, alt):
            if alt:
                xt_t, sq_t, xb_t, nslot = xts_alt, sqj_alt, xsb_alt, 2
                nxt_, nsq, nxb, q = "xta", "sqja", "xsba", "pool"
            else:
                xt_t, sq_t, xb_t, nslot = xts, sqj, xsb, 3
                nxt_, nsq, nxb, q = "xt", "sqj", "xsb", "sp"

            def xload(tt):
                npt = 16 if tt == 10 else 128
                slot = tt % nslot
                if L == 0:
                    src = xp[tt * 128:(tt + 1) * 128] if tt < 8 else (xs[(tt - 8) * 128:(tt - 7) * 128] if tt < 10 else xh)
                    rd = []
                else:
                    src = x1[tt * 128:(tt + 1) * 128]
                    rd = [("x1", tt)]
                S.dma(q, (nxt_, slot), rd, [(nxt_, slot)], lambda e: e.dma_start(out=xt_t[slot][0:npt, :], in_=src))

            def stats(tt):
                npt = 16 if tt == 10 else 128
                slot = tt % nslot
                bslot = tt % 2
                si, sn = statr.next()
                S.op("act", [(nxt_, slot)], [nsq, sn], lambda e: e.activation(
                    out=sq_t[0:npt, :], in_=xt_t[slot][0:npt, :], func=AF.Square, accum_out=stat[0:npt, si, 0:1]))
                S.op("act", [sn], [sn], lambda e: e.activation(
                    out=stat[0:npt, si, 1:2], in_=stat[0:npt, si, 0:1], func=AF.Ln, scale=1.0 / D, bias=EPS))
                S.op("act", [sn], [sn], lambda e: e.activation(
                    out=stat[0:npt, si, 2:3], in_=stat[0:npt, si, 1:2], func=AF.Exp, scale=-0.5))
                S.op("dve", [sn, (nxt_, slot)], [(nxb, bslot)], lambda e: e.tensor_scalar(
                    out=xb_t[bslot][0:npt, :], in0=xt_t[slot][0:npt, :], scalar1=stat[0:npt, si, 2:3],
                    scalar2=None, op0=ALU.mult))

            def pe_part(tt):
                npt = 16 if tt == 10 else 128
                slot = tt % 2
                t0 = tile_tok0(tt)
                for g in range(2):
                    pi, pn = pOr.next()
                    pv = pO[pi][:, :, :].rearrange("p a b -> p (a b)").bitcast(BF16)

                    def mm(e):
                        for j in range(8):
                            c = g * 8 + j
                            ins = e.transpose(pv[:, j * 128:j * 128 + npt], xb_t[slot][0:npt, c * 128:(c + 1) * 128],
                                              identb[0:npt, 0:npt])
                        return ins
                    S.op("pe", [(nxb, slot), "identb"], [pn], mm)
                    pv3 = pv.rearrange("p (k t) -> p k t", k=8)[:, :, 0:npt]
                    wr = [("hT", tt, g * 8 + j) for j in range(8)]
                    if g == 0:
                        S.op("dve", [pn], wr, lambda e: e.tensor_copy(out=hT[:, 0:8, t0:t0 + npt], in_=pv3))
                    else:
                        S.op("act", [pn], wr, lambda e: e.activation(out=hT[:, 8:16, t0:t0 + npt], in_=pv3, func=AF.Copy))
            return xload, stats, pe_part

        def modulate(L, tiles, hooks=None):
            flush_defer()
            S.alias(R2mod, yT_names + R2ada)
            xload, stats, pe_part = make_modulate(L, False)
            xload(tiles[0])
            xload(tiles[1])
            stats(tiles[0])
            for i, tt in enumerate(tiles):
                if i + 2 < len(tiles):
                    xload(tiles[i + 2])
                if i + 1 < len(tiles):
                    stats(tiles[i + 1])
                pe_part(tt)
                if hooks and tt in hooks:
                    hooks[tt]()

        def modulate_affine(L, with_halo):
            tiles_p = list(range(8))
            tiles_s = [8, 9] + ([10] if with_halo else [])
            for jsel, tl, a0, a1 in ((0, tiles_p, 0, 1024), (1, tiles_s, 1024, 1296 if with_halo else 1280)):
                for kc in range(16):
                    names = [("hT", tt, kc) for tt in tl]
                    if kc % 2 == 0:
                        S.op("dve", names + s1_names + ["shsc"], names, lambda e: e.tensor_scalar(
                            out=hT[:, kc, a0:a1], in0=hT[:, kc, a0:a1],
                            scalar1=s1t[:, L, jsel, kc:kc + 1], scalar2=shsc[:, L, jsel, 0, kc:kc + 1],
                            op0=ALU.mult, op1=ALU.add))
                    else:
                        S.op("act", names + s1_names + ["shsc"], names, lambda e: e.activation(
                            out=hT[:, kc, a0:a1], in_=hT[:, kc, a0:a1], func=AF.Identity,
                            scale=s1t[:, L, jsel, kc:kc + 1], bias=shsc[:, L, jsel, 0, kc:kc + 1]))

        def proj(slot, ncols, tiles, evac):
            for tt in tiles:
                npt = 16 if tt == 10 else 128
                t0 = tile_tok0(tt)
                pi, pn = pPr.next()

                def mm(e, pi=pi, npt=npt, t0=t0):
                    for kc in range(16):
                        ins = e.matmul(pP[pi][0:npt, 0:ncols], lhsT=hT[:, kc, t0:t0 + npt], rhs=WB[:, slot, kc, 0:ncols],
                                       start=(kc == 0), stop=(kc == 15))
                    return ins
                S.op("pe", [("hT", tt, k) for k in range(16)] + [("wb", slot)], [pn], mm)
                n0 = len(DEFER)
                evac(tt, pP[pi], pn)
                flush_defer(keep=len(DEFER) - n0)

        def transpose4(src_bf, src_name, dst_fn, dst_names, nblk=4):
            pi, pn = pSr.next()
            pv = pS[pi][:, :].bitcast(BF16)[:, 0:nblk * 128]

            def mm(e):
                for b in range(nblk):
                    ins = e.transpose(pv[:, b * 128:(b + 1) * 128], src_bf[:, b * 128:(b + 1) * 128], identb[:])
                return ins
            S.op("pe", [src_name, "identb"], [pn], mm)
            if int(os.environ.get("MK_SUBK", "99")) <= 5:
                return
            S.op("act", [pn], dst_names, lambda e: dst_fn(e, pv.rearrange("p (b t) -> p b t", b=nblk)))

        def evac_qk(L, kind, hg):
            dk = 64 if L == 0 else 128
            nch = 512 // dk
            wrow = qnw[:, 0 if kind == 'q' else 1, 0:dk]
            rope = ropet[:, :, :, 0:dk]
            q4 = dk // 4

            SUBK = int(os.environ.get("MK_SUBK", "99"))

            def f(tt, ps, pn):
                sample = tt >= 8
                if SUBK <= 1:
                    return
                si, sn = statr.next()
                qi, qn_ = sqr.next()
                S.op("act", [pn], [qn_], lambda e: e.activation(out=sqf[:, qi, :], in_=ps[:, :], func=AF.Square))
                S.op("dve", [qn_], [sn], lambda e: e.tensor_reduce(
                    out=stat[:, si, 0:nch], in_=sqf[:, qi, :].rearrange("p (c d) -> p c d", d=dk), axis=AX.X, op=ALU.add))
                S.op("act", [sn], [sn], lambda e: e.activation(
                    out=stat[:, si, 8:8 + nch], in_=stat[:, si, 0:nch], func=AF.Ln, scale=1.0 / dk, bias=EPS))
                S.op("act", [sn], [sn], lambda e: e.activation(
                    out=stat[:, si, 0:nch], in_=stat[:, si, 8:8 + nch], func=AF.Exp, scale=-0.5))
                if SUBK <= 2:
                    return
                fi, fn_ = qkfr.next()
                S.op("dve", [pn, sn], [fn_], lambda e: e.tensor_tensor(
                    out=qkf[:, fi, :].rearrange("p (c d) -> p c d", d=dk), in0=ps[:, :].rearrange("p (c d) -> p c d", d=dk),
                    in1=stat[:, si, 0:nch].unsqueeze(2).to_broadcast([128, nch, dk]), op=ALU.mult))
                if SUBK <= 3:
                    return
                bi, bn_ = qkbr.next()
                wbc = wrow.unsqueeze(1).to_broadcast([128, nch, dk])
                if not sample:
                    if kind == 'k':
                        ki, kn_ = kstr.next()
                        S.op("dve", [fn_, "qnw"], [kn_], lambda e: e.tensor_tensor(
                            out=kst[:, ki, :].rearrange("p (c d) -> p c d", d=dk),
                            in0=qkf[:, fi, :].rearrange("p (c d) -> p c d", d=dk), in1=wbc, op=ALU.mult))
                        seq, s0 = tt // 2, (tt % 2) * 128
                        if L == 0:
                            dst = nak[seq, hg * 4:(hg + 1) * 4, s0:s0 + 128, :]
                        else:
                            dst = nck[seq, :, s0:s0 + 128, :]
                        S.dma("sp", kn_, [kn_], [("out_k", L, hg, tt)], lambda e: e.dma_start(
                            out=dst.rearrange("h s d -> s h d"), in_=kst[:, ki, :].rearrange("p (h d) -> p h d", h=4)))
                        S.op("act", [kn_], [bn_], lambda e: e.activation(out=qkb[:, bi, :], in_=kst[:, ki, :], func=AF.Copy))
                    else:
                        S.op("dve", [fn_, "qnw"], [bn_], lambda e: e.tensor_tensor(
                            out=qkb[:, bi, :].rearrange("p (c d) -> p c d", d=dk),
                            in0=qkf[:, fi, :].rearrange("p (c d) -> p c d", d=dk), in1=wbc, op=ALU.mult))
                else:
                    ts = tt - 8
                    gi, gn_ = qkgr.next()
                    S.op("dve", [fn_, "qnw"], [gn_], lambda e: e.tensor_tensor(
                        out=qkg[:, gi, :].rearrange("p (c d) -> p c d", d=dk),
                        in0=qkf[:, fi, :].rearrange("p (c d) -> p c d", d=dk), in1=wbc, op=ALU.mult))
                    cosb = rope[:, ts, 0, :].unsqueeze(1).to_broadcast([128, nch, dk])
                    S.op("dve", [gn_, "qnw"], [fn_], lambda e: e.tensor_tensor(
                        out=qkf[:, fi, :].rearrange("p (c d) -> p c d", d=dk),
                        in0=qkg[:, gi, :].rearrange("p (c d) -> p c d", d=dk), in1=cosb, op=ALU.mult))
                    x5 = qkg[:, gi, :].rearrange("p (c a b q) -> p c a b q", a=2, b=2, q=q4)
                    t5 = sqf[:, qi, :].rearrange("p (c a b q) -> p c a b q", a=2, b=2, q=q4)
                    s5 = rope[:, ts, 1, :].rearrange("p (a b q) -> p a b q", a=2, b=2)
                    for b in range(2):
                        S.op("dve", [gn_, "qnw"], [qn_], lambda e, b=b: e.tensor_tensor(
                            out=t5[:, :, :, b, :], in0=x5[:, :, :, 1 - b, :],
                            in1=s5[:, :, b, :].unsqueeze(1).to_broadcast([128, nch, 2, q4]), op=ALU.mult))
                    S.op("dve", [fn_, qn_], [bn_], lambda e: e.tensor_tensor(
                        out=qkb[:, bi, :], in0=qkf[:, fi, :], in1=sqf[:, qi, :], op=ALU.add))
                t0 = tt * 128
                if SUBK <= 4:
                    return
                if kind == 'q':
                    defer(lambda: transpose4(qkb[:, bi, :], bn_, lambda e, pv: e.activation(out=QT[:, :, t0:t0 + 128], in_=pv, func=AF.Copy), [("QT", tt)]))
                elif not sample:
                    defer(lambda: transpose4(qkb[:, bi, :], bn_, lambda e, pv: e.activation(out=KTp[:, :, t0:t0 + 128], in_=pv, func=AF.Copy), ["KTp"]))
                else:
                    ts = tt - 8
                    defer(lambda: transpose4(qkb[:, bi, :], bn_, lambda e, pv: e.activation(out=ktst[:, :, ts * 128:(ts + 1) * 128], in_=pv, func=AF.Copy), [("ktst", ts)]))
            return f

        def evac_v(L, hg):
            def f(tt, ps, pn):
                if tt < 8:
                    vi, vn_ = vstr.next()
                    S.op("act", [pn], [vn_], lambda e: e.activation(out=vst[:, vi, :], in_=ps[:, :], func=AF.Copy))
                    seq, s0 = tt // 2, (tt % 2) * 128
                    if L == 0:
                        dst = nav[seq, hg * 4:(hg + 1) * 4, s0:s0 + 128, :]
                    else:
                        dst = ncv[seq, :, s0:s0 + 128, :]
                    S.dma("sp", vn_, [vn_], [("out_v", L, hg, tt)], lambda e: e.dma_start(
                        out=dst.rearrange("h s d -> s h d"), in_=vst[:, vi, :].rearrange("p (h d) -> p h d", h=4)))
                    S.op("dve", [vn_], ["V1p"], lambda e: e.tensor_copy(
                        out=V1p[:, tt, :, 0:128], in_=vst[:, vi, :].rearrange("p (h d) -> p h d", h=4)))
                else:
                    ts = tt - 8
                    S.op("dve", [pn], [("vsst", ts)], lambda e: e.tensor_copy(out=vsst[:, ts, :], in_=ps[:, :]))
            return f

        def evac_g(tt, ps, pn):
            S.op("act", [pn], [("SG", tt)], lambda e: e.activation(out=SG[:, tt, :], in_=ps[:, :], func=AF.Silu))

        def evac_g_sub(tt, ps, pn):
            qi, qn_ = qkfr.next()
            S.op("act", [pn], [qn_], lambda e: e.activation(out=qkf[:, qi, :], in_=ps[:, :], func=AF.Silu))
            S.op("dve", [qn_, "wsub"], [("SG", tt)], lambda e: e.tensor_tensor(
                out=SG[:, tt, :].rearrange("p (h d) -> p h d", h=4), in0=qkf[:, qi, :].rearrange("p (h d) -> p h d", h=4),
                in1=wsub[:, :].unsqueeze(1).to_broadcast([128, 4, 128]), op=ALU.mult))

        def evac_u(tt, ps, pn):
            npt = 16 if tt == 10 else 128
            S.op("dve", [pn], [("u_tok", tt)], lambda e: e.tensor_copy(out=u_tok[0:npt, tt, :], in_=ps[0:npt, :]))

        def kv_prefetch(L, hg):
            ck = cak if L == 0 else cck
            cv = cav if L == 0 else ccv
            h0 = hg * 4 if L == 0 else 0
            S.dma("pool", "cstg", [], ["cstg"], lambda e: [e.dma_start(
                out=cstg[:, t, :, :], in_=ck[h0:h0 + 4, t * 128:(t + 1) * 128, :].rearrange("h p d -> p h d")) for t in range(2)])
            S.dma("pool", "v1s_a", [], ["V1s"], lambda e: [e.dma_start(
                out=V1s[:, t, :, 0:128], in_=cv[h0:h0 + 4, t * 128:(t + 1) * 128, :].rearrange("h p d -> p h d")) for t in range(2)])

        def kv_exchange(L, hg, agi):
            flush_defer()
            ain, aout = agkv_in[agi], agkv_out[agi]
            S.dma("sp", "kvst", [("ktst", 0), ("ktst", 1), ("vsst", 0), ("vsst", 1)], [("agin", agi)], lambda e: [e.dma_start(
                out=ain[0:512, :].rearrange("(h d) t -> d h t", h=4), in_=ktst[:, :, :]), e.dma_start(
                out=ain[512:1024, :].rearrange("(t p a) b -> p t (a b)", t=2, a=2), in_=vsst[:, :, :])])
            S.dma("pool", "cc", [("agin", agi)], [("agout", agi)], lambda e: e.collective_compute(
                "AllGather", ALU.bypass, replica_groups=GROUPS, ins=[ain], outs=[aout]), inc=1)
            for t in range(2):
                transpose4(cstg[:, t, :, :].rearrange("p h d -> p (h d)"), "cstg",
                           lambda e, pv, t=t: e.activation(out=KTs[:, :, t * 128:(t + 1) * 128], in_=pv, func=AF.Copy), ["KTs"])
            aview = aout.rearrange("(r x) t -> r x t", r=4)
            S.dma("sp", "kts_b", [("agout", agi)], ["KTs"], lambda e: [e.dma_start(
                out=KTs[:, :, 256 + r * 256:256 + (r + 1) * 256],
                in_=aview[r, 0:512, :].rearrange("(h d) t -> d h t", h=4)) for r in range(4)])
            S.dma("sp", "v1s_b", [("agout", agi)], ["V1s"], lambda e: [e.dma_start(
                out=V1s[:, 2 + 2 * r + t, :, 0:128],
                in_=aview[r, 512 + t * 256:512 + (t + 1) * 256, :].rearrange("(p a) b -> p (a b)", a=2).rearrange("p (h d) -> p h d", h=4))
                for r in range(4) for t in range(2)])

        def set_ones():
            S.dma("pool", "c_ones", [], ["V1p", "V1s"], lambda e: [
                e.dma_start(out=R5[:, 4096:8256], in_=onesd[:, 0:4160]),
                e.dma_start(out=R6[:, 5120:10320], in_=onesd[:, 0:5200])])

        def attention(L, kc0, kv_of_head, hooks=None):
            flush_defer()
            nj = 2 if L == 0 else 1
            dk = 64 if L == 0 else 128
            sc = dk ** -0.5
            units = []
            for seq in range(4):
                for qh in range(4):
                    for j in range(nj):
                        units.append((seq, qh, j, (0, 1)))
            for qh in range(4):
                for qt in range(2):
                    for j in range(nj):
                        units.append((4, qh, j, (qt,)))

            def keys_of(seq):
                if seq < 4:
                    return [("p", seq * 2 + t) for t in range(2)]
                return [("s", t) for t in range(10)]

            def scores(u, ui):
                seq, qh, j, qts = u
                kvh = kv_of_head(qh)
                kts = keys_of(seq)
                nq = 128 * len(qts)
                q0 = seq * 256 + qts[0] * 128
                pslot = ui % 2
                per = 512 // nq
                ptv = PT[:, pslot, 0:len(kts) * nq].rearrange("p (k q) -> p k q", q=nq)
                for c in range(0, len(kts), per):
                    pi, pn = pSr.next()
                    nper = min(per, len(kts) - c)

                    def mm(e, c=c, pi=pi, nper=nper):
                        for t in range(nper):
                            kind, kt = kts[c + t]
                            if kind == "p":
                                ksrc = KTp[j * dk:(j + 1) * dk, kvh, kt * 128:(kt + 1) * 128]
                            else:
                                ksrc = KTs[j * dk:(j + 1) * dk, kvh, kt * 128:(kt + 1) * 128]
                            ins = e.matmul(pS[pi][:, t * nq:(t + 1) * nq], lhsT=ksrc,
                                           rhs=QT[j * dk:(j + 1) * dk, qh, q0:q0 + nq], start=True, stop=True)
                        return ins
                    S.op("pe", ["KTp" if seq < 4 else "KTs", ("QT", seq * 2), ("QT", seq * 2 + 1)], [pn], mm)
                    S.op("act", [pn], [("PT", pslot)], lambda e, c=c, pi=pi, nper=nper: e.activation(
                        out=ptv[:, c:c + nper, :], in_=pS[pi][:, 0:nper * nq].rearrange("p (t q) -> p t q", t=nper),
                        func=AF.Exp, scale=sc))

            def pv(u, ui, oslots):
                seq, qh, j, qts = u
                kvh = kv_of_head(qh)
                kts = keys_of(seq)
                nq = 128 * len(qts)
                pslot = ui % 2
                ptv = PT[:, pslot, 0:len(kts) * nq].rearrange("p (k q) -> p k q", q=nq)
                for qi_, qt in enumerate(qts):
                    oi, on = oslots[qi_]

                    def mm(e, qi_=qi_, oi=oi):
                        for i, (kind, kt) in enumerate(kts):
                            vsrc = V1p[:, kt, kvh, 0:129] if kind == "p" else V1s[:, kt, kvh, 0:129]
                            ins = e.matmul(pO[oi][:, j, 0:129], lhsT=ptv[:, i, qi_ * 128:(qi_ + 1) * 128], rhs=vsrc,
                                           start=(i == 0), stop=(i == len(kts) - 1))
                        return ins
                    S.op("pe", [("PT", pslot), "V1p" if seq < 4 else "V1s"], [on], mm)

            def combine(seq, qh, qts, oslots):
                chains = []
                for qi_, qt in enumerate(qts):
                    chains.append(combine_ops(seq, qh, qt, oslots[qi_]))
                n = max(len(c) for c in chains)
                for k in range(n):
                    for c in chains:
                        if k < len(c):
                            eng, rd, wr, fn = c[k]
                            if eng == "defer":
                                defer(fn)
                            else:
                                S.op(eng, rd, wr, fn)

            def combine_ops(seq, qh, qt, oslot):
                ops = []
                oi, on = oslot
                tt = seq * 2 + qt
                si, sn = statr.next()
                yi, yn = ytr4.next()
                ytv = ytk[:, yi // 2, (yi % 2) * 128:(yi % 2 + 1) * 128]
                if L == 0:
                    ti, tn = otr.next()
                    fi, fn_ = ofr.next()
                    ops.append(("dve", [on], [sn], lambda e: e.reciprocal(out=stat[:, si, 0:2], in_=pO[oi][:, :, 128])))
                    ops.append(("dve", [sn, "nlam"], [sn], lambda e: e.tensor_tensor(
                        out=stat[:, si, 2:3], in0=stat[:, si, 1:2], in1=nlam[:, 0:1], op=ALU.mult)))
                    ops.append(("dve", [on, sn], [tn], lambda e: e.tensor_scalar(
                        out=otm[:, ti, :], in0=pO[oi][:, 1, 0:128], scalar1=stat[:, si, 2:3], scalar2=None, op0=ALU.mult)))
                    ops.append(("dve", [on, sn, tn], [fn_], lambda e: e.scalar_tensor_tensor(
                        out=ofm[:, fi, :], in0=pO[oi][:, 0, 0:128], scalar=stat[:, si, 0:1], in1=otm[:, ti, :],
                        op0=ALU.mult, op1=ALU.add)))
                    ops.append(("act", [fn_], [tn, sn], lambda e: e.activation(
                        out=otm[:, ti, :], in_=ofm[:, fi, :], func=AF.Square, accum_out=stat[:, si, 4:5])))
                    ops.append(("act", [sn], [sn], lambda e: e.activation(
                        out=stat[:, si, 5:6], in_=stat[:, si, 4:5], func=AF.Ln, scale=1.0 / 128, bias=EPS)))
                    ops.append(("act", [sn], [sn], lambda e: e.activation(
                        out=stat[:, si, 6:7], in_=stat[:, si, 5:6], func=AF.Exp, scale=-0.5)))
                    ops.append(("dve", [fn_, sn, ("SG", tt)], [yn], lambda e: e.scalar_tensor_tensor(
                        out=ytv, in0=ofm[:, fi, :], scalar=stat[:, si, 6:7], in1=SG[:, tt, qh * 128:(qh + 1) * 128],
                        op0=ALU.mult, op1=ALU.mult)))
                else:
                    ops.append(("dve", [on], [sn], lambda e: e.reciprocal(out=stat[:, si, 0:1], in_=pO[oi][:, 0, 128:129])))
                    ops.append(("dve", [on, sn, ("SG", tt)], [yn], lambda e: e.scalar_tensor_tensor(
                        out=ytv, in0=pO[oi][:, 0, 0:128], scalar=stat[:, si, 0:1],
                        in1=SG[:, tt, qh * 128:(qh + 1) * 128], op0=ALU.mult, op1=ALU.mult)))

                def ytrans():
                    pi, pn = pPr.next()
                    pvw = pP[pi][:, :].bitcast(BF16)[:, 0:128]
                    S.op("pe", [yn, "identb"], [pn], lambda e: e.transpose(pvw, ytv, identb[:]))
                    S.op("act", [pn], [("yT", kc0 + qh)], lambda e: e.activation(
                        out=yT[:, kc0 + qh, tt * 128:(tt + 1) * 128], in_=pvw, func=AF.Copy))
                ops.append(("defer", None, None, ytrans))
                return ops

            scores(units[0], 0)
            oslots = None
            for ui, u in enumerate(units):
                if ui + 1 < len(units):
                    scores(units[ui + 1], ui + 1)
                seq, qh, j, qts = u
                if j == 0:
                    oslots = [pOr.next() for _ in qts]
                pv(u, ui, oslots)
                if hooks and ui in hooks:
                    hooks[ui]()
                flush_defer(keep=2 if L == 0 else 0)
                if j == nj - 1:
                    combine(seq, qh, qts, oslots)

        def pool_block(ub):
            flush_defer()
            its = [(seq, gl) for seq in range(5) for gl in range(2)]
            state = {}

            def stage_a(i):
                seq, gl = its[i]
                g = ub * 2 + gl
                stiles = [(seq * 2 + t, 128) for t in range(2)] if seq < 4 else [(8, 128), (9, 128), (10, 16)]
                qi, qn_ = ppr.next()
                for cb in range(2):
                    pi, pn = pSr.next()

                    def mm(e):
                        for k, (tt, npt) in enumerate(stiles):
                            bsrc = Bpt[0:npt, g, k, :] if seq < 4 else Bst[0:npt, g, k, :]
                            ins = e.matmul(pS[pi][:, 0:256], lhsT=u_tok[0:npt, tt, gl * 256 + cb * 128: gl * 256 + (cb + 1) * 128],
                                           rhs=bsrc, start=(k == 0), stop=(k == len(stiles) - 1))
                        return ins
                    S.op("pe", [("u_tok", tt) for tt, _ in stiles] + ["Bt"], [pn], mm)
                    S.op("act", [pn], [qn_], lambda e: e.activation(
                        out=pooledT[:, qi, cb, :], in_=pS[pi][:, 0:256], func=AF.Copy))
                state[i] = {"q": (qi, qn_), "y": []}

            def stage_b(i):
                seq, gl = its[i]
                g = ub * 2 + gl
                qi, qn_ = state[i]["q"]
                for qt in range(2):
                    tt = seq * 2 + qt
                    pi, pn = pPr.next()

                    def mm2(e):
                        for cb in range(2):
                            ins = e.matmul(pP[pi][:, 0:256], lhsT=pooledT[:, qi, cb, qt * 128:(qt + 1) * 128],
                                           rhs=pwt[:, g, cb, :], start=(cb == 0), stop=(cb == 1))
                        return ins
                    S.op("pe", [qn_, "pwt"], [pn], mm2)
                    ti, _ = ptr_.next()
                    tns = [("otm", 2 * ti), ("otm", 2 * ti + 1)]
                    ptv = otm[:, 2 * ti:2 * ti + 2, :].rearrange("p a d -> p (a d)")
                    S.op("dve", [pn, "pscbc"], tns, lambda e: e.tensor_tensor(
                        out=ptv, in0=pP[pi][:, 0:256], in1=pscbc[:, g * 256:(g + 1) * 256], op=ALU.mult))
                    yi, _ = ytr.next()
                    yns = [("ytk", 2 * yi), ("ytk", 2 * yi + 1)]
                    S.op("dve", tns + [("SG", tt)], yns, lambda e: e.tensor_tensor(
                        out=ytk[:, yi, :], in0=ptv, in1=SG[:, tt, gl * 256:(gl + 1) * 256], op=ALU.mult))
                    state[i]["y"].append((yi, yns, tt))

            def stage_c(i):
                seq, gl = its[i]
                g = ub * 2 + gl
                kcb = 8 + g * 2
                for yi, yns, tt in state[i]["y"]:
                    p2, pn2 = pPr.next()
                    pvw = pP[p2][:, :].bitcast(BF16)[:, 0:256]

                    def tr(e):
                        for b_ in range(2):
                            ins = e.transpose(pvw[:, b_ * 128:(b_ + 1) * 128], ytk[:, yi, b_ * 128:(b_ + 1) * 128], identb[:])
                        return ins
                    S.op("pe", yns + ["identb"], [pn2], tr)
                    S.op("act", [pn2], [("yT", kcb), ("yT", kcb + 1)], lambda e: e.activation(
                        out=yT[:, kcb:kcb + 2, tt * 128:(tt + 1) * 128], in_=pvw.rearrange("p (b t) -> p b t", b=2), func=AF.Copy))

            n = len(its)
            for i in range(n + 2):
                if i < n:
                    stage_a(i)
                if 0 <= i - 2 < n:
                    stage_c(i - 2)
                if 0 <= i - 1 < n:
                    stage_b(i - 1)

        def w_out_phase(L, w_out, inter=None, next_w=None):
            flush_defer()
            S.alias(R1wo, R5att + R6att + R6pool + [("u_tok", t) for t in range(11)])
            if inter is not None:
                S.alias(ALTmod, QSP_names + R5att + R6att + R6pool + [("u_tok", t) for t in range(11)])
                ixload, istats, ipe = make_modulate(L + 1, True)
            for j in range(2):
                S.dma("sp", ("gbc", j), [("mlin", L)], [("gbc", j)], lambda e, j=j: e.dma_start(
                    out=gbc[j][:, :], in_=mlin[L, j, 4096:6144].partition_broadcast(128)))
            slot = wload(w_out[:, 0:512], 512)

            def tile_io(tt):
                if L == 0:
                    src = xp[tt * 128:(tt + 1) * 128] if tt < 8 else xs[(tt - 8) * 128:(tt - 7) * 128]
                    return src, [], x1[tt * 128:(tt + 1) * 128], lambda cb: [("x1", tt)]
                dst = yp[tt * 128:(tt + 1) * 128] if tt < 8 else ys[(tt - 8) * 128:(tt - 7) * 128]
                return x1[tt * 128:(tt + 1) * 128], [("x1", tt)], dst, lambda cb: [("out_y", tt, cb)]

            xr = Ring("xblk", 4)
            xslots = {}
            pre = {}

            def xb_load(cb, tt):
                src, rd, _, _ = tile_io(tt)
                xi, xn = xr.next()
                xslots[(cb, tt)] = (xi, xn)
                S.dma("act", xn, rd, [xn], lambda e: e.dma_start(out=xblk[xi][:, :], in_=src[:, cb * 512:(cb + 1) * 512]))

            seq = [(cb, tt) for cb in range(4) for tt in range(10)]
            for k in range(3):
                xb_load(*seq[k])
            orr = Ring("oblk", 2)
            for k, (cb, tt) in enumerate(seq):
                if tt == 0:
                    if cb < 3:
                        nslot = wload(w_out[:, (cb + 1) * 512:(cb + 2) * 512], 512)
                    else:
                        nslot = None
                        if next_w is not None:
                            pre["slot"] = wload(next_w, 512)
                if k + 3 < len(seq):
                    xb_load(*seq[k + 3])
                jsel = 0 if tt < 8 else 1
                _, _, dst, wrf = tile_io(tt)
                xi, xn = xslots[(cb, tt)]
                pi, pn = pPr.next()

                def mm(e):
                    for kc in range(16):
                        ins = e.matmul(pP[pi][:, :], lhsT=yT[:, kc, tt * 128:(tt + 1) * 128], rhs=WB[:, slot, kc, :],
                                       start=(kc == 0), stop=(kc == 15))
                    return ins
                S.op("pe", yT_names + [("wb", slot)], [pn], mm)
                oi, on = orr.next()
                S.op("dve", [pn, ("gbc", jsel)], [on], lambda e: e.tensor_tensor(
                    out=oblk[oi][:, :], in0=pP[pi][:, :], in1=gbc[jsel][:, cb * 512:(cb + 1) * 512], op=ALU.mult))
                S.op("dve", [on, xn], [on], lambda e: e.tensor_tensor(
                    out=oblk[oi][:, :], in0=oblk[oi][:, :], in1=xblk[xi][:, :], op=ALU.add))
                S.dma("sp", on, [on], wrf(cb), lambda e: e.dma_start(
                    out=dst[:, cb * 512:(cb + 1) * 512], in_=oblk[oi][:, :]))
                if inter is not None and cb == 3:
                    if tt >= 4:
                        ipe(tt - 4)
                    if tt >= 2:
                        istats(tt - 2)
                    ixload(tt)
                if tt == 9:
                    slot = nslot
            if inter is not None:
                ipe(6)
                istats(8)
                ipe(7)
                istats(9)
                ipe(8)
                ipe(9)
            return pre.get("slot")

        ALL10 = list(range(10))

        class Stop(Exception):
            pass

        def gate(n):
            if stage < n:
                raise Stop()

        def main_flow():
            S.alias([("qkf", 0), ("qkf", 1)], ["qnw0", "lamp", "onesr"])
            gate(1)
            load_layer_consts(0)
            st3 = {}

            def hk0():
                ada_block(0, 0, ada0_slots[0])
                st3["s"] = wload(adaw[0, :, 1024:1536], 512)
            hooks0 = {5: hk0, 7: (lambda: ada_block(0, 1, ada0_slots[1]))}
            modulate(0, list(range(11)), hooks0)
            ada_block(0, 2, st3["s"])
            ada_finish(0)
            nxt = wload(ew_in[:, 1024:1536], 512)
            modulate_affine(0, True)
            S.alias(yT_names, R2mod + R2ada)
            set_ones()
            gate(2)
            for hg in range(2):
                sK = nxt
                kv_prefetch(0, hg)
                sV = wload(ew_in[:, 2048 + hg * 512:2048 + (hg + 1) * 512], 512)
                proj(sK, 512, ALL10, evac_qk(0, 'k', hg))
                if hg == 1:
                    fin_b()
                if int(os.environ.get("MK_SUBK", "99")) < 99:
                    raise Stop()
                sQ = wload(ew_in[:, hg * 512:(hg + 1) * 512], 512)
                proj(sV, 512, ALL10, evac_v(0, hg))
                gate(3)
                kv_exchange(0, hg, hg)
                gate(4)
                sG = wload(ew_in[:, 4096 + hg * 512:4096 + (hg + 1) * 512], 512)
                proj(sQ, 512, ALL10, evac_qk(0, 'q', hg))
                nxt = wload(ew_in[:, 1024 + 512:1024 + 1024], 512) if hg == 0 else wload(ew_in[:, 3072:3584], 512)
                proj(sG, 512, ALL10, evac_g_sub)
                gate(5)
                if hg == 0:
                    hk = {ui: (lambda nb=nb: ada_block(1, nb, wload(adaw[1, :, nb * 512:(nb + 1) * 512], 512, slot=sG)))
                          for nb, ui in enumerate((10, 28, 46))}
                    attention(0, hg * 4, lambda qh: qh, hk)
                    fin_b = ada_finish(1, split=True)
                else:
                    attention(0, hg * 4, lambda qh: qh)
                gate(6)
            flush_defer()
            S.alias(["Bt", "pwt", "pscbc"], R6att)
            S.alias([("u_tok", t) for t in range(11)], R5att)
            S.dma("pool", "c_B", [], ["Bt", "pwt", "pscbc"], lambda e: [
                e.dma_start(out=Bst[0:16, :, 2, :], in_=Bs[:, 256:272, :].rearrange("g p t -> p g t")),
                e.dma_start(out=pscbc[:, :], in_=pscale.partition_broadcast(128))] + [
                e.dma_start(out=Bpt[:, g, :, :], in_=Bp[g].rearrange("(s p) t -> p s t", p=128)) for g in range(4)] + [
                e.dma_start(out=Bst[:, g, 0:2, :], in_=Bs[g, 0:256, :].rearrange("(s p) t -> p s t", p=128)) for g in range(4)] + [
                e.dma_start(out=pwt[:, g, :, :], in_=poolw[g].rearrange("(c p) d -> p c d", p=128)) for g in range(4)])
            for ub in range(2):
                sU = nxt
                sGp = wload(ew_in[:, 5120 + ub * 512:5120 + (ub + 1) * 512], 512)
                proj(sU, 512, list(range(11)), evac_u)
                if ub == 0:
                    nxt = wload(ew_in[:, 3584:4096], 512)
                proj(sGp, 512, ALL10, evac_g)
                pool_block(ub)
            gate(7)
            load_layer_consts(1)
            sK1 = w_out_phase(0, ew_out, inter=True, next_w=gw_in[:, 2048:2560])
            gate(8)

            modulate_affine(1, False)
            S.alias(QSP_names, ALTmod)
            S.alias(R6att, R1wo + ALTmod)
            S.alias(R5att, R1wo + ALTmod)
            set_ones()
            sK = sK1
            kv_prefetch(1, 0)
            sV = wload(gw_in[:, 2560:3072], 512)
            proj(sK, 512, ALL10, evac_qk(1, 'k', 0))
            sQ = wload(gw_in[:, 0:512], 512)
            proj(sV, 512, ALL10, evac_v(1, 0))
            gate(9)
            kv_exchange(1, 0, 2)
            for kvh in range(4):
                sG = wload(gw_in[:, 3072 + kvh * 512:3072 + (kvh + 1) * 512], 512)
                proj(sQ, 512, ALL10, evac_qk(1, 'q', kvh))
                if kvh < 3:
                    sQ = wload(gw_in[:, (kvh + 1) * 512:(kvh + 2) * 512], 512)
                proj(sG, 512, ALL10, evac_g)
                attention(1, kvh * 4, lambda qh, kvh=kvh: kvh)
            gate(10)
            w_out_phase(1, gw_out)

        try:
            main_flow()
        except Stop:
            pass
        flush_defer()

        S.wait_all("sp")
        build_program.stats = (S.n_ops, dict(S.count))
    return nc


def _rope_tables(pos0, dim):
    pos = np.arange(pos0, pos0 + 256)
    row = (pos // GRID_W).astype(np.float32)
    col = (pos % GRID_W).astype(np.float32)
    quarter = dim // 4
    freqs = (ROPE_THETA ** (-np.arange(quarter, dtype=np.float32) / quarter)).astype(np.float32)
    ar = row[:, None] * freqs
    ac = col[:, None] * freqs
    cos = np.concatenate([np.cos(ar), np.cos(ar), np.cos(ac), np.cos(ac)], -1)
    sinm = np.concatenate([-np.sin(ar), np.sin(ar), -np.sin(ac), np.sin(ac)], -1)
    return np.stack([cos, sinm], 1).astype(np.float32)


def _pool_mats(L, t0, t1, src_idx):
    out = np.zeros((4, len(src_idx), t1 - t0), np.float32)
    for g, w in enumerate(POOL_WINDOWS):
        half = w // 2
        for tl, t in enumerate(range(t0, t1)):
            lo, hi = max(t - half, 0), min(t + half, L)
            cnt = float(hi - lo)
            for sl, s in enumerate(src_idx):
                if s < 0:
                    continue
                v = 0.0
                if lo <= s < hi:
                    v += 1.0 / cnt
                if s == t:
                    v -= 1.0
                out[g, sl, tl] = v
    return out


_CACHE = {}


def _make_in_maps(x_prompt, x_sample, cache_a_k, cache_a_v, cache_c_k, cache_c_v, c, c_ctx,
           norm_w, ada_w, ada_b,
           even_w_in, even_q_norm_w, even_k_norm_w, even_lam_q1, even_lam_k1,
           even_lam_q2, even_lam_k2, even_subln_w, even_pool_w, even_pool_scale,
           even_w_out, gqa_w_in, gqa_q_norm_w, gqa_k_norm_w, gqa_w_out):
    f = lambda a: np.ascontiguousarray(np.asarray(a, dtype=np.float32))
    x_prompt, x_sample = f(x_prompt), f(x_sample)
    ada_w, ada_b = f(ada_w), f(ada_b)
    ident = np.eye(128, dtype=np.float32)
    Bp = _pool_mats(256, 0, 256, list(range(256)))
    lamv = np.stack([f(even_lam_q1)[0], f(even_lam_k1)[0], f(even_lam_q2)[0], f(even_lam_k2)[0]], 0)
    shared = {
        "norm_w": f(norm_w), "ew_in": f(even_w_in)[0], "ew_out": f(even_w_out)[0],
        "gw_in": f(gqa_w_in)[0], "gw_out": f(gqa_w_out)[0],
        "eqn": f(even_q_norm_w)[0], "ekn": f(even_k_norm_w)[0], "lamv": f(lamv),
        "subw": f(even_subln_w)[0], "poolw": f(even_pool_w)[0], "pscale": f(even_pool_scale)[0],
        "gqn": f(gqa_q_norm_w)[0], "gkn": f(gqa_k_norm_w)[0], "ident": ident, "Bp": Bp,
        "onesd": np.ones((128, 5200), np.float32),
    }
    in_maps = []
    for core in range(8):
        b, r = core // 4, core % 4
        t0 = r * 256
        halo_idx = list(range(t0 - 8, t0)) + list(range(t0 + 256, t0 + 264))
        xh = np.zeros((16, D), np.float32)
        src_idx = list(range(t0, t0 + 256))
        for i, s in enumerate(halo_idx):
            if 0 <= s < 1024:
                xh[i] = x_sample[b, s]
                src_idx.append(s)
            else:
                src_idx.append(-1)
        m = dict(shared)
        m.update({
            "xp": x_prompt[core * 4:(core + 1) * 4].reshape(1024, D),
            "xs": np.ascontiguousarray(x_sample[b, t0:t0 + 256]),
            "xh": xh,
            "cond": np.stack([f(c_ctx), f(c)[b]], 0),
            "cak": f(cache_a_k)[b, 0], "cav": f(cache_a_v)[b, 0],
            "cck": f(cache_c_k)[b, 0], "ccv": f(cache_c_v)[b, 0],
            "adaw": np.ascontiguousarray(ada_w[:, :, r * 1536:(r + 1) * 1536]),
            "adab": np.ascontiguousarray(ada_b[:, r * 1536:(r + 1) * 1536]),
            "ropeA": _rope_tables(t0, 64), "ropeC": _rope_tables(t0, 128),
            "Bs": _pool_mats(1024, t0, t0 + 256, src_idx),
        })
        in_maps.append(m)
    return in_maps


def _assemble(R):
    y_p = np.concatenate([R[i]["yp"].reshape(4, 256, D) for i in range(8)], 0)
    y_s = np.stack([np.concatenate([R[b * 4 + r]["ys"] for r in range(4)], 0) for b in range(2)], 0)
    nak = np.concatenate([R[i]["nak"] for i in range(8)], 0)[:, None]
    nav = np.concatenate([R[i]["nav"] for i in range(8)], 0)[:, None]
    nck = np.concatenate([R[i]["nck"] for i in range(8)], 0)[:, None]
    ncv = np.concatenate([R[i]["ncv"] for i in range(8)], 0)[:, None]
    return (y_p.astype(np.float32), y_s.astype(np.float32), nak.astype(np.float32),
            nav.astype(np.float32), nck.astype(np.float32), ncv.astype(np.float32))


def kernel(**inputs):
    stage = _CACHE.get("stage", 99)
    if ("nc", stage) not in _CACHE:
        _CACHE[("nc", stage)] = build_program(stage)
    nc = _CACHE[("nc", stage)]
    in_maps = _make_in_maps(**inputs)
    res = run_bass_kernel_spmd(nc, in_maps, core_ids=list(range(8)))
    return _assemble(res.results)
```

```python
import contextlib
import math
import os
import numpy as np
import ml_dtypes
import concourse.bass as bass
import concourse.mybir as mybir
from concourse.bass_utils import run_bass_kernel_spmd

F32 = mybir.dt.float32
BF16 = mybir.dt.bfloat16
AF = mybir.ActivationFunctionType
ALU = mybir.AluOpType
AX = mybir.AxisListType

D = 2048
KC = 16
EPS = 1e-6
NTOK = 1280
HALO0 = 1280
GRID_W = 64
ROPE_THETA = 10000.0
POOL_WINDOWS = (2, 4, 8, 16)


class Sched:
    def __init__(self, nc, stack):
        self.nc = nc
        self.stack = stack
        self.engs = {"pe": nc.tensor, "act": nc.scalar, "dve": nc.vector,
                     "pool": nc.gpsimd, "sp": nc.sync}
        self.sems = {}
        self.count = {}
        for e in self.engs:
            self.sems[e] = stack.enter_context(nc.semaphore("s_" + e))
            self.count[e] = 0
        self.waited = {e: {} for e in self.engs}
        self.last_write = {}
        self.reads = {}
        self.n_ops = 0

    def _deps(self, reads, writes):
        deps = {}

        def add(ev):
            if ev is None:
                return
            k, v = ev
            if deps.get(k, 0) < v:
                deps[k] = v
        for r in reads:
            add(self.last_write.get(r))
        for w in writes:
            add(self.last_write.get(w))
            for ev in self.reads.get(w, ()):
                add(ev)
        return deps

    def _wait(self, eng, deps, skip_self=False):
        e = self.engs[eng]
        for k, v in deps.items():
            if skip_self and k == eng:
                continue
            if self.waited[eng].get(k, 0) >= v:
                continue
            e.wait_ge(self.sems[k], v)
            self.waited[eng][k] = v

    def _record(self, ev, reads, writes):
        for r in reads:
            lst = self.reads.setdefault(r, [])
            lst.append(ev)
            if len(lst) > 64:
                mx = {}
                for k, v in lst:
                    if mx.get(k, 0) < v:
                        mx[k] = v
                self.reads[r] = list(mx.items())
        for w in writes:
            self.last_write[w] = ev
            self.reads[w] = []

    def op(self, eng, reads, writes, fn):
        deps = self._deps(reads, writes)
        self._wait(eng, deps, skip_self=(eng == "pe"))
        ins = fn(self.engs[eng])
        ins.then_inc(self.sems[eng], 1)
        self.count[eng] += 1
        ev = (eng, self.count[eng])
        self._record(ev, reads, writes)
        self.n_ops += 1
        return ev

    def dma(self, queue, semkey, reads, writes, fn, inc=16):
        if semkey not in self.sems:
            nm = "d_" + "".join(ch for ch in str(semkey) if ch.isalnum() or ch == "_")
            self.sems[semkey] = self.stack.enter_context(self.nc.semaphore(nm))
            self.count[semkey] = 0
        deps = self._deps(reads, writes)
        self._wait(queue, deps)
        inss = fn(self.engs[queue])
        if not isinstance(inss, (list, tuple)):
            inss = [inss]
        for ins in inss:
            ins.then_inc(self.sems[semkey], inc)
            self.count[semkey] += inc
        ev = (semkey, self.count[semkey])
        self._record(ev, reads, writes)
        return ev

    def alias(self, new_names, old_names):
        evs = []
        for o in old_names:
            if self.last_write.get(o) is not None:
                evs.append(self.last_write[o])
            evs.extend(self.reads.get(o, ()))
        mx = {}
        for k, v in evs:
            if mx.get(k, 0) < v:
                mx[k] = v
        for n in new_names:
            prev = []
            if self.last_write.get(n) is not None:
                prev.append(self.last_write[n])
            prev.extend(self.reads.get(n, ()))
            m2 = dict(mx)
            for k, v in prev:
                if m2.get(k, 0) < v:
                    m2[k] = v
            self.last_write[n] = None
            self.reads[n] = list(m2.items())

    def wait_all(self, eng):
        deps = {k: c for k, c in self.count.items() if c > 0}
        self._wait(eng, deps)


class Ring:
    def __init__(self, name, n):
        self.name, self.n, self.i = name, n, 0

    def next(self):
        s = self.i % self.n
        self.i += 1
        return s, (self.name, s)


def build_program(stage=99):
    nc = bass.Bass("TRN2", target_bir_lowering=False)

    def din(name, shape, dt=F32):
        return nc.dram_tensor(name, list(shape), dt, kind="ExternalInput").ap()

    def dout(name, shape, dt=F32):
        return nc.dram_tensor(name, list(shape), dt, kind="ExternalOutput").ap()

    def dint(name, shape, dt=F32):
        return nc.dram_tensor(name, list(shape), dt).ap()

    xp = din("xp", [1024, D]); xs = din("xs", [256, D]); xh = din("xh", [16, D])
    cond = din("cond", [2, D])
    cak = din("cak", [8, 256, 128]); cav = din("cav", [8, 256, 128])
    cck = din("cck", [4, 256, 128]); ccv = din("ccv", [4, 256, 128])
    norm_w = din("norm_w", [2, D])
    adaw = din("adaw", [2, D, 1536]); adab = din("adab", [2, 1536])
    ew_in = din("ew_in", [D, 6144]); ew_out = din("ew_out", [D, D])
    gw_in = din("gw_in", [D, 5120]); gw_out = din("gw_out", [D, D])
    eqn = din("eqn", [64]); ekn = din("ekn", [64]); lamv = din("lamv", [4, 64])
    subw = din("subw", [128]); poolw = din("poolw", [4, 256, 256]); pscale = din("pscale", [1024])
    gqn = din("gqn", [128]); gkn = din("gkn", [128])
    ident = din("ident", [128, 128])
    ropeA = din("ropeA", [256, 2, 64]); ropeC = din("ropeC", [256, 2, 128])
    Bp = din("Bp", [4, 256, 256]); Bs = din("Bs", [4, 272, 256])
    onesd = din("onesd", [128, 5200])

    yp = dout("yp", [1024, D]); ys = dout("ys", [256, D])
    nak = dout("nak", [4, 8, 256, 128]); nav = dout("nav", [4, 8, 256, 128])
    nck = dout("nck", [4, 4, 256, 128]); ncv = dout("ncv", [4, 4, 256, 128])

    x1 = dint("x1", [NTOK, D])
    agm_in = [dint("agm_in%d" % l, [2, 1536]) for l in range(2)]
    agm_out = [dint("agm_out%d" % l, [8, 1536]) for l in range(2)]
    mlin = dint("mlin", [2, 2, 6144])
    agkv_in = [dint("agkv_in%d" % i, [1024, 256], BF16) for i in range(3)]
    agkv_out = [dint("agkv_out%d" % i, [4096, 256], BF16) for i in range(3)]

    with contextlib.ExitStack() as st:
        S = Sched(nc, st)

        def sbt(name, shape, dt):
            return st.enter_context(nc.sbuf_tensor(name, list(shape), dt))

        def pst(name, shape, dt=F32):
            return st.enter_context(nc.psum_tensor(name, list(shape), dt))

        R1 = sbt("R1", [128, 16 * 1296], BF16)
        R2 = sbt("R2", [128, 16 * 1280], BF16)
        WB = sbt("WB", [128, 2, 16, 512], BF16)
        R5 = sbt("R5", [128, 8256], BF16)
        R6 = sbt("R6", [128, 10320], BF16)
        QT = sbt("QT", [128, 4, 1280], BF16)
        SG = sbt("SG", [128, 10, 512], BF16)
        PT = sbt("PT", [128, 2, 1280], BF16)

        hT = R1[:, :].rearrange("p (k t) -> p k t", k=16)
        yT = R2[:, :].rearrange("p (k t) -> p k t", k=16)
        KTp = R5[:, 0:4096].rearrange("p (h t) -> p h t", h=4)
        V1p = R5[:, 4096:8256].rearrange("p (t h c) -> p t h c", t=8, h=4)
        KTs = R6[:, 0:5120].rearrange("p (h t) -> p h t", h=4)
        V1s = R6[:, 5120:10320].rearrange("p (t h c) -> p t h c", t=10, h=4)
        xts = [R2[:, s * 4096:(s + 1) * 4096].bitcast(F32) for s in range(3)]
        sqj = R2[:, 12288:12288 + 2048]
        xsb = [R2[:, 14336 + s * 2048:14336 + (s + 1) * 2048] for s in range(2)]
        condt = R2[:, 0:4096].bitcast(F32)
        sct = R2[:, 4096:8192].bitcast(F32)
        adabt = R2[:, 8192:8192 + 6144].bitcast(F32).rearrange("p (l n) -> p l n", l=2)
        mrow = R2[:, 14336:14336 + 6144].bitcast(F32).rearrange("p (l n) -> p l n", l=2)
        gbc = [R6[:, j * 4096:(j + 1) * 4096].bitcast(F32) for j in range(2)]
        xblk = [R5[:, s * 1024:(s + 1) * 1024].bitcast(F32) for s in range(4)]
        oblk = [R5[:, 4096 + s * 1024: 4096 + (s + 1) * 1024].bitcast(F32) for s in range(2)]
        u_tok = R5[:, 0:11 * 512].rearrange("p (t c) -> p t c", t=11)
        Bpt = R6[:, 0:2048].rearrange("p (g s t) -> p g s t", g=4, s=2)
        Bst = R6[:, 2048:2048 + 3072].rearrange("p (g s t) -> p g s t", g=4, s=3)
        pwt = R6[:, 5120:5120 + 2048].rearrange("p (g c d) -> p g c d", g=4, c=2)
        pscbc = R6[:, 7168:7168 + 2048].bitcast(F32)

        QTf = QT[:, :, :].rearrange("p h t -> p (h t)")
        SGf = SG[:, :, :].rearrange("p t c -> p (t c)")
        PTf = PT[:, :, :].rearrange("p s q -> p (s q)")
        xts_alt = [QTf[:, 0:4096].bitcast(F32), SGf[:, 0:4096].bitcast(F32)]
        sqj_alt = PTf[:, 0:2048]
        xsb_alt = [R5[:, 6144:8192], R6[:, 8192:10240]]
        identf = sbt("identf", [128, 128], F32)
        identb = sbt("identb", [128, 128], BF16)
        dg = sbt("dg", [128, 2, 128], F32)
        stat = sbt("stat", [128, 16, 16], F32)
        shsc = sbt("shsc", [128, 2, 2, 2, 16], F32)
        nwt = sbt("nwt", [128, 2, 16], F32)
        s1t = sbt("s1t", [128, 2, 2, 16], F32)
        scT = sbt("scT", [128, 16, 2], BF16)
        qnw = sbt("qnw", [128, 2, 128], F32)
        lams = sbt("lams", [128, 8], F32)
        nlam = sbt("nlam", [128, 1], F32)
        wsub = sbt("wsub", [128, 128], F32)
        ropet = sbt("ropet", [128, 2, 2, 128], F32)
        sqf = sbt("sqf", [128, 1, 512], F32)
        qkf = sbt("qkf", [128, 2, 512], F32)
        qkg = sbt("qkg", [128, 1, 512], F32)
        qkb = sbt("qkb", [128, 2, 512], BF16)
        kst = sbt("kst", [128, 1, 512], F32)
        vst = sbt("vst", [128, 1, 512], F32)
        lamt = qkf[:, 0, 0:256].rearrange("p (a d) -> p a d", a=4)
        lampt = qkf[:, 0, 256:384].rearrange("p (a d) -> p a d", a=2)
        onesr = qkf[:, 0, 384:512]
        ktst = sbt("ktst", [128, 4, 256], BF16)
        vsst = sbt("vsst", [128, 2, 512], BF16)
        cstg = sbt("cstg", [128, 2, 4, 128], BF16)
        otm = sbt("otm", [128, 4, 128], F32)
        ofm = sbt("ofm", [128, 2, 128], F32)
        ytk = sbt("ytk", [128, 2, 256], BF16)
        pooledT = sbt("pooledT", [128, 2, 2, 256], BF16)

        pP = [pst("pP%d" % i, [128, 512]) for i in range(2)]
        pS = [pst("pS%d" % i, [128, 512]) for i in range(2)]
        pO = [pst("pO%d" % i, [128, 2, 256]) for i in range(4)]

        statr = Ring("stat", 16)
        sqr, qkfr, qkgr, qkbr = Ring("sqf", 1), Ring("qkf", 2), Ring("qkg", 1), Ring("qkb", 2)
        kstr, vstr = Ring("kst", 1), Ring("vst", 1)
        otr, ofr, ytr, ytr4 = Ring("otm", 4), Ring("ofm", 2), Ring("ytkp", 2), Ring("ytk", 4)
        pPr, pSr, pOr = Ring("pP", 2), Ring("pS", 2), Ring("pO", 4)
        ptr_, ppr = Ring("ptmp", 2), Ring("pooledT", 2)

        R2mod = [("xt", 0), ("xt", 1), ("xt", 2), "sqj", ("xsb", 0), ("xsb", 1)]
        R2ada = ["condt", "sct"]
        ALTmod = [("xta", 0), ("xta", 1), "sqja", ("xsba", 0), ("xsba", 1)]
        QSP_names = [("QT", t) for t in range(10)] + [("SG", t) for t in range(10)] + [("PT", 0), ("PT", 1)]
        R1wo = [("gbc", 0), ("gbc", 1), ("xblk", 0), ("xblk", 1), ("xblk", 2), ("xblk", 3), ("oblk", 0), ("oblk", 1)]
        hT_names = [("hT", t, k) for t in range(11) for k in range(16)]
        yT_names = [("yT", k) for k in range(16)]
        R5att = ["KTp", "V1p"]
        R6att = ["KTs", "V1s"]
        R6pool = ["Bt", "pwt", "pscbc"]

        S.dma("sp", "c0", [], ["identf", "qnw0", "condt"], lambda e: [
            e.dma_start(out=identf[:], in_=ident),
            e.dma_start(out=condt[0:2, :], in_=cond),
            e.dma_start(out=wsub[:], in_=subw.partition_broadcast(128)),
            e.dma_start(out=lamt[0:1, :, :], in_=lamv.rearrange("(o a) d -> o a d", o=1)),
        ])

        def load_layer_consts(L):
            if L == 0:
                S.dma("sp", "c2", [], ["qnw"], lambda e: [
                    e.dma_start(out=qnw[:, 0, 0:64], in_=eqn.partition_broadcast(128)),
                    e.dma_start(out=qnw[:, 1, 0:64], in_=ekn.partition_broadcast(128)),
                ] + [e.dma_start(out=ropet[:, t, :, 0:64], in_=ropeA[t * 128:(t + 1) * 128]) for t in range(2)])
            else:
                S.dma("sp", "c2", [], ["qnw"], lambda e: [
                    e.dma_start(out=qnw[:, 0, :], in_=gqn.partition_broadcast(128)),
                    e.dma_start(out=qnw[:, 1, :], in_=gkn.partition_broadcast(128)),
                ] + [e.dma_start(out=ropet[:, t, :, :], in_=ropeC[t * 128:(t + 1) * 128]) for t in range(2)])
        S.op("dve", ["identf"], ["identb"], lambda e: e.tensor_copy(out=identb[:], in_=identf[:]))
        LAM_INIT = 0.8 - 0.6 * math.exp(0.0)
        S.op("dve", ["qnw0"], ["wsub"], lambda e: e.tensor_scalar(
            out=wsub[:], in0=wsub[:], scalar1=1.0 - LAM_INIT, scalar2=None, op0=ALU.mult))
        S.op("dve", ["qnw0"], ["lamp"], lambda e: e.tensor_tensor(
            out=lampt[0:1, :, :], in0=lamt[0:1, 0:4:2, :], in1=lamt[0:1, 1:4:2, :], op=ALU.mult))
        S.op("dve", ["lamp"], ["lams0"], lambda e: e.tensor_reduce(
            out=lams[0:1, 0:2], in_=lampt[0:1, :, :], axis=AX.X, op=ALU.add))
        S.op("act", ["lams0"], ["lams1"], lambda e: e.activation(
            out=lams[0:1, 2:4], in_=lams[0:1, 0:2], func=AF.Exp))
        S.op("dve", ["lams1"], ["lams2"], lambda e: e.tensor_tensor(
            out=lams[0:1, 4:5], in0=lams[0:1, 3:4], in1=lams[0:1, 2:3], op=ALU.subtract))
        S.op("dve", ["lams2"], ["lams3"], lambda e: e.tensor_scalar(
            out=lams[0:1, 5:6], in0=lams[0:1, 4:5], scalar1=-LAM_INIT, scalar2=None, op0=ALU.add))
        S.op("dve", [], ["onesr"], lambda e: e.memset(onesr[0:1, :], 1.0))
        pi0, pn0 = pPr.next()
        S.op("pe", ["onesr", "lams3"], [pn0], lambda e: e.matmul(
            pP[pi0][:, 0:1], lhsT=onesr[0:1, :], rhs=lams[0:1, 5:6], start=True, stop=True))
        S.op("dve", [pn0], ["nlam"], lambda e: e.tensor_copy(out=nlam[:], in_=pP[pi0][:, 0:1]))

        S.op("act", ["condt"], ["sct"], lambda e: e.activation(out=sct[0:2, :], in_=condt[0:2, :], func=AF.Silu))

        ps0, psn0 = pSr.next()

        def mm_sct(e):
            for kc in range(16):
                ins = e.matmul(pS[ps0][:, 2 * kc:2 * kc + 2], lhsT=sct[0:2, kc * 128:(kc + 1) * 128],
                               rhs=identf[0:2, 0:2], start=True, stop=True)
            return ins
        S.op("pe", ["sct", "identf"], [psn0], mm_sct)
        S.op("dve", [psn0], ["scT"], lambda e: e.tensor_copy(
            out=scT[:].rearrange("p k j -> p (k j)"), in_=pS[ps0][:, 0:32]))

        wstate = {"n": 0}

        def wload(src, ncols, slot=None):
            if slot is None:
                slot = wstate["n"] % 2
                wstate["n"] += 1
            S.dma("pool", ("wb", slot), [], [("wb", slot)], lambda e: e.dma_start(
                out=WB[:, slot, :, 0:ncols], in_=src.rearrange("(kc p) n -> p kc n", p=128)))
            return slot

        GROUPS = [[0, 1, 2, 3], [4, 5, 6, 7]]
        s1_names = [("s1t", l, j) for l in range(2) for j in range(2)]

        def ada_block(L, nb, slot):
            S.dma("sp", "c_bias", [], [("vst", 0)], lambda e: [
                e.dma_start(out=vst[p:p + 1, 0, :], in_=adab[L:L + 1, nb * 512:(nb + 1) * 512]) for p in range(2)])
            pi, pn = pPr.next()

            def mm_ada(e):
                for kc in range(16):
                    ins = e.matmul(pP[pi][0:2, :], lhsT=scT[:, kc, :], rhs=WB[:, slot, kc, :],
                                   start=(kc == 0), stop=(kc == 15))
                return ins
            S.op("pe", ["scT", ("wb", slot)], [pn], mm_ada)
            S.op("dve", [pn, ("vst", 0)], [("kst", 0)], lambda e: e.tensor_tensor(
                out=kst[0:2, 0, :], in0=pP[pi][0:2, :], in1=vst[0:2, 0, :], op=ALU.add))
            S.dma("sp", "c_st", [("kst", 0)], [("agm_in", L)], lambda e: e.dma_start(
                out=agm_in[L][:, nb * 512:(nb + 1) * 512], in_=kst[0:2, 0, :]))

        def ada_finish(L, split=False):
            S.dma("pool", "cc", [("agm_in", L)], [("agm_out", L)], lambda e: e.collective_compute(
                "AllGather", ALU.bypass, replica_groups=GROUPS, ins=[agm_in[L]], outs=[agm_out[L]]), inc=1)
            S.dma("sp", "c1", [("agm_out", L)], [("mlin", L)], lambda e: e.dma_start(
                out=mlin[L].rearrange("j (r i) -> r j i", r=4),
                in_=agm_out[L].rearrange("(r j) i -> r j i", r=4)))
            if split:
                mt_a = otm[:, 0:2, :]
                mt_b = otm[:, 2, :]
                stg = [("otm", 0), ("otm", 1), ("otm", 2)]
            else:
                mt_a = kst[:, 0, 0:256].rearrange("p (i c) -> p i c", i=2)
                mt_b = vst[:, 0, 0:128]
                stg = [("kst", 0), ("vst", 0)]
            S.dma("sp", "c1", [("mlin", L)], stg, lambda e: [
                e.dma_start(out=mt_a[0:32, j, :], in_=mlin[L, j, 0:4096].rearrange("(a p) -> a p", p=128))
                for j in range(2)] + [
                e.dma_start(out=mt_b[0:16, :], in_=norm_w[L].rearrange("(a p) -> a p", p=128))])
            def part_b():
                ps1, psn1 = pSr.next()

                def mm_mt(e):
                    for j in range(2):
                        ins = e.matmul(pS[ps1][:, j * 32:(j + 1) * 32], lhsT=mt_a[0:32, j, :], rhs=identf[0:32, 0:32], start=True, stop=True)
                    ins = e.matmul(pS[ps1][:, 64:80], lhsT=mt_b[0:16, :], rhs=identf[0:16, 0:16], start=True, stop=True)
                    return ins
                S.op("pe", stg + ["identf"], [psn1], mm_mt)
                S.op("dve", [psn1], ["shsc"], lambda e: e.tensor_copy(
                    out=shsc[:, L, :, :, :].rearrange("p j t k -> p (j t k)"), in_=pS[ps1][:, 0:64]))
                S.op("dve", [psn1], ["shsc"], lambda e: e.tensor_copy(out=nwt[:, L, :], in_=pS[ps1][:, 64:80]))
                for j in range(2):
                    S.op("dve", ["shsc"], [("s1t", L, j)], lambda e: e.scalar_tensor_tensor(
                        out=s1t[:, L, j, :], in0=shsc[:, L, j, 1, :], scalar=1.0, in1=nwt[:, L, :],
                        op0=ALU.add, op1=ALU.mult))

            if split:
                return part_b
            part_b()

        ada0_slots = [wload(adaw[0, :, nb * 512:(nb + 1) * 512], 512) for nb in range(2)]

        def tile_tok0(tt):
            return HALO0 if tt == 10 else tt * 128

        DEFER = []

        def defer(fn):
            DEFER.append(fn)

        def flush_defer(keep=0):
            while len(DEFER) > keep:
                DEFER.pop(0)()

        def make_modulate(L, alt):
            if alt:
                xt_t, sq_t, xb_t, nslot = xts_alt, sqj_alt, xsb_alt, 2
                nxt_, nsq, nxb, q = "xta", "sqja", "xsba", "pool"
            else:
                xt_t, sq_t, xb_t, nslot = xts, sqj, xsb, 3
                nxt_, nsq, nxb, q = "xt", "sqj", "xsb", "sp"

            def xload(tt):
                npt = 16 if tt == 10 else 128
                slot = tt % nslot
                if L == 0:
                    src = xp[tt * 128:(tt + 1) * 128] if tt < 8 else (xs[(tt - 8) * 128:(tt - 7) * 128] if tt < 10 else xh)
                    rd = []
                else:
                    src = x1[tt * 128:(tt + 1) * 128]
                    rd = [("x1", tt)]
                S.dma(q, (nxt_, slot), rd, [(nxt_, slot)], lambda e: e.dma_start(out=xt_t[slot][0:npt, :], in_=src))

            def stats(tt):
                npt = 16 if tt == 10 else 128
                slot = tt % nslot
                bslot = tt % 2
                si, sn = statr.next()
                S.op("act", [(nxt_, slot)], [nsq, sn], lambda e: e.activation(
                    out=sq_t[0:npt, :], in_=xt_t[slot][0:npt, :], func=AF.Square, accum_out=stat[0:npt, si, 0:1]))
                S.op("act", [sn], [sn], lambda e: e.activation(
                    out=stat[0:npt, si, 1:2], in_=stat[0:npt, si, 0:1], func=AF.Ln, scale=1.0 / D, bias=EPS))
                S.op("act", [sn], [sn], lambda e: e.activation(
                    out=stat[0:npt, si, 2:3], in_=stat[0:npt, si, 1:2], func=AF.Exp, scale=-0.5))
                S.op("dve", [sn, (nxt_, slot)], [(nxb, bslot)], lambda e: e.tensor_scalar(
                    out=xb_t[bslot][0:npt, :], in0=xt_t[slot][0:npt, :], scalar1=stat[0:npt, si, 2:3],
                    scalar2=None, op0=ALU.mult))

            def pe_part(tt):
                npt = 16 if tt == 10 else 128
                slot = tt % 2
                t0 = tile_tok0(tt)
                for g in range(2):
                    pi, pn = pOr.next()
                    pv = pO[pi][:, :, :].rearrange("p a b -> p (a b)").bitcast(BF16)

                    def mm(e):
                        for j in range(8):
                            c = g * 8 + j
                            ins = e.transpose(pv[:, j * 128:j * 128 + npt], xb_t[slot][0:npt, c * 128:(c + 1) * 128],
                                              identb[0:npt, 0:npt])
                        return ins
                    S.op("pe", [(nxb, slot), "identb"], [pn], mm)
                    pv3 = pv.rearrange("p (k t) -> p k t", k=8)[:, :, 0:npt]
                    wr = [("hT", tt, g * 8 + j) for j in range(8)]
                    if g == 0:
                        S.op("dve", [pn], wr, lambda e: e.tensor_copy(out=hT[:, 0:8, t0:t0 + npt], in_=pv3))
                    else:
                        S.op("act", [pn], wr, lambda e: e.activation(out=hT[:, 8:16, t0:t0 + npt], in_=pv3, func=AF.Copy))
            return xload, stats, pe_part

        def modulate(L, tiles, hooks=None):
            flush_defer()
            S.alias(R2mod, yT_names + R2ada)
            xload, stats, pe_part = make_modulate(L, False)
            xload(tiles[0])
            xload(tiles[1])
            stats(tiles[0])
            for i, tt in enumerate(tiles):
                if i + 2 < len(tiles):
                    xload(tiles[i + 2])
                if i + 1 < len(tiles):
                    stats(tiles[i + 1])
                pe_part(tt)
                if hooks and tt in hooks:
                    hooks[tt]()

        def modulate_affine(L, with_halo):
            tiles_p = list(range(8))
            tiles_s = [8, 9] + ([10] if with_halo else [])
            for jsel, tl, a0, a1 in ((0, tiles_p, 0, 1024), (1, tiles_s, 1024, 1296 if with_halo else 1280)):
                for kc in range(16):
                    names = [("hT", tt, kc) for tt in tl]
                    if kc % 2 == 0:
                        S.op("dve", names + s1_names + ["shsc"], names, lambda e: e.tensor_scalar(
                            out=hT[:, kc, a0:a1], in0=hT[:, kc, a0:a1],
                            scalar1=s1t[:, L, jsel, kc:kc + 1], scalar2=shsc[:, L, jsel, 0, kc:kc + 1],
                            op0=ALU.mult, op1=ALU.add))
                    else:
                        S.op("act", names + s1_names + ["shsc"], names, lambda e: e.activation(
                            out=hT[:, kc, a0:a1], in_=hT[:, kc, a0:a1], func=AF.Identity,
                            scale=s1t[:, L, jsel, kc:kc + 1], bias=shsc[:, L, jsel, 0, kc:kc + 1]))

        def proj(slot, ncols, tiles, evac):
            for tt in tiles:
                npt = 16 if tt == 10 else 128
                t0 = tile_tok0(tt)
                pi, pn = pPr.next()

                def mm(e, pi=pi, npt=npt, t0=t0):
                    for kc in range(16):
                        ins = e.matmul(pP[pi][0:npt, 0:ncols], lhsT=hT[:, kc, t0:t0 + npt], rhs=WB[:, slot, kc, 0:ncols],
                                       start=(kc == 0), stop=(kc == 15))
                    return ins
                S.op("pe", [("hT", tt, k) for k in range(16)] + [("wb", slot)], [pn], mm)
                n0 = len(DEFER)
                evac(tt, pP[pi], pn)
                flush_defer(keep=len(DEFER) - n0)

        def transpose4(src_bf, src_name, dst_fn, dst_names, nblk=4):
            pi, pn = pSr.next()
            pv = pS[pi][:, :].bitcast(BF16)[:, 0:nblk * 128]

            def mm(e):
                for b in range(nblk):
                    ins = e.transpose(pv[:, b * 128:(b + 1) * 128], src_bf[:, b * 128:(b + 1) * 128], identb[:])
                return ins
            S.op("pe", [src_name, "identb"], [pn], mm)
            if int(os.environ.get("MK_SUBK", "99")) <= 5:
                return
            S.op("act", [pn], dst_names, lambda e: dst_fn(e, pv.rearrange("p (b t) -> p b t", b=nblk)))

        def evac_qk(L, kind, hg):
            dk = 64 if L == 0 else 128
            nch = 512 // dk
            wrow = qnw[:, 0 if kind == 'q' else 1, 0:dk]
            rope = ropet[:, :, :, 0:dk]
            q4 = dk // 4

            SUBK = int(os.environ.get("MK_SUBK", "99"))

            def f(tt, ps, pn):
                sample = tt >= 8
                if SUBK <= 1:
                    return
                si, sn = statr.next()
                qi, qn_ = sqr.next()
                S.op("act", [pn], [qn_], lambda e: e.activation(out=sqf[:, qi, :], in_=ps[:, :], func=AF.Square))
                S.op("dve", [qn_], [sn], lambda e: e.tensor_reduce(
                    out=stat[:, si, 0:nch], in_=sqf[:, qi, :].rearrange("p (c d) -> p c d", d=dk), axis=AX.X, op=ALU.add))
                S.op("act", [sn], [sn], lambda e: e.activation(
                    out=stat[:, si, 8:8 + nch], in_=stat[:, si, 0:nch], func=AF.Ln, scale=1.0 / dk, bias=EPS))
                S.op("act", [sn], [sn], lambda e: e.activation(
                    out=stat[:, si, 0:nch], in_=stat[:, si, 8:8 + nch], func=AF.Exp, scale=-0.5))
                if SUBK <= 2:
                    return
                fi, fn_ = qkfr.next()
                S.op("dve", [pn, sn], [fn_], lambda e: e.tensor_tensor(
                    out=qkf[:, fi, :].rearrange("p (c d) -> p c d", d=dk), in0=ps[:, :].rearrange("p (c d) -> p c d", d=dk),
                    in1=stat[:, si, 0:nch].unsqueeze(2).to_broadcast([128, nch, dk]), op=ALU.mult))
                if SUBK <= 3:
                    return
                bi, bn_ = qkbr.next()
                wbc = wrow.unsqueeze(1).to_broadcast([128, nch, dk])
                if not sample:
                    if kind == 'k':
                        ki, kn_ = kstr.next()
                        S.op("dve", [fn_, "qnw"], [kn_], lambda e: e.tensor_tensor(
                            out=kst[:, ki, :].rearrange("p (c d) -> p c d", d=dk),
                            in0=qkf[:, fi, :].rearrange("p (c d) -> p c d", d=dk), in1=wbc, op=ALU.mult))
                        seq, s0 = tt // 2, (tt % 2) * 128
                        if L == 0:
                            dst = nak[seq, hg * 4:(hg + 1) * 4, s0:s0 + 128, :]
                        else:
                            dst = nck[seq, :, s0:s0 + 128, :]
                        S.dma("sp", kn_, [kn_], [("out_k", L, hg, tt)], lambda e: e.dma_start(
                            out=dst.rearrange("h s d -> s h d"), in_=kst[:, ki, :].rearrange("p (h d) -> p h d", h=4)))
                        S.op("act", [kn_], [bn_], lambda e: e.activation(out=qkb[:, bi, :], in_=kst[:, ki, :], func=AF.Copy))
                    else:
                        S.op("dve", [fn_, "qnw"], [bn_], lambda e: e.tensor_tensor(
                            out=qkb[:, bi, :].rearrange("p (c d) -> p c d", d=dk),
                            in0=qkf[:, fi, :].rearrange("p (c d) -> p c d", d=dk), in1=wbc, op=ALU.mult))
                else:
                    ts = tt - 8
                    gi, gn_ = qkgr.next()
                    S.op("dve", [fn_, "qnw"], [gn_], lambda e: e.tensor_tensor(
                        out=qkg[:, gi, :].rearrange("p (c d) -> p c d", d=dk),
                        in0=qkf[:, fi, :].rearrange("p (c d) -> p c d", d=dk), in1=wbc, op=ALU.mult))
                    cosb = rope[:, ts, 0, :].unsqueeze(1).to_broadcast([128, nch, dk])
                    S.op("dve", [gn_, "qnw"], [fn_], lambda e: e.tensor_tensor(
                        out=qkf[:, fi, :].rearrange("p (c d) -> p c d", d=dk),
                        in0=qkg[:, gi, :].rearrange("p (c d) -> p c d", d=dk), in1=cosb, op=ALU.mult))
                    x5 = qkg[:, gi, :].rearrange("p (c a b q) -> p c a b q", a=2, b=2, q=q4)
                    t5 = sqf[:, qi, :].rearrange("p (c a b q) -> p c a b q", a=2, b=2, q=q4)
                    s5 = rope[:, ts, 1, :].rearrange("p (a b q) -> p a b q", a=2, b=2)
                    for b in range(2):
                        S.op("dve", [gn_, "qnw"], [qn_], lambda e, b=b: e.tensor_tensor(
                            out=t5[:, :, :, b, :], in0=x5[:, :, :, 1 - b, :],
                            in1=s5[:, :, b, :].unsqueeze(1).to_broadcast([128, nch, 2, q4]), op=ALU.mult))
                    S.op("dve", [fn_, qn_], [bn_], lambda e: e.tensor_tensor(
                        out=qkb[:, bi, :], in0=qkf[:, fi, :], in1=sqf[:, qi, :], op=ALU.add))
                t0 = tt * 128
                if SUBK <= 4:
                    return
                if kind == 'q':
                    defer(lambda: transpose4(qkb[:, bi, :], bn_, lambda e, pv: e.activation(out=QT[:, :, t0:t0 + 128], in_=pv, func=AF.Copy), [("QT", tt)]))
                elif not sample:
                    defer(lambda: transpose4(qkb[:, bi, :], bn_, lambda e, pv: e.activation(out=KTp[:, :, t0:t0 + 128], in_=pv, func=AF.Copy), ["KTp"]))
                else:
                    ts = tt - 8
                    defer(lambda: transpose4(qkb[:, bi, :], bn_, lambda e, pv: e.activation(out=ktst[:, :, ts * 128:(ts + 1) * 128], in_=pv, func=AF.Copy), [("ktst", ts)]))
            return f

        def evac_v(L, hg):
            def f(tt, ps, pn):
                if tt < 8:
                    vi, vn_ = vstr.next()
                    S.op("act", [pn], [vn_], lambda e: e.activation(out=vst[:, vi, :], in_=ps[:, :], func=AF.Copy))
                    seq, s0 = tt // 2, (tt % 2) * 128
                    if L == 0:
                        dst = nav[seq, hg * 4:(hg + 1) * 4, s0:s0 + 128, :]
                    else:
                        dst = ncv[seq, :, s0:s0 + 128, :]
                    S.dma("sp", vn_, [vn_], [("out_v", L, hg, tt)], lambda e: e.dma_start(
                        out=dst.rearrange("h s d -> s h d"), in_=vst[:, vi, :].rearrange("p (h d) -> p h d", h=4)))
                    S.op("dve", [vn_], ["V1p"], lambda e: e.tensor_copy(
                        out=V1p[:, tt, :, 0:128], in_=vst[:, vi, :].rearrange("p (h d) -> p h d", h=4)))
                else:
                    ts = tt - 8
                    S.op("dve", [pn], [("vsst", ts)], lambda e: e.tensor_copy(out=vsst[:, ts, :], in_=ps[:, :]))
            return f

        def evac_g(tt, ps, pn):
            S.op("act", [pn], [("SG", tt)], lambda e: e.activation(out=SG[:, tt, :], in_=ps[:, :], func=AF.Silu))

        def evac_g_sub(tt, ps, pn):
            qi, qn_ = qkfr.next()
            S.op("act", [pn], [qn_], lambda e: e.activation(out=qkf[:, qi, :], in_=ps[:, :], func=AF.Silu))
            S.op("dve", [qn_, "wsub"], [("SG", tt)], lambda e: e.tensor_tensor(
                out=SG[:, tt, :].rearrange("p (h d) -> p h d", h=4), in0=qkf[:, qi, :].rearrange("p (h d) -> p h d", h=4),
                in1=wsub[:, :].unsqueeze(1).to_broadcast([128, 4, 128]), op=ALU.mult))

        def evac_u(tt, ps, pn):
            npt = 16 if tt == 10 else 128
            S.op("dve", [pn], [("u_tok", tt)], lambda e: e.tensor_copy(out=u_tok[0:npt, tt, :], in_=ps[0:npt, :]))

        def kv_prefetch(L, hg):
            ck = cak if L == 0 else cck
            cv = cav if L == 0 else ccv
            h0 = hg * 4 if L == 0 else 0
            S.dma("pool", "cstg", [], ["cstg"], lambda e: [e.dma_start(
                out=cstg[:, t, :, :], in_=ck[h0:h0 + 4, t * 128:(t + 1) * 128, :].rearrange("h p d -> p h d")) for t in range(2)])
            S.dma("pool", "v1s_a", [], ["V1s"], lambda e: [e.dma_start(
                out=V1s[:, t, :, 0:128], in_=cv[h0:h0 + 4, t * 128:(t + 1) * 128, :].rearrange("h p d -> p h d")) for t in range(2)])

        def kv_exchange(L, hg, agi):
            flush_defer()
            ain, aout = agkv_in[agi], agkv_out[agi]
            S.dma("sp", "kvst", [("ktst", 0), ("ktst", 1), ("vsst", 0), ("vsst", 1)], [("agin", agi)], lambda e: [e.dma_start(
                out=ain[0:512, :].rearrange("(h d) t -> d h t", h=4), in_=ktst[:, :, :]), e.dma_start(
                out=ain[512:1024, :].rearrange("(t p a) b -> p t (a b)", t=2, a=2), in_=vsst[:, :, :])])
            S.dma("pool", "cc", [("agin", agi)], [("agout", agi)], lambda e: e.collective_compute(
                "AllGather", ALU.bypass, replica_groups=GROUPS, ins=[ain], outs=[aout]), inc=1)
            for t in range(2):
                transpose4(cstg[:, t, :, :].rearrange("p h d -> p (h d)"), "cstg",
                           lambda e, pv, t=t: e.activation(out=KTs[:, :, t * 128:(t + 1) * 128], in_=pv, func=AF.Copy), ["KTs"])
            aview = aout.rearrange("(r x) t -> r x t", r=4)
            S.dma("sp", "kts_b", [("agout", agi)], ["KTs"], lambda e: [e.dma_start(
                out=KTs[:, :, 256 + r * 256:256 + (r + 1) * 256],
                in_=aview[r, 0:512, :].rearrange("(h d) t -> d h t", h=4)) for r in range(4)])
            S.dma("sp", "v1s_b", [("agout", agi)], ["V1s"], lambda e: [e.dma_start(
                out=V1s[:, 2 + 2 * r + t, :, 0:128],
                in_=aview[r, 512 + t * 256:512 + (t + 1) * 256, :].rearrange("(p a) b -> p (a b)", a=2).rearrange("p (h d) -> p h d", h=4))
                for r in range(4) for t in range(2)])

        def set_ones():
            S.dma("pool", "c_ones", [], ["V1p", "V1s"], lambda e: [
                e.dma_start(out=R5[:, 4096:8256], in_=onesd[:, 0:4160]),
                e.dma_start(out=R6[:, 5120:10320], in_=onesd[:, 0:5200])])

        def attention(L, kc0, kv_of_head, hooks=None):
            flush_defer()
            nj = 2 if L == 0 else 1
            dk = 64 if L == 0 else 128
            sc = dk ** -0.5
            units = []
            for seq in range(4):
                for qh in range(4):
                    for j in range(nj):
                        units.append((seq, qh, j, (0, 1)))
            for qh in range(4):
                for qt in range(2):
                    for j in range(nj):
                        units.append((4, qh, j, (qt,)))

            def keys_of(seq):
                if seq < 4:
                    return [("p", seq * 2 + t) for t in range(2)]
                return [("s", t) for t in range(10)]

            def scores(u, ui):
                seq, qh, j, qts = u
                kvh = kv_of_head(qh)
                kts = keys_of(seq)
                nq = 128 * len(qts)
                q0 = seq * 256 + qts[0] * 128
                pslot = ui % 2
                per = 512 // nq
                ptv = PT[:, pslot, 0:len(kts) * nq].rearrange("p (k q) -> p k q", q=nq)
                for c in range(0, len(kts), per):
                    pi, pn = pSr.next()
                    nper = min(per, len(kts) - c)

                    def mm(e, c=c, pi=pi, nper=nper):
                        for t in range(nper):
                            kind, kt = kts[c + t]
                            if kind == "p":
                                ksrc = KTp[j * dk:(j + 1) * dk, kvh, kt * 128:(kt + 1) * 128]
                            else:
                                ksrc = KTs[j * dk:(j + 1) * dk, kvh, kt * 128:(kt + 1) * 128]
                            ins = e.matmul(pS[pi][:, t * nq:(t + 1) * nq], lhsT=ksrc,
                                           rhs=QT[j * dk:(j + 1) * dk, qh, q0:q0 + nq], start=True, stop=True)
                        return ins
                    S.op("pe", ["KTp" if seq < 4 else "KTs", ("QT", seq * 2), ("QT", seq * 2 + 1)], [pn], mm)
                    S.op("act", [pn], [("PT", pslot)], lambda e, c=c, pi=pi, nper=nper: e.activation(
                        out=ptv[:, c:c + nper, :], in_=pS[pi][:, 0:nper * nq].rearrange("p (t q) -> p t q", t=nper),
                        func=AF.Exp, scale=sc))

            def pv(u, ui, oslots):
                seq, qh, j, qts = u
                kvh = kv_of_head(qh)
                kts = keys_of(seq)
                nq = 128 * len(qts)
                pslot = ui % 2
                ptv = PT[:, pslot, 0:len(kts) * nq].rearrange("p (k q) -> p k q", q=nq)
                for qi_, qt in enumerate(qts):
                    oi, on = oslots[qi_]

                    def mm(e, qi_=qi_, oi=oi):
                        for i, (kind, kt) in enumerate(kts):
                            vsrc = V1p[:, kt, kvh, 0:129] if kind == "p" else V1s[:, kt, kvh, 0:129]
                            ins = e.matmul(pO[oi][:, j, 0:129], lhsT=ptv[:, i, qi_ * 128:(qi_ + 1) * 128], rhs=vsrc,
                                           start=(i == 0), stop=(i == len(kts) - 1))
                        return ins
                    S.op("pe", [("PT", pslot), "V1p" if seq < 4 else "V1s"], [on], mm)

            def combine(seq, qh, qts, oslots):
                chains = []
                for qi_, qt in enumerate(qts):
                    chains.append(combine_ops(seq, qh, qt, oslots[qi_]))
                n = max(len(c) for c in chains)
                for k in range(n):
                    for c in chains:
                        if k < len(c):
                            eng, rd, wr, fn = c[k]
                            if eng == "defer":
                                defer(fn)
                            else:
                                S.op(eng, rd, wr, fn)

            def combine_ops(seq, qh, qt, oslot):
                ops = []
                oi, on = oslot
                tt = seq * 2 + qt
                si, sn = statr.next()
                yi, yn = ytr4.next()
                ytv = ytk[:, yi // 2, (yi % 2) * 128:(yi % 2 + 1) * 128]
                if L == 0:
                    ti, tn = otr.next()
                    fi, fn_ = ofr.next()
                    ops.append(("dve", [on], [sn], lambda e: e.reciprocal(out=stat[:, si, 0:2], in_=pO[oi][:, :, 128])))
                    ops.append(("dve", [sn, "nlam"], [sn], lambda e: e.tensor_tensor(
                        out=stat[:, si, 2:3], in0=stat[:, si, 1:2], in1=nlam[:, 0:1], op=ALU.mult)))
                    ops.append(("dve", [on, sn], [tn], lambda e: e.tensor_scalar(
                        out=otm[:, ti, :], in0=pO[oi][:, 1, 0:128], scalar1=stat[:, si, 2:3], scalar2=None, op0=ALU.mult)))
                    ops.append(("dve", [on, sn, tn], [fn_], lambda e: e.scalar_tensor_tensor(
                        out=ofm[:, fi, :], in0=pO[oi][:, 0, 0:128], scalar=stat[:, si, 0:1], in1=otm[:, ti, :],
                        op0=ALU.mult, op1=ALU.add)))
                    ops.append(("act", [fn_], [tn, sn], lambda e: e.activation(
                        out=otm[:, ti, :], in_=ofm[:, fi, :], func=AF.Square, accum_out=stat[:, si, 4:5])))
                    ops.append(("act", [sn], [sn], lambda e: e.activation(
                        out=stat[:, si, 5:6], in_=stat[:, si, 4:5], func=AF.Ln, scale=1.0 / 128, bias=EPS)))
                    ops.append(("act", [sn], [sn], lambda e: e.activation(
                        out=stat[:, si, 6:7], in_=stat[:, si, 5:6], func=AF.Exp, scale=-0.5)))
                    ops.append(("dve", [fn_, sn, ("SG", tt)], [yn], lambda e: e.scalar_tensor_tensor(
                        out=ytv, in0=ofm[:, fi, :], scalar=stat[:, si, 6:7], in1=SG[:, tt, qh * 128:(qh + 1) * 128],
                        op0=ALU.mult, op1=ALU.mult)))
                else:
                    ops.append(("dve", [on], [sn], lambda e: e.reciprocal(out=stat[:, si, 0:1], in_=pO[oi][:, 0, 128:129])))
                    ops.append(("dve", [on, sn, ("SG", tt)], [yn], lambda e: e.scalar_tensor_tensor(
                        out=ytv, in0=pO[oi][:, 0, 0:128], scalar=stat[:, si, 0:1],
                        in1=SG[:, tt, qh * 128:(qh + 1) * 128], op0=ALU.mult, op1=ALU.mult)))

                def ytrans():
                    pi, pn = pPr.next()
                    pvw = pP[pi][:, :].bitcast(BF16)[:, 0:128]
                    S.op("pe", [yn, "identb"], [pn], lambda e: e.transpose(pvw, ytv, identb[:]))
                    S.op("act", [pn], [("yT", kc0 + qh)], lambda e: e.activation(
                        out=yT[:, kc0 + qh, tt * 128:(tt + 1) * 128], in_=pvw, func=AF.Copy))
                ops.append(("defer", None, None, ytrans))
                return ops

            scores(units[0], 0)
            oslots = None
            for ui, u in enumerate(units):
                if ui + 1 < len(units):
                    scores(units[ui + 1], ui + 1)
                seq, qh, j, qts = u
                if j == 0:
                    oslots = [pOr.next() for _ in qts]
                pv(u, ui, oslots)
                if hooks and ui in hooks:
                    hooks[ui]()
                flush_defer(keep=2 if L == 0 else 0)
                if j == nj - 1:
                    combine(seq, qh, qts, oslots)

        def pool_block(ub):
            flush_defer()
            its = [(seq, gl) for seq in range(5) for gl in range(2)]
            state = {}

            def stage_a(i):
                seq, gl = its[i]
                g = ub * 2 + gl
                stiles = [(seq * 2 + t, 128) for t in range(2)] if seq < 4 else [(8, 128), (9, 128), (10, 16)]
                qi, qn_ = ppr.next()
                for cb in range(2):
                    pi, pn = pSr.next()

                    def mm(e):
                        for k, (tt, npt) in enumerate(stiles):
                            bsrc = Bpt[0:npt, g, k, :] if seq < 4 else Bst[0:npt, g, k, :]
                            ins = e.matmul(pS[pi][:, 0:256], lhsT=u_tok[0:npt, tt, gl * 256 + cb * 128: gl * 256 + (cb + 1) * 128],
                                           rhs=bsrc, start=(k == 0), stop=(k == len(stiles) - 1))
                        return ins
                    S.op("pe", [("u_tok", tt) for tt, _ in stiles] + ["Bt"], [pn], mm)
                    S.op("act", [pn], [qn_], lambda e: e.activation(
                        out=pooledT[:, qi, cb, :], in_=pS[pi][:, 0:256], func=AF.Copy))
                state[i] = {"q": (qi, qn_), "y": []}

            def stage_b(i):
                seq, gl = its[i]
                g = ub * 2 + gl
                qi, qn_ = state[i]["q"]
                for qt in range(2):
                    tt = seq * 2 + qt
                    pi, pn = pPr.next()

                    def mm2(e):
                        for cb in range(2):
                            ins = e.matmul(pP[pi][:, 0:256], lhsT=pooledT[:, qi, cb, qt * 128:(qt + 1) * 128],
                                           rhs=pwt[:, g, cb, :], start=(cb == 0), stop=(cb == 1))
                        return ins
                    S.op("pe", [qn_, "pwt"], [pn], mm2)
                    ti, _ = ptr_.next()
                    tns = [("otm", 2 * ti), ("otm", 2 * ti + 1)]
                    ptv = otm[:, 2 * ti:2 * ti + 2, :].rearrange("p a d -> p (a d)")
                    S.op("dve", [pn, "pscbc"], tns, lambda e: e.tensor_tensor(
                        out=ptv, in0=pP[pi][:, 0:256], in1=pscbc[:, g * 256:(g + 1) * 256], op=ALU.mult))
                    yi, _ = ytr.next()
                    yns = [("ytk", 2 * yi), ("ytk", 2 * yi + 1)]
                    S.op("dve", tns + [("SG", tt)], yns, lambda e: e.tensor_tensor(
                        out=ytk[:, yi, :], in0=ptv, in1=SG[:, tt, gl * 256:(gl + 1) * 256], op=ALU.mult))
                    state[i]["y"].append((yi, yns, tt))

            def stage_c(i):
                seq, gl = its[i]
                g = ub * 2 + gl
                kcb = 8 + g * 2
                for yi, yns, tt in state[i]["y"]:
                    p2, pn2 = pPr.next()
                    pvw = pP[p2][:, :].bitcast(BF16)[:, 0:256]

                    def tr(e):
                        for b_ in range(2):
                            ins = e.transpose(pvw[:, b_ * 128:(b_ + 1) * 128], ytk[:, yi, b_ * 128:(b_ + 1) * 128], identb[:])
                        return ins
                    S.op("pe", yns + ["identb"], [pn2], tr)
                    S.op("act", [pn2], [("yT", kcb), ("yT", kcb + 1)], lambda e: e.activation(
                        out=yT[:, kcb:kcb + 2, tt * 128:(tt + 1) * 128], in_=pvw.rearrange("p (b t) -> p b t", b=2), func=AF.Copy))

            n = len(its)
            for i in range(n + 2):
                if i < n:
                    stage_a(i)
                if 0 <= i - 2 < n:
                    stage_c(i - 2)
                if 0 <= i - 1 < n:
                    stage_b(i - 1)

        def w_out_phase(L, w_out, inter=None, next_w=None):
            flush_defer()
            S.alias(R1wo, R5att + R6att + R6pool + [("u_tok", t) for t in range(11)])
            if inter is not None:
                S.alias(ALTmod, QSP_names + R5att + R6att + R6pool + [("u_tok", t) for t in range(11)])
                ixload, istats, ipe = make_modulate(L + 1, True)
            for j in range(2):
                S.dma("sp", ("gbc", j), [("mlin", L)], [("gbc", j)], lambda e, j=j: e.dma_start(
                    out=gbc[j][:, :], in_=mlin[L, j, 4096:6144].partition_broadcast(128)))
            slot = wload(w_out[:, 0:512], 512)

            def tile_io(tt):
                if L == 0:
                    src = xp[tt * 128:(tt + 1) * 128] if tt < 8 else xs[(tt - 8) * 128:(tt - 7) * 128]
                    return src, [], x1[tt * 128:(tt + 1) * 128], lambda cb: [("x1", tt)]
                dst = yp[tt * 128:(tt + 1) * 128] if tt < 8 else ys[(tt - 8) * 128:(tt - 7) * 128]
                return x1[tt * 128:(tt + 1) * 128], [("x1", tt)], dst, lambda cb: [("out_y", tt, cb)]

            xr = Ring("xblk", 4)
            xslots = {}
            pre = {}

            def xb_load(cb, tt):
                src, rd, _, _ = tile_io(tt)
                xi, xn = xr.next()
                xslots[(cb, tt)] = (xi, xn)
                S.dma("act", xn, rd, [xn], lambda e: e.dma_start(out=xblk[xi][:, :], in_=src[:, cb * 512:(cb + 1) * 512]))

            seq = [(cb, tt) for cb in range(4) for tt in range(10)]
            for k in range(3):
                xb_load(*seq[k])
            orr = Ring("oblk", 2)
            for k, (cb, tt) in enumerate(seq):
                if tt == 0:
                    if cb < 3:
                        nslot = wload(w_out[:, (cb + 1) * 512:(cb + 2) * 512], 512)
                    else:
                        nslot = None
                        if next_w is not None:
                            pre["slot"] = wload(next_w, 512)
                if k + 3 < len(seq):
                    xb_load(*seq[k + 3])
                jsel = 0 if tt < 8 else 1
                _, _, dst, wrf = tile_io(tt)
                xi, xn = xslots[(cb, tt)]
                pi, pn = pPr.next()

                def mm(e):
                    for kc in range(16):
                        ins = e.matmul(pP[pi][:, :], lhsT=yT[:, kc, tt * 128:(tt + 1) * 128], rhs=WB[:, slot, kc, :],
                                       start=(kc == 0), stop=(kc == 15))
                    return ins
                S.op("pe", yT_names + [("wb", slot)], [pn], mm)
                oi, on = orr.next()
                S.op("dve", [pn, ("gbc", jsel)], [on], lambda e: e.tensor_tensor(
                    out=oblk[oi][:, :], in0=pP[pi][:, :], in1=gbc[jsel][:, cb * 512:(cb + 1) * 512], op=ALU.mult))
                S.op("dve", [on, xn], [on], lambda e: e.tensor_tensor(
                    out=oblk[oi][:, :], in0=oblk[oi][:, :], in1=xblk[xi][:, :], op=ALU.add))
                S.dma("sp", on, [on], wrf(cb), lambda e: e.dma_start(
                    out=dst[:, cb * 512:(cb + 1) * 512], in_=oblk[oi][:, :]))
                if inter is not None and cb == 3:
                    if tt >= 4:
                        ipe(tt - 4)
                    if tt >= 2:
                        istats(tt - 2)
                    ixload(tt)
                if tt == 9:
                    slot = nslot
            if inter is not None:
                ipe(6)
                istats(8)
                ipe(7)
                istats(9)
                ipe(8)
                ipe(9)
            return pre.get("slot")

        ALL10 = list(range(10))

        class Stop(Exception):
            pass

        def gate(n):
            if stage < n:
                raise Stop()

        def main_flow():
            S.alias([("qkf", 0), ("qkf", 1)], ["qnw0", "lamp", "onesr"])
            gate(1)
            load_layer_consts(0)
            st3 = {}

            def hk0():
                ada_block(0, 0, ada0_slots[0])
                st3["s"] = wload(adaw[0, :, 1024:1536], 512)
            hooks0 = {5: hk0, 7: (lambda: ada_block(0, 1, ada0_slots[1]))}
            modulate(0, list(range(11)), hooks0)
            ada_block(0, 2, st3["s"])
            ada_finish(0)
            nxt = wload(ew_in[:, 1024:1536], 512)
            modulate_affine(0, True)
            S.alias(yT_names, R2mod + R2ada)
            set_ones()
            gate(2)
            for hg in range(2):
                sK = nxt
                kv_prefetch(0, hg)
                sV = wload(ew_in[:, 2048 + hg * 512:2048 + (hg + 1) * 512], 512)
                proj(sK, 512, ALL10, evac_qk(0, 'k', hg))
                if hg == 1:
                    fin_b()
                if int(os.environ.get("MK_SUBK", "99")) < 99:
                    raise Stop()
                sQ = wload(ew_in[:, hg * 512:(hg + 1) * 512], 512)
                proj(sV, 512, ALL10, evac_v(0, hg))
                gate(3)
                kv_exchange(0, hg, hg)
                gate(4)
                sG = wload(ew_in[:, 4096 + hg * 512:4096 + (hg + 1) * 512], 512)
                proj(sQ, 512, ALL10, evac_qk(0, 'q', hg))
                nxt = wload(ew_in[:, 1024 + 512:1024 + 1024], 512) if hg == 0 else wload(ew_in[:, 3072:3584], 512)
                proj(sG, 512, ALL10, evac_g_sub)
                gate(5)
                if hg == 0:
                    hk = {ui: (lambda nb=nb: ada_block(1, nb, wload(adaw[1, :, nb * 512:(nb + 1) * 512], 512, slot=sG)))
                          for nb, ui in enumerate((10, 28, 46))}
                    attention(0, hg * 4, lambda qh: qh, hk)
                    fin_b = ada_finish(1, split=True)
                else:
                    attention(0, hg * 4, lambda qh: qh)
                gate(6)
            flush_defer()
            S.alias(["Bt", "pwt", "pscbc"], R6att)
            S.alias([("u_tok", t) for t in range(11)], R5att)
            S.dma("pool", "c_B", [], ["Bt", "pwt", "pscbc"], lambda e: [
                e.dma_start(out=Bst[0:16, :, 2, :], in_=Bs[:, 256:272, :].rearrange("g p t -> p g t")),
                e.dma_start(out=pscbc[:, :], in_=pscale.partition_broadcast(128))] + [
                e.dma_start(out=Bpt[:, g, :, :], in_=Bp[g].rearrange("(s p) t -> p s t", p=128)) for g in range(4)] + [
                e.dma_start(out=Bst[:, g, 0:2, :], in_=Bs[g, 0:256, :].rearrange("(s p) t -> p s t", p=128)) for g in range(4)] + [
                e.dma_start(out=pwt[:, g, :, :], in_=poolw[g].rearrange("(c p) d -> p c d", p=128)) for g in range(4)])
            for ub in range(2):
                sU = nxt
                sGp = wload(ew_in[:, 5120 + ub * 512:5120 + (ub + 1) * 512], 512)
                proj(sU, 512, list(range(11)), evac_u)
                if ub == 0:
                    nxt = wload(ew_in[:, 3584:4096], 512)
                proj(sGp, 512, ALL10, evac_g)
                pool_block(ub)
            gate(7)
            load_layer_consts(1)
            sK1 = w_out_phase(0, ew_out, inter=True, next_w=gw_in[:, 2048:2560])
            gate(8)

            modulate_affine(1, False)
            S.alias(QSP_names, ALTmod)
            S.alias(R6att, R1wo + ALTmod)
            S.alias(R5att, R1wo + ALTmod)
            set_ones()
            sK = sK1
            kv_prefetch(1, 0)
            sV = wload(gw_in[:, 2560:3072], 512)
            proj(sK, 512, ALL10, evac_qk(1, 'k', 0))
            sQ = wload(gw_in[:, 0:512], 512)
            proj(sV, 512, ALL10, evac_v(1, 0))
            gate(9)
            kv_exchange(1, 0, 2)
            for kvh in range(4):
                sG = wload(gw_in[:, 3072 + kvh * 512:3072 + (kvh + 1) * 512], 512)
                proj(sQ, 512, ALL10, evac_qk(1, 'q', kvh))
                if kvh < 3:
                    sQ = wload(gw_in[:, (kvh + 1) * 512:(kvh + 2) * 512], 512)
                proj(sG, 512, ALL10, evac_g)
                attention(1, kvh * 4, lambda qh, kvh=kvh: kvh)
            gate(10)
            w_out_phase(1, gw_out)

        try:
            main_flow()
        except Stop:
            pass
        flush_defer()

        S.wait_all("sp")
        build_program.stats = (S.n_ops, dict(S.count))
    return nc


def _rope_tables(pos0, dim):
    pos = np.arange(pos0, pos0 + 256)
    row = (pos // GRID_W).astype(np.float32)
    col = (pos % GRID_W).astype(np.float32)
    quarter = dim // 4
    freqs = (ROPE_THETA ** (-np.arange(quarter, dtype=np.float32) / quarter)).astype(np.float32)
    ar = row[:, None] * freqs
    ac = col[:, None] * freqs
    cos = np.concatenate([np.cos(ar), np.cos(ar), np.cos(ac), np.cos(ac)], -1)
    sinm = np.concatenate([-np.sin(ar), np.sin(ar), -np.sin(ac), np.sin(ac)], -1)
    return np.stack([cos, sinm], 1).astype(np.float32)


def _pool_mats(L, t0, t1, src_idx):
    out = np.zeros((4, len(src_idx), t1 - t0), np.float32)
    for g, w in enumerate(POOL_WINDOWS):
        half = w // 2
        for tl, t in enumerate(range(t0, t1)):
            lo, hi = max(t - half, 0), min(t + half, L)
            cnt = float(hi - lo)
            for sl, s in enumerate(src_idx):
                if s < 0:
                    continue
                v = 0.0
                if lo <= s < hi:
                    v += 1.0 / cnt
                if s == t:
                    v -= 1.0
                out[g, sl, tl] = v
    return out


_CACHE = {}


def _make_in_maps(x_prompt, x_sample, cache_a_k, cache_a_v, cache_c_k, cache_c_v, c, c_ctx,
           norm_w, ada_w, ada_b,
           even_w_in, even_q_norm_w, even_k_norm_w, even_lam_q1, even_lam_k1,
           even_lam_q2, even_lam_k2, even_subln_w, even_pool_w, even_pool_scale,
           even_w_out, gqa_w_in, gqa_q_norm_w, gqa_k_norm_w, gqa_w_out):
    f = lambda a: np.ascontiguousarray(np.asarray(a, dtype=np.float32))
    x_prompt, x_sample = f(x_prompt), f(x_sample)
    ada_w, ada_b = f(ada_w), f(ada_b)
    ident = np.eye(128, dtype=np.float32)
    Bp = _pool_mats(256, 0, 256, list(range(256)))
    lamv = np.stack([f(even_lam_q1)[0], f(even_lam_k1)[0], f(even_lam_q2)[0], f(even_lam_k2)[0]], 0)
    shared = {
        "norm_w": f(norm_w), "ew_in": f(even_w_in)[0], "ew_out": f(even_w_out)[0],
        "gw_in": f(gqa_w_in)[0], "gw_out": f(gqa_w_out)[0],
        "eqn": f(even_q_norm_w)[0], "ekn": f(even_k_norm_w)[0], "lamv": f(lamv),
        "subw": f(even_subln_w)[0], "poolw": f(even_pool_w)[0], "pscale": f(even_pool_scale)[0],
        "gqn": f(gqa_q_norm_w)[0], "gkn": f(gqa_k_norm_w)[0], "ident": ident, "Bp": Bp,
        "onesd": np.ones((128, 5200), np.float32),
    }
    in_maps = []
    for core in range(8):
        b, r = core // 4, core % 4
        t0 = r * 256
        halo_idx = list(range(t0 - 8, t0)) + list(range(t0 + 256, t0 + 264))
        xh = np.zeros((16, D), np.float32)
        src_idx = list(range(t0, t0 + 256))
        for i, s in enumerate(halo_idx):
            if 0 <= s < 1024:
                xh[i] = x_sample[b, s]
                src_idx.append(s)
            else:
                src_idx.append(-1)
        m = dict(shared)
        m.update({
            "xp": x_prompt[core * 4:(core + 1) * 4].reshape(1024, D),
            "xs": np.ascontiguousarray(x_sample[b, t0:t0 + 256]),
            "xh": xh,
            "cond": np.stack([f(c_ctx), f(c)[b]], 0),
            "cak": f(cache_a_k)[b, 0], "cav": f(cache_a_v)[b, 0],
            "cck": f(cache_c_k)[b, 0], "ccv": f(cache_c_v)[b, 0],
            "adaw": np.ascontiguousarray(ada_w[:, :, r * 1536:(r + 1) * 1536]),
            "adab": np.ascontiguousarray(ada_b[:, r * 1536:(r + 1) * 1536]),
            "ropeA": _rope_tables(t0, 64), "ropeC": _rope_tables(t0, 128),
            "Bs": _pool_mats(1024, t0, t0 + 256, src_idx),
        })
        in_maps.append(m)
    return in_maps


def _assemble(R):
    y_p = np.concatenate([R[i]["yp"].reshape(4, 256, D) for i in range(8)], 0)
    y_s = np.stack([np.concatenate([R[b * 4 + r]["ys"] for r in range(4)], 0) for b in range(2)], 0)
    nak = np.concatenate([R[i]["nak"] for i in range(8)], 0)[:, None]
    nav = np.concatenate([R[i]["nav"] for i in range(8)], 0)[:, None]
    nck = np.concatenate([R[i]["nck"] for i in range(8)], 0)[:, None]
    ncv = np.concatenate([R[i]["ncv"] for i in range(8)], 0)[:, None]
    return (y_p.astype(np.float32), y_s.astype(np.float32), nak.astype(np.float32),
            nav.astype(np.float32), nck.astype(np.float32), ncv.astype(np.float32))


def kernel(**inputs):
    stage = _CACHE.get("stage", 99)
    if ("nc", stage) not in _CACHE:
        _CACHE[("nc", stage)] = build_program(stage)
    nc = _CACHE[("nc", stage)]
    in_maps = _make_in_maps(**inputs)
    res = run_bass_kernel_spmd(nc, in_maps, core_ids=list(range(8)))
    return _assemble(res.results)
```
